# Optimizing a Trainium2 kernel written in Bass

```python
import jax, jax.numpy as jnp
from jax import lax
import numpy as np

D_MODEL = 1024
BATCH = 8
SEQ = 2048
DEPTH = 1
DEC_BATCH = 128
DEC_SEQ = 1
PAST_LEN = 16384
PAGE_SIZE = 128

MLSTM_HEADS = 4
MLSTM_DK = D_MODEL // 8
MLSTM_DV = D_MODEL // 8
D_A = MLSTM_HEADS * MLSTM_DV
QK_A = MLSTM_HEADS * MLSTM_DK
MLSTM_CHUNK = 128
POOL_WINDOWS = (2, 4, 8, 16)
N_POOL_GROUPS = 4
D_B = D_MODEL // 2
POOL_GROUP = D_B // N_POOL_GROUPS
POOL_BUF = max(POOL_WINDOWS) - 1
PEER_HEADS = 8
PEER_DKEY = D_MODEL // 4
PEER_HALF = PEER_DKEY // 2
PEER_N_KEYS = 128
PEER_N_EXPERTS = PEER_N_KEYS ** 2
PEER_TOPK = 16
PEER_TOKEN_BLOCK = 128
OFF_Q = 0
OFF_K = OFF_Q + QK_A
OFF_V = OFF_K + QK_A
OFF_O = OFF_V + D_A
OFF_I = OFF_O + D_A
OFF_F = OFF_I + MLSTM_HEADS
OFF_U = OFF_F + MLSTM_HEADS
OFF_GA = OFF_U + D_B
OFF_GB = OFF_GA + D_MODEL
N_IN = OFF_GB + D_MODEL
DEEPNORM_ALPHA = (2 * DEPTH) ** 0.25
DEEPNORM_BETA = (8 * DEPTH) ** -0.25
LN_EPS = 1e-5

kernel_name = 'hybrid_mlstm_pool_peer_step'


def layer_norm(x):
    xf = x.astype(jnp.float32)
    mu = jnp.mean(xf, axis=-1, keepdims=True)
    var = jnp.mean(jnp.square(xf - mu), axis=-1, keepdims=True)
    return (xf - mu) * lax.rsqrt(var + LN_EPS)


def mlstm_chunk(carry, inp):
    C, n, m = carry
    q, k, v, ig, lf = inp
    T = q.shape[2]
    b = jnp.cumsum(lf, axis=-1)
    causal = jnp.tril(jnp.ones((T, T), dtype=bool))
    log_d = b[..., :, None] - b[..., None, :] + ig[..., None, :]
    log_d = jnp.where(causal, log_d, -jnp.inf)
    log_g = b + m[..., None]
    m_t = jnp.maximum(log_g, jnp.max(log_d, axis=-1))
    w = jnp.exp(log_d - m_t[..., None]) * jnp.einsum('bhtd,bhsd->bhts', q, k)
    w_state = jnp.exp(log_g - m_t)
    num = jnp.einsum('bhts,bhsv->bhtv', w, v) + w_state[..., None] * jnp.einsum('bhtd,bhdv->bhtv', q, C)
    den = jnp.sum(w, axis=-1) + w_state * jnp.einsum('bhtd,bhd->bht', q, n)
    h = num / jnp.maximum(jnp.abs(den), jnp.exp(-m_t))[..., None]
    m_new = m_t[..., -1]
    decay_s = jnp.exp(b[..., -1:] - b + ig - m_new[..., None])
    decay_c = jnp.exp(b[..., -1] + m - m_new)
    C_new = decay_c[..., None, None] * C + jnp.einsum('bhs,bhsd,bhsv->bhdv', decay_s, k, v)
    n_new = decay_c[..., None] * n + jnp.einsum('bhs,bhsd->bhd', decay_s, k)
    return (C_new, n_new, m_new), h


def mlstm_scan(q, k, v, ig, lf, state):
    T = q.shape[2]
    L = MLSTM_CHUNK if T % MLSTM_CHUNK == 0 else T
    nc = T // L

    def split(a):
        a = a.astype(jnp.float32)
        a = a.reshape(a.shape[:2] + (nc, L) + a.shape[3:])
        return jnp.moveaxis(a, 2, 0)

    state = (state[0].astype(jnp.float32), state[1].astype(jnp.float32), state[2].astype(jnp.float32))
    state, h = lax.scan(mlstm_chunk, state, (split(q), split(k), split(v), split(ig), split(lf)))
    h = jnp.moveaxis(h, 0, 2).reshape(q.shape[:3] + (v.shape[-1],))
    return h, state


def multiscale_pool(u_ext, pos0, T):
    Bn = u_ext.shape[0]
    uf = u_ext.astype(jnp.float32).reshape(Bn, POOL_BUF + T, N_POOL_GROUPS, POOL_GROUP)
    cs = jnp.pad(jnp.cumsum(uf, axis=1), ((0, 0), (1, 0), (0, 0), (0, 0)))
    pos = pos0 + jnp.arange(T)
    end = POOL_BUF + 1
    outs = []
    for g, w in enumerate(POOL_WINDOWS):
        win_sum = cs[:, end:end + T, g] - cs[:, end - w:end - w + T, g]
        cnt = jnp.minimum(w, pos + 1).astype(jnp.float32)
        outs.append(win_sum / cnt[None, :, None] - uf[:, POOL_BUF:, g])
    return jnp.stack(outs, axis=2).reshape(Bn, T, D_B)


def token_mixer(h, C0, n0, m0, pool_prefix, pos0, w_in, b_in, b_fgate, gn_gain, w_pool, pool_scale,
                w_branch_a, w_branch_b, w_out):
    Bn, T, _ = h.shape
    z = h @ w_in + b_in

    def heads(a, d):
        return a.reshape(Bn, T, MLSTM_HEADS, d).transpose(0, 2, 1, 3)

    q = heads(z[..., OFF_Q:OFF_K], MLSTM_DK)
    k = heads(z[..., OFF_K:OFF_V], MLSTM_DK) * (MLSTM_DK ** -0.5)
    v = heads(z[..., OFF_V:OFF_O], MLSTM_DV)
    o = jax.nn.sigmoid(z[..., OFF_O:OFF_I])
    ig = z[..., OFF_I:OFF_F].transpose(0, 2, 1)
    lf = jax.nn.log_sigmoid(z[..., OFF_F:OFF_U] + b_fgate).transpose(0, 2, 1)
    hA, mstate = mlstm_scan(q, k, v, ig, lf, (C0, n0, m0))
    hA = layer_norm(hA) * gn_gain.reshape(MLSTM_HEADS, 1, MLSTM_DV)
    hA = hA.transpose(0, 2, 1, 3).reshape(Bn, T, D_A) * o
    u = z[..., OFF_U:OFF_GA]
    u_ext = jnp.concatenate([pool_prefix.astype(u.dtype), u], axis=1)
    pooled = multiscale_pool(u_ext, pos0, T)
    pB = jnp.einsum('btgc,gcd->btgd', pooled.reshape(Bn, T, N_POOL_GROUPS, POOL_GROUP), w_pool)
    pB = pB.reshape(Bn, T, D_B) * pool_scale
    gA = jax.nn.sigmoid(z[..., OFF_GA:OFF_GB])
    gB = jax.nn.sigmoid(z[..., OFF_GB:N_IN])
    merged = gA * (hA @ w_branch_a) + gB * (pB @ w_branch_b)
    return merged @ w_out, mstate, u_ext[:, -POOL_BUF:]


def peer_ffn(h, w_peer_q, peer_subkeys, peer_u, peer_v):
    Bn, T, D = h.shape
    x = h.reshape(Bn * T, D)
    N = x.shape[0]
    q = (x @ w_peer_q).astype(jnp.float32).reshape(N, PEER_HEADS, 2, PEER_HALF)
    s = jnp.einsum('nhpd,pkd->nhpk', q, peer_subkeys.astype(jnp.float32))
    sv, si = lax.top_k(s, PEER_TOPK)
    cand = (sv[:, :, 0, :, None] + sv[:, :, 1, None, :]).reshape(N, PEER_HEADS, PEER_TOPK * PEER_TOPK)
    cand_id = (si[:, :, 0, :, None] * PEER_N_KEYS + si[:, :, 1, None, :]).reshape(N, PEER_HEADS, PEER_TOPK * PEER_TOPK)
    top_s, top_pos = lax.top_k(cand, PEER_TOPK)
    ids = jnp.take_along_axis(cand_id, top_pos, axis=-1).reshape(N, PEER_HEADS * PEER_TOPK)
    gates = jax.nn.softmax(top_s, axis=-1).reshape(N, PEER_HEADS * PEER_TOPK)
    nb = -(-N // PEER_TOKEN_BLOCK)
    pad = nb * PEER_TOKEN_BLOCK - N
    xb = jnp.pad(x, ((0, pad), (0, 0))).reshape(nb, PEER_TOKEN_BLOCK, D)
    idb = jnp.pad(ids, ((0, pad), (0, 0))).reshape(nb, PEER_TOKEN_BLOCK, -1)
    gb = jnp.pad(gates, ((0, pad), (0, 0))).reshape(nb, PEER_TOKEN_BLOCK, -1)

    def block(args):
        xs, ids_b, g_b = args
        act = jax.nn.gelu(jnp.einsum('ned,nd->ne', peer_u[ids_b], xs), approximate=False)
        return jnp.einsum('ne,ned->nd', g_b * act, peer_v[ids_b])

    y = lax.map(block, (xb, idb, gb)).reshape(nb * PEER_TOKEN_BLOCK, D)[:N]
    return y.reshape(Bn, T, D)


def decoder_layer(x, c, C0, n0, m0, pool_prefix, pos0, w_mod, b_mod, w_in, b_in, b_fgate, gn_gain, w_pool,
                  pool_scale, w_branch_a, w_branch_b, w_out, ln1_g, ln1_b, w_peer_q, peer_subkeys, peer_u,
                  peer_v, ln2_g, ln2_b):
    mod = (jax.nn.silu(c) @ w_mod + b_mod).astype(jnp.float32)
    sh1, sc1, g1, sh2, sc2, g2 = jnp.split(mod, 6, axis=-1)
    h1 = layer_norm(x) * (1.0 + sc1[:, None]) + sh1[:, None]
    t_out, mstate, pool_state = token_mixer(h1, C0, n0, m0, pool_prefix, pos0, w_in, b_in, b_fgate, gn_gain,
                                            w_pool, pool_scale, w_branch_a, w_branch_b, w_out)
    x1 = layer_norm(DEEPNORM_ALPHA * x.astype(jnp.float32) + g1[:, None] * t_out) * ln1_g + ln1_b
    h2 = layer_norm(x1) * (1.0 + sc2[:, None]) + sh2[:, None]
    f_out = peer_ffn(h2, w_peer_q, peer_subkeys, peer_u, peer_v)
    x2 = layer_norm(DEEPNORM_ALPHA * x1 + g2[:, None] * f_out) * ln2_g + ln2_b
    return x2, mstate, pool_state


def setup_inputs(seed: int = 0) -> dict:
    key = jax.random.key(seed)
    ks = jax.random.split(key, 32)
    f32 = jnp.float32

    def nrm(k, shape, s):
        return jax.random.normal(k, shape, f32) * s

    L = DEPTH
    return {
        'x_prompt': nrm(ks[0], (BATCH, SEQ, D_MODEL), 1.0),
        'x_sample': nrm(ks[1], (DEC_BATCH, DEC_SEQ, D_MODEL), 1.0),
        'c_prompt': nrm(ks[2], (BATCH, D_MODEL), 1.0),
        'c_sample': nrm(ks[3], (DEC_BATCH, D_MODEL), 1.0),
        'state_mlstm_C': nrm(ks[4], (L, DEC_BATCH, MLSTM_HEADS, MLSTM_DK, MLSTM_DV), MLSTM_DK ** -0.5),
        'state_mlstm_n': nrm(ks[5], (L, DEC_BATCH, MLSTM_HEADS, MLSTM_DK), MLSTM_DK ** -0.5),
        'state_mlstm_m': nrm(ks[6], (L, DEC_BATCH, MLSTM_HEADS), 1.0),
        'state_pool': nrm(ks[7], (L, DEC_BATCH, POOL_BUF, D_B), 1.0),
        'w_mod': nrm(ks[8], (L, D_MODEL, 6 * D_MODEL), 0.5 * D_MODEL ** -0.5),
        'b_mod': nrm(ks[9], (L, 6 * D_MODEL), 0.02),
        'w_in': nrm(ks[10], (L, D_MODEL, N_IN), D_MODEL ** -0.5),
        'b_in': nrm(ks[11], (L, N_IN), 0.02),
        'b_fgate': jnp.linspace(3.0, 6.0, MLSTM_HEADS, dtype=f32)[None, :] + nrm(ks[12], (L, MLSTM_HEADS), 0.1),
        'gn_gain': 1.0 + nrm(ks[13], (L, D_A), 0.05),
        'w_pool': nrm(ks[14], (L, N_POOL_GROUPS, POOL_GROUP, POOL_GROUP), POOL_GROUP ** -0.5),
        'pool_scale': 1.0 + nrm(ks[15], (L, D_B), 0.1),
        'w_branch_a': nrm(ks[16], (L, D_A, D_MODEL), D_A ** -0.5),
        'w_branch_b': nrm(ks[17], (L, D_B, D_MODEL), D_B ** -0.5),
        'w_out': nrm(ks[18], (L, D_MODEL, D_MODEL), DEEPNORM_BETA * D_MODEL ** -0.5),
        'ln1_g': 1.0 + nrm(ks[19], (L, D_MODEL), 0.05),
        'ln1_b': nrm(ks[20], (L, D_MODEL), 0.02),
        'w_peer_q': nrm(ks[21], (L, D_MODEL, PEER_HEADS * PEER_DKEY), D_MODEL ** -0.5),
        'peer_subkeys': nrm(ks[22], (L, 2, PEER_N_KEYS, PEER_HALF), PEER_HALF ** -0.5),
        'peer_u': nrm(ks[23], (L, PEER_N_EXPERTS, D_MODEL), D_MODEL ** -0.5),
        'peer_v': nrm(ks[24], (L, PEER_N_EXPERTS, D_MODEL), DEEPNORM_BETA * PEER_HEADS ** -0.5),
        'ln2_g': 1.0 + nrm(ks[25], (L, D_MODEL), 0.05),
        'ln2_b': nrm(ks[26], (L, D_MODEL), 0.02),
    }


def reference(x_prompt, x_sample, c_prompt, c_sample, state_mlstm_C, state_mlstm_n, state_mlstm_m, state_pool,
              w_mod, b_mod, w_in, b_in, b_fgate, gn_gain, w_pool, pool_scale, w_branch_a, w_branch_b, w_out,
              ln1_g, ln1_b, w_peer_q, peer_subkeys, peer_u, peer_v, ln2_g, ln2_b):
    bp = x_prompt.shape[0]
    yp = x_prompt
    ys = x_sample
    Cp_l, np_l, mp_l, pp_l, Cs_l, ns_l, ms_l, ps_l = [], [], [], [], [], [], [], []
    for l in range(DEPTH):
        lw = (w_mod[l], b_mod[l], w_in[l], b_in[l], b_fgate[l], gn_gain[l], w_pool[l], pool_scale[l],
              w_branch_a[l], w_branch_b[l], w_out[l], ln1_g[l], ln1_b[l], w_peer_q[l], peer_subkeys[l],
              peer_u[l], peer_v[l], ln2_g[l], ln2_b[l])
        C0 = jnp.zeros((bp, MLSTM_HEADS, MLSTM_DK, MLSTM_DV), jnp.float32)
        n0 = jnp.zeros((bp, MLSTM_HEADS, MLSTM_DK), jnp.float32)
        m0 = jnp.zeros((bp, MLSTM_HEADS), jnp.float32)
        pool0 = jnp.zeros((bp, POOL_BUF, D_B), jnp.float32)
        yp, (Cp, npp, mp), pp = decoder_layer(yp, c_prompt, C0, n0, m0, pool0, 0, *lw)
        ys, (Cs, ns, ms), ps = decoder_layer(ys, c_sample, state_mlstm_C[l], state_mlstm_n[l], state_mlstm_m[l],
                                             state_pool[l], PAST_LEN, *lw)
        Cp_l.append(Cp); np_l.append(npp); mp_l.append(mp); pp_l.append(pp)
        Cs_l.append(Cs); ns_l.append(ns); ms_l.append(ms); ps_l.append(ps)
    y_prompt = yp.astype(x_prompt.dtype)
    y_sample = ys.astype(x_sample.dtype)
    C_prompt = jnp.stack(Cp_l)
    n_prompt = jnp.stack(np_l)
    m_prompt = jnp.stack(mp_l)
    pool_prompt = jnp.stack(pp_l)
    C_sample = jnp.stack(Cs_l)
    n_sample = jnp.stack(ns_l)
    m_sample = jnp.stack(ms_l)
    pool_sample = jnp.stack(ps_l)
    return (y_prompt, y_sample, C_prompt, n_prompt, m_prompt, pool_prompt, C_sample, n_sample, m_sample, pool_sample)
```

```python
import os
import numpy as np
from contextlib import ExitStack
import concourse.bass as bass
import concourse.mybir as mybir
from concourse.bass_utils import run_bass_kernel_spmd

F32 = mybir.dt.float32
BF16 = mybir.dt.bfloat16
I32 = mybir.dt.int32
U32 = mybir.dt.uint32
ALU = mybir.AluOpType
AF = mybir.ActivationFunctionType
AX = mybir.AxisListType

D = 1024
NIN = 4616
ALPHA = 2.0 ** 0.25
EPS = 1e-5
KSCALE = 128.0 ** -0.5
NCORES = 8
SEQ = 2048
NS = 16
SCT = 256
NCH = SCT // 128
NSC = SEQ // SCT
POOLW = (2, 4, 8, 16)

ENGS = ["pe", "dve", "act", "pool", "sp"]


class Res:
    __slots__ = ("name", "w", "r", "al")

    def __init__(self, name):
        self.name = name
        self.w = None
        self.r = []
        self.al = []


class Prog:
    def __init__(self, nc, n_dma_sems):
        self.nc = nc
        self.q = {e: [] for e in ENGS}
        self.tick = {e: 0 for e in ENGS}
        self.esem = {e: nc.alloc_semaphore(name="es_" + e) for e in ENGS}
        self.waited = {e: {} for e in ENGS}
        self.dsem, self.dcnt, self.dval = {}, {}, {}
        for e, n in n_dma_sems.items():
            self.dsem[e] = [nc.alloc_semaphore(name="ds_%s%d" % (e, i)) for i in range(n)]
            self.dcnt[e] = 0
            self.dval[e] = [0] * n
        self.final_events = []
        self.pending = {e: [] for e in ENGS}

    def fence(self, eng, evs):
        self.pending[eng].extend(evs)

    def _collect(self, eng, reads, writes):
        evs = []
        for R in reads:
            if R.w is not None:
                evs.append((R.w, True))
        for R in writes:
            for Q in [R] + R.al:
                if Q.w is not None:
                    evs.append((Q.w, False))
                for ev in Q.r:
                    evs.append((ev, False))
        for ev in self.pending[eng]:
            evs.append((ev, True))
        self.pending[eng] = []
        best = {}
        for (ev, raw) in evs:
            sem, val, src = ev
            if src == eng and eng == "pe":
                continue
            k = id(sem)
            if k not in best or best[k][1] < val:
                best[k] = (sem, val)
        waits = []
        wd = self.waited[eng]
        for k, (sem, val) in best.items():
            if wd.get(k, 0) >= val:
                continue
            wd[k] = val
            waits.append((sem, val))
        return waits

    def _commit(self, ev, reads, writes):
        for R in reads:
            R.r.append(ev)
        for R in writes:
            R.w = ev
            R.r = []

    def op(self, eng, fn, reads=(), writes=()):
        waits = self._collect(eng, reads, writes)
        self.tick[eng] += 1
        ev = (self.esem[eng], self.tick[eng], eng)
        self.q[eng].append((waits, fn, (self.esem[eng], 1)))
        self._commit(ev, reads, writes)
        return ev

    def dma(self, eng, fn, reads=(), writes=(), final=False):
        waits = self._collect(eng, reads, writes)
        n = len(self.dsem[eng])
        i = self.dcnt[eng] % n
        self.dcnt[eng] += 1
        sem = self.dsem[eng][i]
        prev = self.dval[eng][i]
        if prev > 0 and self.waited[eng].get(id(sem), 0) < prev:
            self.waited[eng][id(sem)] = prev
            waits.append((sem, prev))
        val = prev + 16
        self.dval[eng][i] = val
        ev = (sem, val, "dma_" + eng)
        self.q[eng].append((waits, fn, (sem, 16)))
        self._commit(ev, reads, writes)
        if final:
            self.final_events.append(ev)
        return ev

    def emit(self, block):
        fin = {}
        for (sem, val, src) in self.final_events:
            if id(sem) not in fin or fin[id(sem)][1] < val:
                fin[id(sem)] = (sem, val)
        fin_waits = list(fin.values())

        def run(engname, e):
            for (waits, fn, inc) in self.q[engname]:
                for (sem, val) in waits:
                    e.wait_ge(sem, val)
                fn(e).then_inc(inc[0], inc[1])
            if engname == "sp":
                for (sem, val) in fin_waits:
                    e.wait_ge(sem, val)

        @block.tensor
        def _(e):
            run("pe", e)

        @block.vector
        def _(e):
            run("dve", e)

        @block.scalar
        def _(e):
            run("act", e)

        @block.gpsimd
        def _(e):
            run("pool", e)

        @block.sync
        def _(e):
            run("sp", e)


class Arena:
    def __init__(self, b, name, n, dt):
        self.t, _ = b.sb(name, [128, n], dt)
        self.items = []
        self.n = n

    def carve(self, name, off, shape):
        n = 1
        for d_ in shape[1:]:
            n *= d_
        assert off + n <= self.n, (name, off, n, self.n)
        R = Res(name)
        for (lo, hi, Q) in self.items:
            if lo < off + n and off < hi:
                R.al.append(Q)
                Q.al.append(R)
        self.items.append((off, off + n, R))
        ap = self.t[0:shape[0], off:off + n]
        if len(shape) == 3:
            ap = ap.rearrange("p (a b) -> p a b", a=shape[1])
        elif len(shape) == 4:
            ap = ap.rearrange("p (a b c) -> p a b c", a=shape[1], b=shape[2])
        return ap, R


class Builder:
    def __init__(self, debug=False):
        self.debug = debug
        self.nc = bass.Bass("TRN2", target_bir_lowering=False)
        self.P = Prog(self.nc, {"sp": 16, "pool": 12, "act": 4})
        self.es = ExitStack()
        self.dbg_names = []
        self.wcount = 0
        self.pmcount = 0

    def din(self, name, shape, dt=F32):
        return self.nc.dram_tensor(name, list(shape), dt, kind="ExternalInput").ap()

    def dout(self, name, shape, dt=F32):
        return self.nc.dram_tensor(name, list(shape), dt, kind="ExternalOutput").ap()

    def sb(self, name, shape, dt=F32):
        return self.es.enter_context(self.nc.sbuf_tensor(name, list(shape), dt)), Res(name)

    def ps(self, name, shape, dt=F32):
        return self.es.enter_context(self.nc.psum_tensor(name, list(shape), dt)), Res(name)

    def V(self, m, reads, writes, *a, **kw):
        return self.P.op("dve", lambda e: getattr(e, m)(*a, **kw), reads, writes)

    def A(self, m, reads, writes, *a, **kw):
        return self.P.op("act", lambda e: getattr(e, m)(*a, **kw), reads, writes)

    def G(self, m, reads, writes, *a, **kw):
        return self.P.op("pool", lambda e: getattr(e, m)(*a, **kw), reads, writes)

    def T(self, m, reads, writes, *a, **kw):
        return self.P.op("pe", lambda e: getattr(e, m)(*a, **kw), reads, writes)

    def DMA(self, q, reads, writes, final=False, **kw):
        return self.P.dma(q, lambda e: e.dma_start(**kw), reads, writes, final=final)

    def dump(self, name, ap, R, shape):
        if not self.debug:
            return
        d = self.dout("dbg_" + name, shape, ap.dtype)
        self.dbg_names.append("dbg_" + name)
        self.DMA("sp", [R], [], final=True, out=d, in_=ap)

    def build(self):
        nc, P = self.nc, self.P
        xp_d = self.din("xp", [SEQ, D]); xs_d = self.din("xs", [NS, D]); call_d = self.din("call", [NS + 1, D])
        sC_d = self.din("sC", [NS, 4, 128, 128]); sn_d = self.din("sn", [NS, 512]); sm_d = self.din("sm", [NS, 4])
        spool_d = self.din("spool", [NS, 15, 512])
        w_mod_d = self.din("w_mod", [D, 6 * D]); b_mod_d = self.din("b_mod", [1, 6 * D])
        w_in_d = self.din("w_in", [D, NIN]); b_in_d = self.din("b_in", [1, NIN]); b_fg_d = self.din("b_fgate", [1, 4])
        gn_d = self.din("gn_gain", [1, 512]); w_pool_d = self.din("w_pool", [4, 128, 128]); pscale_d = self.din("pool_scale", [1, 512])
        w_a_d = self.din("w_branch_a", [512, D]); w_b_d = self.din("w_branch_b", [512, D]); w_out_d = self.din("w_out", [D, D])
        ln1g_d = self.din("ln1_g", [1, D]); ln1b_d = self.din("ln1_b", [1, D])
        w_pq_d = self.din("w_peer_q", [D, 2048]); sk_d = self.din("peer_subkeys", [2, 128, 128])
        pu_d = self.din("peer_u", [16384, D]); pv_d = self.din("peer_v", [16384, D])
        ln2g_d = self.din("ln2_g", [1, D]); ln2b_d = self.din("ln2_b", [1, D])
        yp_d = self.dout("yp", [SEQ, D]); ys_d = self.dout("ys", [NS, D])
        Cp_d = self.dout("Cp", [4, 128, 128]); np_d = self.dout("np_", [4, 128]); mp_d = self.dout("mp", [1, 4]); pp_d = self.dout("pp", [15, 512])
        Cs_d = self.dout("Cs", [NS, 4, 128, 128]); ns_d = self.dout("ns", [NS, 512]); ms_d = self.dout("ms", [NS, 4]); pls_d = self.dout("pls", [NS, 15, 512])

        sb, ps, V, A, G, T, DMA = self.sb, self.ps, self.V, self.A, self.G, self.T, self.DMA
        def scr(name, shape):
            return nc.dram_tensor("scr_" + name, list(shape), BF16, kind="Internal").ap()
        wq_in = scr("w_in", [D, NIN]); wq_a = scr("w_a", [512, D]); wq_b = scr("w_b", [512, D]); wq_out = scr("w_out", [D, D]); wq_pq = scr("w_pq", [D, 2048])
        tabq = scr("tab", [16384, 2 * D])
        wmap = {id(w_in_d): wq_in, id(w_a_d): wq_a, id(w_b_d): wq_b, id(w_out_d): wq_out, id(w_pq_d): wq_pq}

        identf, Ridf = sb("identf", [128, 128]); identb, Ridb = sb("identb", [128, 128], BF16)
        ones, Rones = sb("ones", [128, 128]); maskT, Rmask = sb("maskT", [128, 128], BF16)
        selP, RselP = sb("selP", [NS + 1, 128]); eyeb, Reyeb = sb("eyeb", [128, 16, 16], BF16)
        iota16, Riota = sb("iota16", [128, 16]); inv16, Rinv = sb("inv16", [128, 4, 16])
        MOD, RMOD = sb("MOD", [128, 6 * D])
        lnbc, Rlnbc = sb("lnbc", [128, 4, D]); gnbc, Rgnbc = sb("gnbc", [128, 512])
        bias_tm, Rbtm = sb("bias_tm", [128, 2048], BF16); bias_if, Rbif = sb("bias_if", [128, 8]); bfg_bc, Rbfgbc = sb("bfg_bc", [128, 4])
        bcol, Rbcol = sb("bcol", [128, 28]); bIF, RbIF = sb("bIF", [4, 2]); bfg, Rbfg = sb("bfg", [4, 1])
        pscol, Rpscol = sb("pscol", [128, 4])
        wpool, Rwpool = sb("wpool", [128, 4, 128], BF16); skT, RskT = sb("skT", [128, 2, 128], BF16)
        stage = [sb("stage%d" % i, [128, 8, 512]) for i in range(2)]
        NWB = 3
        wbuf = [sb("wbuf%d" % i, [128, 8, 512], BF16) for i in range(NWB)]
        hT, RhT = sb("hT", [128, 8, SCT], BF16)
        xs_t = [sb("xs%d" % i, [128, D]) for i in range(NCH)]
        h2tok = [sb("h2tok%d" % i, [128, D], BF16) for i in range(NCH)]
        tmpA, RtmpA = sb("tmpA", [128, D]); tmpB, RtmpB = sb("tmpB", [128, D]); hb, Rhb = sb("hb", [128, D], BF16)
        st6, Rst6 = sb("st6", [128, 4, 6]); mv, Rmv = sb("mv", [128, 4, 2]); rstd, Rrstd = sb("rstd", [128, 4])
        uT, RuT = sb("uT", [128, 4, 15 + SCT])
        Caug, RCaug = sb("Caug", [128, 4, 129]); mstate, Rmstate = sb("mstate", [4, 1])
        hraw, Rhraw = sb("hraw", [128, 4, 128]); den, Rden = sb("den", [128, 4])
        siu, Rsiu = sb("siu", [128, 16, 16], U32); tpu, Rtpu = sb("tpu", [128, 8, 16], U32)
        ta_i, Rta_i = sb("ta_i", [128, 8, 16], I32)
        ids2 = [sb("ids%d" % i, [128, 128], I32) for i in range(2)]
        NGB = 8
        gbuf = []
        for i in range(NGB):
            stg, Rstg = stage[i // 4]
            R_ = Res("gb%d" % i)
            R_.al.append(Rstg); Rstg.al.append(R_)
            q4 = i % 4
            gbuf.append((stg[:, 2 * q4:2 * q4 + 2, :].rearrange("p a n -> p (a n)").bitcast(BF16), R_))
        dg = [sb("dg%d" % i, [128, 128], BF16) for i in range(2)]
        AB = Arena(self, "arenaB", 12288, BF16)
        qT, RqT = AB.carve("qT", 0, [128, 4, SCT]); kT, RkT = AB.carve("kT", 1024, [128, 4, SCT])
        ktok, Rktok = AB.carve("ktok", 2048, [128, NCH, 512]); vaug, Rvaug = AB.carve("vaug", 3072, [128, NCH, 4, 129])
        osig, Rosig = AB.carve("osig", 4104, [128, NCH, 512]); hAT, RhAT = AB.carve("hAT", 5128, [128, 4, SCT])
        pooledT, Rpooled = AB.carve("pooledT", 6152, [128, 4, SCT]); pBT, RpBT = AB.carve("pBT", 7176, [128, 4, SCT])
        mergedT, Rmerged = AB.carve("mergedT", 8200, [128, 8, SCT]); gsig, Rgsig = AB.carve("gsig", 10248, [128, 4, SCT])
        mtmp, Rmtmp = AB.carve("mtmp", 11272, [128, SCT]); Cs_bf, RCs_bf = AB.carve("Cs_bf", 11528, [128, 129])
        wTt, RwTt = AB.carve("wTt", 11660, [128, 128]); kd, Rkd = AB.carve("kd", 11788, [128, 128])
        qpT, RqpT = AB.carve("qpT", 0, [128, 16, SCT])
        AFa = Arena(self, "arenaF", 9984, F32)
        zI, RzI = AFa.carve("zI", 0, [4, SCT]); zF, RzF = AFa.carve("zF", 256, [4, SCT]); brow, Rbrow = AFa.carve("brow", 512, [4, SCT])
        arow, Rarow = AFa.carve("arow", 768, [4, SCT]); ea_r, Rea_r = AFa.carve("ea_r", 1024, [4, SCT]); ds_r, Rds_r = AFa.carve("ds_r", 1280, [4, SCT])
        eb_r, Reb_r = AFa.carve("eb_r", 1536, [4, SCT])
        poolA, RpoolA = AFa.carve("poolA", 1792, [128, 15 + SCT]); poolB, RpoolB = AFa.carve("poolB", 2112, [128, 15 + SCT])
        gbc, Rgbc = AFa.carve("gbc", 2432, [128, 2, NCH, 4]); gcol, Rgcol = AFa.carve("gcol", 2464, [128, NCH, 3, 4])
        gsm, Rgsm = AFa.carve("gsm", 2496, [4, 8, NCH]); bd, Rbd = AFa.carve("bd", 2528, [4, 2, NCH, 4])
        utok, Rutok = AFa.carve("utok", 2560, [128, 512])
        sq, Rsq = AFa.carve("sq", 0, [NS, 512]); sk, Rsk = AFa.carve("sk", 512, [NS, 512]); sva, Rsva = AFa.carve("sva", 1024, [NS, 4, 129])
        Qm, RQm = AFa.carve("Qm", 1540, [128, 16, 16]); Cst, RCst = AFa.carve("Cst", 1796, [128, 16, 129]); Vm, RVm = AFa.carve("Vm", 3860, [NS, 16, 129])
        sn_t, Rsn_t = AFa.carve("sn_t", 5924, [NS, 512]); nso, Rnso = AFa.carve("nso", 6436, [NS, 512])
        sprows, Rsprows = AFa.carve("sprows", 6948, [128, 2, 512]); upre, Rupre = AFa.carve("upre", 7972, [128, 4, 16, 16])
        sutok, Rsutok = AFa.carve("sutok", 8996, [NS, 512]); psumg, Rpsumg = AFa.carve("psumg", 9508, [128, 4, 16])
        kds, Rkds = AFa.carve("kds", 9572, [NS, 128]); dcB, RdcB = AFa.carve("dcB", 9700, [128, 16]); DCm, RDCm = AFa.carve("DCm", 9716, [NS, 16])
        ssm, Rssm = AFa.carve("ssm", 9732, [NS, 16, 4]); sIF, RsIF = AFa.carve("sIF", 9796, [NS, 8]); qTs, RqTs = AFa.carve("qTs", 9804, [128, 4, NS])
        nT, RnT = AFa.carve("nT", 9868, [128, 4, NS])
        s_sb, Rs_sb = AFa.carve("s_sb", 0, [128, 16, 128]); cand, Rcand = AFa.carve("cand", 2048, [128, 8, 256]); oh, Roh = AFa.carve("oh", 4096, [128, 8, 16, 16])
        yacc, Ryacc = AFa.carve("yacc", 6144, [128, D]); s2, Rs2 = AFa.carve("s2", 7168, [128, 128]); sv, Rsv = AFa.carve("sv", 7296, [128, 16, 16])
        sif, Rsif = AFa.carve("sif", 7552, [128, 16, 16]); cand2, Rcand2 = AFa.carve("cand2", 7808, [128, 256]); tops, Rtops = AFa.carve("tops", 8064, [128, 8, 16])
        tpf, Rtpf = AFa.carve("tpf", 8192, [128, 8, 16]); ta, Rta = AFa.carve("ta", 8320, [128, 8, 16]); tb, Rtb = AFa.carve("tb", 8448, [128, 8, 16])
        i1, Ri1 = AFa.carve("i1", 8576, [128, 8, 16]); i2, Ri2 = AFa.carve("i2", 8704, [128, 8, 16]); gates, Rgates = AFa.carve("gates", 8832, [128, 8, 16])
        gsum, Rgsum = AFa.carve("gsum", 8960, [128, 8]); dots, Rdots = AFa.carve("dots", 8968, [128, 128]); wts, Rwts = AFa.carve("wts", 9096, [128, 128])
        gates_b, Rgates_b = AFa.carve("gates_b", 7168 - 128, [128, 8, 16])
        gates2 = [(gates, Rgates), (gates_b, Rgates_b)]
        jk, Rjk = AFa.carve("jk", 6144, [128, 512])
        jk = jk.bitcast(BF16)
        pm = [ps("pm%d" % i, [128, 512]) for i in range(2)]
        ptb, Rptb = ps("ptb", [128, 8, 128], BF16); ptf, Rptf = ps("ptf", [128, 512])
        pS, RpS = ps("pS", [128, 512]); pP, RpP = ps("pP", [128, 512])
        pt = [ps("pt%d" % i, [128, 512]) for i in range(2)]

        def next_pm():
            self.pmcount += 1
            return pm[self.pmcount % 2]

        G("memset", [], [Ridf], identf[:], 0.0)
        G("affine_select", [Ridf], [Ridf], out=identf[:], in_=identf[:], pattern=[[-1, 128]], compare_op=ALU.not_equal, fill=1.0, base=0, channel_multiplier=1)
        V("tensor_copy", [Ridf], [Ridb], out=identb[:], in_=identf[:])
        G("memset", [], [Rones], ones[:], 1.0)
        G("affine_select", [Rones], [Rmask], out=maskT[:], in_=ones[:], pattern=[[1, 128]], compare_op=ALU.is_ge, fill=0.0, base=0, channel_multiplier=-1)
        G("affine_select", [Rones], [RselP], out=selP[:], in_=ones[0:NS + 1, :], pattern=[[0, 128]], compare_op=ALU.is_equal, fill=0.0, base=-NS, channel_multiplier=1)
        G("memset", [], [Reyeb], eyeb[:], 1.0)
        G("affine_select", [Reyeb], [Reyeb], out=eyeb[:], in_=eyeb[:], pattern=[[1, 16], [-1, 16]], compare_op=ALU.is_equal, fill=0.0, base=0, channel_multiplier=0)
        G("iota", [], [Riota], iota16[:], pattern=[[1, 16]], base=0, channel_multiplier=0, allow_small_or_imprecise_dtypes=True)
        for g, w in enumerate(POOLW):
            V("tensor_scalar", [Riota], [Rinv], out=inv16[:, g, :], in0=iota16[:], scalar1=1.0, scalar2=float(w), op0=ALU.add, op1=ALU.min)
        V("reciprocal", [Rinv], [Rinv], out=inv16[:], in_=inv16[:])
        G("memset", [], [RCaug], Caug[:], 0.0)
        G("memset", [], [Rmstate], mstate[:], 0.0)
        G("memset", [], [RuT], uT[:], 0.0)

        def conv_dma(dst_ap, src_ap):
            sem = nc.alloc_semaphore(name="cv%d" % len(self.cv_sems))
            self.cv_sems.append(sem)
            ev = (sem, 16, "dma_pool")
            P.q["pool"].append((P._collect("pool", [], []), lambda e: e.dma_start(out=dst_ap, in_=src_ap), (sem, 16)))
            return ev
        self.cv_sems = []
        w_events, t_events = [], []
        for (w_d, wq, K, N) in ((w_in_d, wq_in, D, NIN), (w_a_d, wq_a, 512, D), (w_b_d, wq_b, 512, D), (w_out_d, wq_out, D, D), (w_pq_d, wq_pq, D, 2048)):
            c0 = 0
            while c0 < N:
                cw = min(2048, N - c0)
                w_events.append(conv_dma(wq[0:K, c0:c0 + cw], w_d[0:K, c0:c0 + cw]))
                c0 += cw
        for r0 in range(0, 16384, 1024):
            t_events.append(conv_dma(tabq[r0:r0 + 1024, 0:D], pu_d[r0:r0 + 1024, :]))
            t_events.append(conv_dma(tabq[r0:r0 + 1024, D:2 * D], pv_d[r0:r0 + 1024, :]))
        self.w_events, self.t_events = w_events, t_events

        for i, d_ in enumerate([ln1g_d, ln1b_d, ln2g_d, ln2b_d]):
            DMA("sp", [], [Rlnbc], out=lnbc[:, i, :], in_=d_[0:1, :].partition_broadcast(128))
        DMA("sp", [], [Rgnbc], out=gnbc[:], in_=gn_d[0:1, :].partition_broadcast(128))
        stb, Rstb = stage[1]
        DMA("sp", [], [Rstb], out=stb[:, 0:4, :].rearrange("p a n -> p (a n)"), in_=b_in_d[0:1, 0:2048].partition_broadcast(128))
        V("tensor_copy", [Rstb], [Rbtm], out=bias_tm[:], in_=stb[:, 0:4, :].rearrange("p a n -> p (a n)"))
        DMA("sp", [], [Rbif], out=bias_if[:], in_=b_in_d[0:1, 2048:2056].partition_broadcast(128))
        DMA("sp", [], [Rbfgbc], out=bfg_bc[:], in_=b_fg_d[0:1, :].partition_broadcast(128))
        V("tensor_tensor", [Rbif, Rbfgbc], [Rbif], out=bias_if[:, 4:8], in0=bias_if[:, 4:8], in1=bfg_bc[:], op=ALU.add)
        colparts = [(0, 4, 0), (4, 4, 512), (8, 4, 2056), (12, 8, 2568), (20, 8, 3592)]
        for (c0, nb, off) in colparts:
            DMA("sp", [], [Rbcol], out=bcol[:, c0:c0 + nb], in_=b_in_d[0, off:off + nb * 128].rearrange("(c p) -> p c", p=128),
                allow_slow_non_contiguous=True)
        DMA("sp", [], [RbIF], out=bIF[:], in_=b_in_d[0, 2048:2056].rearrange("(c p) -> p c", p=4), allow_slow_non_contiguous=True)
        DMA("sp", [], [Rbfg], out=bfg[:], in_=b_fg_d[0, 0:4].rearrange("(p o) -> p o", o=1))
        V("tensor_tensor", [RbIF, Rbfg], [RbIF], out=bIF[:, 1:2], in0=bIF[:, 1:2], in1=bfg[:], op=ALU.add)
        DMA("sp", [], [Rpscol], out=pscol[:], in_=pscale_d[0, :].rearrange("(g p) -> p g", p=128), allow_slow_non_contiguous=True)
        st0, Rst0 = stage[0]
        DMA("sp", [], [Rst0], out=st0[:, 0, :].rearrange("p (g d) -> p g d", g=4), in_=w_pool_d.rearrange("g c d -> c g d"))
        V("tensor_copy", [Rst0], [Rwpool], out=wpool[:], in_=st0[:, 0, :].rearrange("p (g d) -> p g d", g=4))
        DMA("sp", [], [Rst0], out=st0[:, 1, 0:256].rearrange("p (a d) -> p a d", a=2), in_=sk_d.rearrange("a k d -> k a d"))
        for a in range(2):
            T("transpose", [Rst0, Ridf], [Rptf], out=ptf[:, a * 128:(a + 1) * 128], in_=st0[:, 1, a * 128:(a + 1) * 128], identity=identf[:])
        A("copy", [Rptf], [RskT], out=skT[:], in_=ptf[:, 0:256].rearrange("p (a k) -> p a k", a=2))

        NR = NS + 1
        DMA("sp", [], [RtmpA], out=tmpA[0:NR, :], in_=call_d)
        A("activation", [RtmpA], [RtmpA], out=tmpA[0:NR, :], in_=tmpA[0:NR, :], func=AF.Silu)
        for k in range(8):
            T("transpose", [RtmpA, Ridf], [Rptf], out=ptf[:, k * NR:(k + 1) * NR], in_=tmpA[0:NR, k * 128:(k + 1) * 128], identity=identf[0:NR, 0:NR])
        siluT = tmpB[:, 0:8 * NR].rearrange("p (k r) -> p k r", k=8)
        A("copy", [Rptf], [RtmpB], out=siluT, in_=ptf[:, 0:8 * NR].rearrange("p (k r) -> p k r", k=8))
        for nb in range(12):
            stg, Rstg = stage[nb % 2]
            DMA("sp", [], [Rstg], out=stg[:], in_=w_mod_d[:, nb * 512:(nb + 1) * 512].rearrange("(k p) n -> p k n", p=128))
            jb = tmpA[0:NR, (nb % 2) * 512:(nb % 2) * 512 + 512]
            DMA("sp", [], [RtmpA], out=jb, in_=b_mod_d[0:1, nb * 512:(nb + 1) * 512].partition_broadcast(NR))
            pmt, Rpm = next_pm()
            for k in range(8):
                T("matmul", [RtmpB, Rstg], [Rpm], pmt[0:NR, :], lhsT=siluT[:, k, :], rhs=stg[:, k, :], start=(k == 0), stop=(k == 7))
            add1 = 1.0 if nb in (2, 3, 8, 9) else 0.0
            V("scalar_tensor_tensor", [Rpm, RtmpA], [RMOD], out=MOD[0:NR, nb * 512:(nb + 1) * 512], in0=pmt[0:NR, :], scalar=add1, in1=jb,
              op0=ALU.add, op1=ALU.add)
        self.dump("mod", MOD[0:NR, :], RMOD, [NR, 6 * D])

        def layer_norm_stats(x_ap, Rx, nt, slot):
            for c in range(2):
                V("bn_stats", [Rx], [Rst6], out=st6[0:nt, c, :], in_=x_ap[:, c * 512:(c + 1) * 512])
            V("bn_aggr", [Rst6], [Rmv], out=mv[0:nt, slot, :], in_=st6[0:nt, 0:2, :])
            A("activation", [Rmv], [Rrstd], out=rstd[0:nt, slot:slot + 1], in_=mv[0:nt, slot, 1:2], func=AF.Sqrt, bias=EPS, scale=1.0)
            V("reciprocal", [Rrstd], [Rrstd], out=rstd[0:nt, slot:slot + 1], in_=rstd[0:nt, slot:slot + 1])

        def ln_affine(x_ap, Rx, nt, mul_ap, add_ap, Rpar, out_ap, Rout):
            layer_norm_stats(x_ap, Rx, nt, 0)
            V("tensor_scalar", [Rx, Rmv, Rrstd], [RtmpB], out=tmpB[0:nt, :], in0=x_ap, scalar1=mv[0:nt, 0, 0:1], scalar2=rstd[0:nt, 0:1],
              op0=ALU.subtract, op1=ALU.mult)
            G("tensor_tensor", [RtmpB, Rpar], [RtmpB], out=tmpB[0:nt, :], in0=tmpB[0:nt, :], in1=mul_ap, op=ALU.mult)
            V("tensor_tensor", [RtmpB, Rpar], [Rout], out=out_ap, in0=tmpB[0:nt, :], in1=add_ap, op=ALU.add)

        def sub_res(big, names):
            out = []
            for n_ in names:
                R_ = Res(n_)
                R_.al.append(big); big.al.append(R_)
                out.append(R_)
            return out
        Rst6_s = sub_res(Rst6, ["st6_s0", "st6_s1"]); Rmv_s = sub_res(Rmv, ["mv_s0", "mv_s1"]); Rrstd_s = sub_res(Rrstd, ["rstd_s0", "rstd_s1"])
        lnscr = [(tmpA, RtmpA), (tmpB, RtmpB)]

        def ln_chain(x_ap, Rx, nt, mul_ap, add_ap, Rpar, out_ap, Rout, s_):
            scr, Rscr = lnscr[s_]
            R6, Rm, Rr = Rst6_s[s_], Rmv_s[s_], Rrstd_s[s_]
            sc = scr[0:nt, :]
            return [
                lambda: V("bn_stats", [Rx], [R6], out=st6[0:nt, 2 * s_, :], in_=x_ap[:, 0:512]),
                lambda: V("bn_stats", [Rx], [R6], out=st6[0:nt, 2 * s_ + 1, :], in_=x_ap[:, 512:1024]),
                lambda: V("bn_aggr", [R6], [Rm], out=mv[0:nt, s_, :], in_=st6[0:nt, 2 * s_:2 * s_ + 2, :]),
                lambda: A("activation", [Rm], [Rr], out=rstd[0:nt, s_:s_ + 1], in_=mv[0:nt, s_, 1:2], func=AF.Sqrt, bias=EPS, scale=1.0),
                lambda: V("reciprocal", [Rr], [Rr], out=rstd[0:nt, s_:s_ + 1], in_=rstd[0:nt, s_:s_ + 1]),
                lambda: V("tensor_scalar", [Rx, Rm, Rr], [Rscr], out=sc, in0=x_ap, scalar1=mv[0:nt, s_, 0:1], scalar2=rstd[0:nt, s_:s_ + 1],
                          op0=ALU.subtract, op1=ALU.mult),
                lambda: G("tensor_tensor", [Rscr, Rpar], [Rscr], out=sc, in0=sc, in1=mul_ap, op=ALU.mult),
                lambda: V("tensor_tensor", [Rscr, Rpar], [Rout], out=out_ap, in0=sc, in1=add_ap, op=ALU.add),
            ]

        def interleave(chains):
            n_ = max(len(c_) for c_ in chains)
            for i_ in range(n_):
                for c_ in chains:
                    if i_ < len(c_):
                        c_[i_]()

        def to_feature_major(src_bf, Rsrc, nt, tok0):
            for k in range(8):
                T("transpose", [Rsrc, Ridb], [Rptb], out=ptb[:, k, 0:nt], in_=src_bf[0:nt, k * 128:(k + 1) * 128], identity=identb[0:nt, 0:nt])
            A("copy", [Rptb], [RhT], out=hT[:, :, tok0:tok0 + nt], in_=ptb[:, :, 0:nt])

        def load_w(w_d, K, c0, ncols):
            kc = K // 128
            i = self.wcount % NWB
            self.wcount += 1
            wb, Rwb = wbuf[i]
            if self.w_events is not None:
                P.fence("sp", self.w_events)
                self.w_events = None
            DMA("sp", [], [Rwb], out=wb[:, 0:kc, 0:ncols], in_=wmap[id(w_d)][0:K, c0:c0 + ncols].rearrange("(k p) n -> p k n", p=128))
            return wb, Rwb, kc

        def proj_fm(wb, Rwb, kc, col0, M, act, Ract, ntok, evac, tok0=0):
            pmt, Rpm = next_pm()
            for k in range(kc):
                T("matmul", [Rwb, Ract], [Rpm], pmt[0:M, 0:ntok], lhsT=wb[:, k, col0:col0 + M], rhs=act[:, k, tok0:tok0 + ntok], start=(k == 0), stop=(k == kc - 1))
            evac(pmt[0:M, 0:ntok], Rpm)

        def proj_tm(wb, Rwb, kc, ncols, act, Ract, tok0, nt, evac):
            pmt, Rpm = next_pm()
            for k in range(kc):
                T("matmul", [Rwb, Ract], [Rpm], pmt[0:nt, 0:ncols], lhsT=act[:, k, tok0:tok0 + nt], rhs=wb[:, k, 0:ncols], start=(k == 0), stop=(k == kc - 1))
            evac(pmt[0:nt, 0:ncols], Rpm)

        def ha_finish(nt, ti, tok0):
            for h in range(4):
                V("bn_stats", [Rhraw], [Rst6], out=st6[0:nt, h, :], in_=hraw[0:nt, h, :])
                V("bn_aggr", [Rst6], [Rmv], out=mv[0:nt, h, :], in_=st6[0:nt, h:h + 1, :])
            A("activation", [Rmv], [Rrstd], out=rstd[0:nt, :], in_=mv[0:nt, :, 1], func=AF.Sqrt, bias=EPS, scale=1.0)
            V("reciprocal", [Rrstd], [Rrstd], out=rstd[0:nt, :], in_=rstd[0:nt, :])
            for h in range(4):
                V("tensor_scalar", [Rhraw, Rmv, Rrstd], [RtmpB], out=tmpB[0:nt, h * 128:(h + 1) * 128], in0=hraw[0:nt, h, :],
                  scalar1=mv[0:nt, h, 0:1], scalar2=rstd[0:nt, h:h + 1], op0=ALU.subtract, op1=ALU.mult)
            G("tensor_tensor", [RtmpB, Rgnbc], [RtmpB], out=tmpB[0:nt, 0:512], in0=tmpB[0:nt, 0:512], in1=gnbc[0:nt, :], op=ALU.mult)
            V("tensor_tensor", [RtmpB, Rosig], [Rhb], out=hb[0:nt, 0:512], in0=tmpB[0:nt, 0:512], in1=osig[0:nt, ti, :], op=ALU.mult)
            for hc in range(4):
                T("transpose", [Rhb, Ridb], [Rptb], out=ptb[:, hc, 0:nt], in_=hb[0:nt, hc * 128:(hc + 1) * 128], identity=identb[0:nt, 0:nt])
            A("copy", [Rptb], [RhAT], out=hAT[:, :, tok0:tok0 + nt], in_=ptb[:, 0:4, 0:nt])

        def process_sc(kind, sc_idx):
            sample = (kind == "sample")
            ntok = NS if sample else SCT
            tiles = [(0, NS)] if sample else [(i * 128, 128) for i in range(NCH)]
            x_src = xs_d if sample else xp_d[sc_idx * SCT:(sc_idx + 1) * SCT, :]
            y_dst = ys_d if sample else yp_d[sc_idx * SCT:(sc_idx + 1) * SCT, :]
            tag = "s" if sample else "p%d" % sc_idx

            def modv(i, nt):
                return MOD[0:nt, i * D:(i + 1) * D]

            chains = []
            for ti, (t0, nt) in enumerate(tiles):
                xt, Rxt = xs_t[ti]
                h2t, Rh2t = h2tok[ti]
                ch = [lambda xt=xt, Rxt=Rxt, t0=t0, nt=nt: DMA("sp", [], [Rxt], out=xt[0:nt, :], in_=x_src[t0:t0 + nt, :])]
                ch += ln_chain(xt[0:nt, :], Rxt, nt, modv(1, nt), modv(0, nt), RMOD, h2t[0:nt, :], Rh2t, ti % 2)
                ch += [lambda h2t=h2t, Rh2t=Rh2t, nt=nt, t0=t0: to_feature_major(h2t, Rh2t, nt, t0)]
                chains.append(ch)
            interleave(chains)
            if self.debug and (sample or sc_idx == 0):
                self.dump("h1T_" + tag, hT[:, :, 0:ntok], RhT, [128, 8, ntok])

            wb, Rwb, kc = load_w(w_in_d, D, 0, 512)
            dstq = qTs if sample else qT
            Rdq = RqTs if sample else RqT
            for h in range(4):
                proj_fm(wb, Rwb, kc, h * 128, 128, hT, RhT, ntok,
                        lambda p_, Rp, h=h: A("activation", [Rp, Rbcol], [Rdq], out=dstq[:, h, 0:ntok], in_=p_, func=AF.Identity, bias=bcol[:, h:h + 1], scale=1.0))
            if sample:
                proj_tm(wb, Rwb, kc, 512, hT, RhT, 0, NS,
                        lambda p_, Rp: V("tensor_tensor", [Rp, Rbtm], [Rsq], out=sq[:], in0=p_, in1=bias_tm[0:NS, 0:512], op=ALU.add))
            wb, Rwb, kc = load_w(w_in_d, D, 512, 512)
            if not sample:
                for h in range(4):
                    proj_fm(wb, Rwb, kc, h * 128, 128, hT, RhT, ntok,
                            lambda p_, Rp, h=h: A("activation", [Rp, Rbcol], [RkT], out=kT[:, h, 0:ntok], in_=p_, func=AF.Identity, bias=bcol[:, 4 + h:5 + h], scale=1.0))
                for ti, (t0, nt) in enumerate(tiles):
                    proj_tm(wb, Rwb, kc, 512, hT, RhT, t0, nt,
                            lambda p_, Rp, ti=ti, nt=nt: V("tensor_tensor", [Rp, Rbtm], [Rktok], out=ktok[0:nt, ti, :], in0=p_, in1=bias_tm[0:nt, 512:1024], op=ALU.add))
            else:
                proj_tm(wb, Rwb, kc, 512, hT, RhT, 0, NS,
                        lambda p_, Rp: V("tensor_tensor", [Rp, Rbtm], [Rsk], out=sk[:], in0=p_, in1=bias_tm[0:NS, 512:1024], op=ALU.add))
            wb, Rwb, kc = load_w(w_in_d, D, 1024, 512)
            if sample:
                G("memset", [], [Rsva], sva[:], 1.0)
                G("memset", [], [Rsprows], sprows[:], 0.0)
            else:
                G("memset", [], [Rvaug], vaug[:], 1.0)
            for ti, (t0, nt) in enumerate(tiles):
                if sample:
                    ev = lambda p_, Rp: V("tensor_tensor", [Rp, Rbtm], [Rsva], out=sva[:, :, 0:128], in0=p_.rearrange("p (h d) -> p h d", h=4),
                                          in1=bias_tm[0:NS, 1024:1536].rearrange("p (h d) -> p h d", h=4), op=ALU.add)
                else:
                    ev = lambda p_, Rp, ti=ti, nt=nt: V("tensor_tensor", [Rp, Rbtm], [Rvaug], out=vaug[0:nt, ti, :, 0:128], in0=p_.rearrange("p (h d) -> p h d", h=4),
                                                        in1=bias_tm[0:nt, 1024:1536].rearrange("p (h d) -> p h d", h=4), op=ALU.add)
                proj_tm(wb, Rwb, kc, 512, hT, RhT, t0, nt, ev)
            wb, Rwb, kc = load_w(w_in_d, D, 1536, 512)
            for ti, (t0, nt) in enumerate(tiles):
                def ev(p_, Rp, ti=ti, nt=nt):
                    V("tensor_tensor", [Rp, Rbtm], [RtmpA], out=tmpA[0:nt, 0:512], in0=p_, in1=bias_tm[0:nt, 1536:2048], op=ALU.add)
                    A("activation", [RtmpA], [Rosig], out=osig[0:nt, ti, :], in_=tmpA[0:nt, 0:512], func=AF.Sigmoid)
                proj_tm(wb, Rwb, kc, 512, hT, RhT, t0, nt, ev)
            wb, Rwb, kc = load_w(w_in_d, D, 2048, 8)
            if sample:
                proj_tm(wb, Rwb, kc, 8, hT, RhT, 0, NS,
                        lambda p_, Rp: V("tensor_tensor", [Rp, Rbif], [RsIF], out=sIF[:], in0=p_, in1=bias_if[0:NS, :], op=ALU.add))
            else:
                proj_fm(wb, Rwb, kc, 0, 4, hT, RhT, ntok,
                        lambda p_, Rp: A("activation", [Rp, RbIF], [RzI], out=zI[:, 0:ntok], in_=p_, func=AF.Identity, bias=bIF[:, 0:1], scale=1.0))
                proj_fm(wb, Rwb, kc, 4, 4, hT, RhT, ntok,
                        lambda p_, Rp: A("activation", [Rp, RbIF], [RzF], out=zF[:, 0:ntok], in_=p_, func=AF.Identity, bias=bIF[:, 1:2], scale=1.0))

            if not sample:
                nchunk = NCH
                A("activation", [RzF], [RzF], out=zF[:], in_=zF[:], func=AF.Exp, scale=-1.0)
                A("activation", [RzF], [RzF], out=zF[:], in_=zF[:], func=AF.Ln, bias=1.0, scale=1.0)
                V("tensor_scalar", [RzF], [RzF], out=zF[:], in0=zF[:], scalar1=-1.0, scalar2=None, op0=ALU.mult)
                for c in range(nchunk):
                    V("tensor_tensor_scan", [RzF, Rones], [Rbrow], out=brow[:, c * 128:(c + 1) * 128], data0=ones[0:4, :], data1=zF[:, c * 128:(c + 1) * 128],
                      initial=0.0, op0=ALU.mult, op1=ALU.add)
                V("tensor_tensor", [RzI, Rbrow], [Rarow], out=arow[:], in0=zI[:], in1=brow[:], op=ALU.subtract)
                V("tensor_reduce", [Rarow], [Rgsm], out=gsm[:, 0, :], in_=arow[:].rearrange("p (c t) -> p c t", c=NCH), axis=AX.X, op=ALU.max)
                bL = brow[:].rearrange("p (c t) -> p c t", c=NCH)[:, :, 127]
                V("tensor_tensor_scan", [Rgsm, Rbrow, Rmstate], [Rgsm], out=gsm[:, 1, :], data0=gsm[:, 0, :], data1=bL, initial=mstate[:, 0:1],
                  op0=ALU.max, op1=ALU.add)
                V("tensor_copy", [Rmstate], [Rgsm], out=gsm[:, 2, 0:1], in_=mstate[:, 0:1])
                V("tensor_copy", [Rgsm], [Rgsm], out=gsm[:, 2, 1:NCH], in_=gsm[:, 1, 0:NCH - 1])
                V("tensor_copy", [Rgsm], [Rmstate], out=mstate[:, 0:1], in_=gsm[:, 1, NCH - 1:NCH])
                V("tensor_tensor", [Rbrow, Rgsm], [Rgsm], out=gsm[:, 5, :], in0=bL, in1=gsm[:, 1, :], op=ALU.subtract)
                V("tensor_tensor", [Rgsm], [Rgsm], out=gsm[:, 6, :], in0=gsm[:, 5, :], in1=gsm[:, 2, :], op=ALU.add)
                A("activation", [Rgsm], [Rgsm], out=gsm[:, 3, :], in_=gsm[:, 6, :], func=AF.Exp)
                A("activation", [Rgsm], [Rgsm], out=gsm[:, 4, :], in_=gsm[:, 2, :], func=AF.Exp)
                A("activation", [Rarow], [Rea_r], out=ea_r[:], in_=arow[:], func=AF.Exp)
                V("tensor_scalar", [Rea_r], [Rea_r], out=ea_r[:], in0=ea_r[:], scalar1=KSCALE, scalar2=None, op0=ALU.mult)
                for c in range(nchunk):
                    A("activation", [Rarow, Rgsm], [Rds_r], out=ds_r[:, c * 128:(c + 1) * 128], in_=arow[:, c * 128:(c + 1) * 128], func=AF.Exp,
                      bias=gsm[:, 5, c:c + 1], scale=1.0)
                V("tensor_scalar", [Rds_r], [Rds_r], out=ds_r[:], in0=ds_r[:], scalar1=KSCALE, scalar2=None, op0=ALU.mult)
                A("activation", [Rbrow], [Reb_r], out=eb_r[:], in_=brow[:], func=AF.Exp, scale=-1.0)
                for c in range(nchunk):
                    for qi, (src, Rs) in enumerate([(ea_r, Rea_r), (ds_r, Rds_r), (eb_r, Reb_r)]):
                        T("transpose", [Rs, Ridf], [Rptf], out=ptf[:, (c * 3 + qi) * 4:(c * 3 + qi) * 4 + 4], in_=src[:, c * 128:(c + 1) * 128], identity=identf[0:4, 0:4])
                A("copy", [Rptf], [Rgcol], out=gcol[:].rearrange("p c q h -> p (c q h)"), in_=ptf[:, 0:12 * NCH])
                for qi, row in enumerate([3, 4]):
                    V("tensor_tensor", [Rgsm, Ridf], [Rbd], out=bd[:, qi, :, :], in0=gsm[:, row, :].unsqueeze(2).broadcast_to([4, NCH, 4]),
                      in1=identf[0:4, 0:4].unsqueeze(1).broadcast_to([4, NCH, 4]), op=ALU.mult)
                T("matmul", [Rones, Rbd], [Rptf], ptf[:, 64:64 + 8 * NCH], lhsT=ones[0:4, :], rhs=bd[:].rearrange("p q c h -> p (q c h)"), start=True, stop=True)
                A("copy", [Rptf], [Rgbc], out=gbc[:].rearrange("p q c h -> p (q c h)"), in_=ptf[:, 64:64 + 8 * NCH])
                for c in range(nchunk):
                    t0 = c * 128
                    for h in range(4):
                        V("tensor_scalar", [RCaug, Rgbc], [RCs_bf], out=Cs_bf[:], in0=Caug[:, h, :], scalar1=gbc[:, 1, c, h:h + 1], scalar2=None, op0=ALU.mult)
                        T("matmul", [RkT, RqT], [RpS], pS[:, 0:128], lhsT=kT[:, h, t0:t0 + 128], rhs=qT[:, h, t0:t0 + 128], start=True, stop=True)
                        V("scalar_tensor_tensor", [RpS, Rgcol, Rmask], [RwTt], out=wTt[:], in0=pS[:, 0:128], scalar=gcol[:, c, 0, h:h + 1], in1=maskT[:],
                          op0=ALU.mult, op1=ALU.mult)
                        T("matmul", [RwTt, Rvaug], [RpP], pP[:, 0:129], lhsT=wTt[:], rhs=vaug[:, c, h, :], start=True, stop=False)
                        T("matmul", [RqT, RCs_bf], [RpP], pP[:, 0:129], lhsT=qT[:, h, t0:t0 + 128], rhs=Cs_bf[:], start=False, stop=True)
                        A("activation", [RpP], [Rden], out=den[:, h:h + 1], in_=pP[:, 128:129], func=AF.Abs)
                        V("tensor_tensor", [Rden, Rgcol], [Rden], out=den[:, h:h + 1], in0=den[:, h:h + 1], in1=gcol[:, c, 2, h:h + 1], op=ALU.max)
                        V("reciprocal", [Rden], [Rden], out=den[:, h:h + 1], in_=den[:, h:h + 1])
                        V("tensor_scalar", [RpP, Rden], [Rhraw], out=hraw[:, h, :], in0=pP[:, 0:128], scalar1=den[:, h:h + 1], scalar2=None, op0=ALU.mult)
                        G("tensor_scalar", [Rktok, Rgcol], [Rkd], out=kd[:], in0=ktok[:, c, h * 128:(h + 1) * 128], scalar1=gcol[:, c, 1, h:h + 1], scalar2=None, op0=ALU.mult)
                        T("matmul", [Rkd, Rvaug], [RpP], pP[:, 256:385], lhsT=kd[:], rhs=vaug[:, c, h, :], start=True, stop=True)
                        V("scalar_tensor_tensor", [RCaug, Rgbc, RpP], [RCaug], out=Caug[:, h, :], in0=Caug[:, h, :], scalar=gbc[:, 0, c, h:h + 1], in1=pP[:, 256:385],
                          op0=ALU.mult, op1=ALU.add)
                    if self.debug and sc_idx == 0 and c == 0:
                        self.dump("hraw_p0", hraw[:], Rhraw, [128, 4, 128])
                    ha_finish(128, c, t0)
            else:
                S = lambda i: ssm[:, i, :]
                DMA("sp", [], [Rssm], out=ssm[:, 0, :], in_=sm_d)
                A("activation", [RsIF], [Rssm], out=S(1), in_=sIF[:, 4:8], func=AF.Exp, scale=-1.0)
                A("activation", [Rssm], [Rssm], out=S(1), in_=S(1), func=AF.Ln, bias=1.0, scale=1.0)
                V("tensor_scalar", [Rssm], [Rssm], out=S(1), in0=S(1), scalar1=-1.0, scalar2=None, op0=ALU.mult)
                V("tensor_tensor", [RsIF, Rssm], [Rssm], out=S(2), in0=sIF[:, 0:4], in1=S(1), op=ALU.subtract)
                V("tensor_tensor", [Rssm], [Rssm], out=S(3), in0=S(0), in1=S(2), op=ALU.max)
                V("tensor_tensor", [Rssm], [Rssm], out=S(4), in0=S(1), in1=S(3), op=ALU.add)
                DMA("sp", [Rssm], [], final=True, out=ms_d, in_=S(4))
                V("tensor_tensor", [Rsq, Rsk], [RtmpA], out=tmpA[0:NS, 0:512], in0=sq[:], in1=sk[:], op=ALU.mult)
                V("tensor_reduce", [RtmpA], [Rssm], out=S(5), in_=tmpA[0:NS, 0:512].rearrange("p (h d) -> p h d", h=4), axis=AX.X, op=ALU.add)
                A("activation", [Rssm], [Rssm], out=S(6), in_=S(2), func=AF.Exp)
                V("scalar_tensor_tensor", [Rssm], [Rssm], out=S(7), in0=S(6), scalar=KSCALE, in1=S(5), op0=ALU.mult, op1=ALU.mult)
                A("activation", [Rssm], [Rssm], out=S(8), in_=S(0), func=AF.Exp)
                A("activation", [Rssm], [Rssm], out=S(9), in_=S(1), func=AF.Exp, scale=-1.0)
                V("tensor_tensor", [RsIF, Rssm], [Rssm], out=S(10), in0=sIF[:, 0:4], in1=S(4), op=ALU.subtract)
                A("activation", [Rssm], [Rssm], out=S(10), in_=S(10), func=AF.Exp)
                V("tensor_scalar", [Rssm], [Rssm], out=S(10), in0=S(10), scalar1=KSCALE, scalar2=None, op0=ALU.mult)
                V("tensor_tensor", [Rssm], [Rssm], out=S(11), in0=S(1), in1=S(0), op=ALU.add)
                V("tensor_tensor", [Rssm], [Rssm], out=S(11), in0=S(11), in1=S(4), op=ALU.subtract)
                A("activation", [Rssm], [Rssm], out=S(11), in_=S(11), func=AF.Exp)
                DMA("sp", [], [Rsn_t], out=sn_t[:], in_=sn_d)
                for h in range(4):
                    T("transpose", [Rsn_t, Ridf], [Rptf], out=ptf[:, h * NS:(h + 1) * NS], in_=sn_t[:, h * 128:(h + 1) * 128], identity=identf[0:NS, 0:NS])
                A("copy", [Rptf], [RnT], out=nT[:].rearrange("p h j -> p (h j)"), in_=ptf[:, 0:4 * NS])
                for h in range(4):
                    DMA("sp", [], [RCst], out=Cst[:, :, 0:128], in_=sC_d[:, h].rearrange("j k v -> k j v"))
                    V("tensor_copy", [RnT], [RCst], out=Cst[:, :, 128], in_=nT[:, h, :])
                    V("tensor_tensor", [RqTs, Reyeb], [RQm], out=Qm[:], in0=qTs[:, h, :].unsqueeze(2).broadcast_to([128, 16, 16]), in1=eyeb[:], op=ALU.mult)
                    for j in range(NS):
                        T("matmul", [RQm, RCst], [RpP], pP[0:NS, 0:129], lhsT=Qm[:, j, :], rhs=Cst[:, j, :], start=(j == 0), stop=(j == NS - 1))
                    V("tensor_scalar", [RpP, Rssm], [RtmpA], out=tmpA[0:NS, 0:129], in0=pP[0:NS, 0:129], scalar1=ssm[:, 8, h:h + 1], scalar2=None, op0=ALU.mult)
                    V("scalar_tensor_tensor", [Rsva, Rssm, RtmpA], [RtmpA], out=tmpA[0:NS, 0:129], in0=sva[:, h, :], scalar=ssm[:, 7, h:h + 1], in1=tmpA[0:NS, 0:129],
                      op0=ALU.mult, op1=ALU.add)
                    V("scalar_tensor_tensor", [RtmpA], [Rden], out=den[0:NS, h:h + 1], in0=tmpA[0:NS, 128:129], scalar=-1.0, in1=tmpA[0:NS, 128:129], op0=ALU.mult, op1=ALU.max)
                    V("tensor_tensor", [Rden, Rssm], [Rden], out=den[0:NS, h:h + 1], in0=den[0:NS, h:h + 1], in1=ssm[:, 9, h:h + 1], op=ALU.max)
                    V("reciprocal", [Rden], [Rden], out=den[0:NS, h:h + 1], in_=den[0:NS, h:h + 1])
                    V("tensor_scalar", [RtmpA, Rden], [Rhraw], out=hraw[0:NS, h, :], in0=tmpA[0:NS, 0:128], scalar1=den[0:NS, h:h + 1], scalar2=None, op0=ALU.mult)
                    V("tensor_tensor", [Rsva, Ridf], [RVm], out=Vm[:], in0=sva[:, h, :].unsqueeze(1).broadcast_to([NS, 16, 129]),
                      in1=identf[0:NS, 0:NS].unsqueeze(2).broadcast_to([NS, 16, 129]), op=ALU.mult)
                    V("tensor_scalar", [Rsk, Rssm], [Rkds], out=kds[:], in0=sk[:, h * 128:(h + 1) * 128], scalar1=ssm[:, 10, h:h + 1], scalar2=None, op0=ALU.mult)
                    V("tensor_scalar", [Ridf, Rssm], [RDCm], out=DCm[:], in0=identf[0:NS, 0:NS], scalar1=ssm[:, 11, h:h + 1], scalar2=None, op0=ALU.mult)
                    T("matmul", [Rones, RDCm], [Rptf], ptf[:, 128:144], lhsT=ones[0:NS, :], rhs=DCm[:], start=True, stop=True)
                    A("copy", [Rptf], [RdcB], out=dcB[:], in_=ptf[:, 128:144])
                    for j in range(NS):
                        pst, Rpst = pt[j % 2]
                        T("matmul", [Rkds, RVm], [Rpst], pst[:, 0:129], lhsT=kds[:], rhs=Vm[:, j, :], start=True, stop=True)
                        V("scalar_tensor_tensor", [RCst, RdcB, Rpst], [RCst], out=Cst[:, j, :], in0=Cst[:, j, :], scalar=dcB[:, j:j + 1], in1=pst[:, 0:129],
                          op0=ALU.mult, op1=ALU.add)
                    DMA("sp", [RCst], [], final=True, out=Cs_d[:, h].rearrange("j k v -> k j v"), in_=Cst[:, :, 0:128])
                    V("tensor_copy", [RCst], [RtmpB], out=tmpB[:, 0:NS], in_=Cst[:, :, 128])
                    T("transpose", [RtmpB, Ridf], [Rptf], out=ptf[0:NS, 256:384], in_=tmpB[:, 0:NS], identity=identf[:])
                    A("copy", [Rptf], [Rnso], out=nso[:, h * 128:(h + 1) * 128], in_=ptf[0:NS, 256:384])
                DMA("sp", [Rnso], [], final=True, out=ns_d, in_=nso[:])
                self.dump("hraw_s", hraw[0:NS], Rhraw, [NS, 4, 128])
                ha_finish(NS, 0, 0)
            if self.debug and (sample or sc_idx == 0):
                self.dump("hAT_" + tag, hAT[:, :, 0:ntok], RhAT, [128, 4, ntok])

            wb, Rwb, kc = load_w(w_in_d, D, 2056, 512)
            for g in range(4):
                proj_fm(wb, Rwb, kc, g * 128, 128, hT, RhT, ntok,
                        lambda p_, Rp, g=g: A("activation", [Rp, Rbcol], [RuT], out=uT[:, g, 15:15 + ntok], in_=p_, func=AF.Identity, bias=bcol[:, 8 + g:9 + g], scale=1.0))
            if not sample:
                L = 15 + ntok
                for g, w in enumerate(POOLW):
                    src, Rsrc = uT[:, g, :], RuT
                    bufs = [(poolA, RpoolA), (poolB, RpoolB)]
                    d_, bi = 1, 0
                    while d_ < w:
                        dst, Rdst = bufs[bi]
                        eng = V if (g + bi) % 2 == 0 else G
                        lo = 2 * d_ - 1
                        eng("tensor_tensor", [Rsrc], [Rdst], out=dst[:, lo:L], in0=src[:, lo:L], in1=src[:, lo - d_:L - d_], op=ALU.add)
                        src, Rsrc = dst[:, :], Rdst
                        d_ *= 2
                        bi ^= 1
                    V("scalar_tensor_tensor", [Rsrc, RuT], [Rpooled], out=pooledT[:, g, 0:ntok], in0=src[:, 15:L], scalar=1.0 / w, in1=uT[:, g, 15:L],
                      op0=ALU.mult, op1=ALU.subtract)
                    if sc_idx == 0:
                        V("tensor_tensor", [Rsrc, Rinv], [RtmpA], out=tmpA[:, 0:16], in0=src[:, 15:31], in1=inv16[:, g, :], op=ALU.mult)
                        V("tensor_tensor", [RtmpA, RuT], [Rpooled], out=pooledT[:, g, 0:16], in0=tmpA[:, 0:16], in1=uT[:, g, 15:31], op=ALU.subtract)
                if sc_idx == NSC - 1:
                    for g in range(4):
                        T("transpose", [RuT, Ridf], [Rptf], out=ptf[:, g * 128:(g + 1) * 128], in_=uT[:, g, 15 + ntok - 128:15 + ntok], identity=identf[:])
                    A("copy", [Rptf], [Rutok], out=utok[:], in_=ptf[:, 0:512])
                    DMA("sp", [Rutok], [], final=True, out=pp_d, in_=utok[113:128, :])
                A("copy", [RuT], [RuT], out=uT[:, :, 0:15], in_=uT[:, :, ntok:ntok + 15])
            else:
                for j in range(NS):
                    r0 = (j % 8) * 16
                    DMA("sp", [], [Rsprows], out=sprows[r0:r0 + 15, j // 8, :], in_=spool_d[j])
                for t_ in range(2):
                    for g in range(4):
                        T("transpose", [Rsprows, Ridf], [Rptf], out=ptf[:, g * 128:(g + 1) * 128], in_=sprows[:, t_, g * 128:(g + 1) * 128], identity=identf[:])
                    A("copy", [Rptf], [Rupre], out=upre[:, :, t_ * 8:(t_ + 1) * 8, :].rearrange("p g j q -> p g (j q)"), in_=ptf[:, 0:512].rearrange("p (g r) -> p g r", g=4))
                for g, w in enumerate(POOLW):
                    V("tensor_reduce", [Rupre], [Rpsumg], out=psumg[:, g, :], in_=upre[:, g, :, 16 - w:15], axis=AX.X, op=ALU.add)
                    V("tensor_tensor", [Rpsumg, RuT], [Rpsumg], out=psumg[:, g, :], in0=psumg[:, g, :], in1=uT[:, g, 15:15 + NS], op=ALU.add)
                    V("scalar_tensor_tensor", [Rpsumg, RuT], [Rpooled], out=pooledT[:, g, 0:NS], in0=psumg[:, g, :], scalar=1.0 / w, in1=uT[:, g, 15:15 + NS],
                      op0=ALU.mult, op1=ALU.subtract)
                for j in range(NS):
                    r0 = (j % 8) * 16
                    DMA("sp", [Rsprows], [], final=True, out=pls_d[j, 0:14, :], in_=sprows[r0 + 1:r0 + 15, j // 8, :])
                for g in range(4):
                    T("transpose", [RuT, Ridf], [Rptf], out=ptf[0:NS, g * 128:(g + 1) * 128], in_=uT[:, g, 15:15 + NS], identity=identf[:])
                A("copy", [Rptf], [Rsutok], out=sutok[0:NS, :], in_=ptf[0:NS, 0:512])
                DMA("sp", [Rsutok], [], final=True, out=pls_d[:, 14, :], in_=sutok[0:NS, :])
                G("memset", [RuT], [RuT], uT[:, :, 0:15], 0.0)
            if self.debug and (sample or sc_idx == 0):
                self.dump("pooledT_" + tag, pooledT[:, :, 0:ntok], Rpooled, [128, 4, ntok])
            for g in range(4):
                pmt, Rpm = next_pm()
                T("matmul", [Rwpool, Rpooled], [Rpm], pmt[:, 0:ntok], lhsT=wpool[:, g, :], rhs=pooledT[:, g, 0:ntok], start=True, stop=True)
                V("tensor_scalar", [Rpm, Rpscol], [RpBT], out=pBT[:, g, 0:ntok], in0=pmt[:, 0:ntok], scalar1=pscol[:, g:g + 1], scalar2=None, op0=ALU.mult)

            for half in range(2):
                wb, Rwb, kc = load_w(w_in_d, D, 2568 + half * 512, 512)
                for j in range(4):
                    proj_fm(wb, Rwb, kc, j * 128, 128, hT, RhT, ntok,
                            lambda p_, Rp, j=j: A("activation", [Rp, Rbcol], [Rgsig], out=gsig[:, j, 0:ntok], in_=p_, func=AF.Sigmoid,
                                                  bias=bcol[:, 12 + half * 4 + j:13 + half * 4 + j], scale=1.0))
                wb, Rwb, kc = load_w(w_a_d, 512, half * 512, 512)
                for j in range(4):
                    proj_fm(wb, Rwb, kc, j * 128, 128, hAT, RhAT, ntok,
                            lambda p_, Rp, j=j: V("tensor_tensor", [Rp, Rgsig], [Rmerged], out=mergedT[:, half * 4 + j, 0:ntok], in0=p_, in1=gsig[:, j, 0:ntok], op=ALU.mult))
                wb, Rwb, kc = load_w(w_in_d, D, 3592 + half * 512, 512)
                for j in range(4):
                    proj_fm(wb, Rwb, kc, j * 128, 128, hT, RhT, ntok,
                            lambda p_, Rp, j=j: A("activation", [Rp, Rbcol], [Rgsig], out=gsig[:, j, 0:ntok], in_=p_, func=AF.Sigmoid,
                                                  bias=bcol[:, 20 + half * 4 + j:21 + half * 4 + j], scale=1.0))
                wb, Rwb, kc = load_w(w_b_d, 512, half * 512, 512)
                for j in range(4):
                    def ev(p_, Rp, j=j):
                        V("tensor_tensor", [Rp, Rgsig], [Rmtmp], out=mtmp[:, 0:ntok], in0=p_, in1=gsig[:, j, 0:ntok], op=ALU.mult)
                        G("tensor_tensor", [Rmtmp, Rmerged], [Rmerged], out=mergedT[:, half * 4 + j, 0:ntok], in0=mergedT[:, half * 4 + j, 0:ntok], in1=mtmp[:, 0:ntok], op=ALU.add)
                    proj_fm(wb, Rwb, kc, j * 128, 128, pBT, RpBT, ntok, ev)
            if self.debug and (sample or sc_idx == 0):
                self.dump("mergedT_" + tag, mergedT[:, :, 0:ntok], Rmerged, [128, 8, ntok])

            wo = [load_w(w_out_d, D, half * 512, 512) for half in range(2)]
            chains = []
            for ti, (t0, nt) in enumerate(tiles):
                xt, Rxt = xs_t[ti]
                h2t, Rh2t = h2tok[ti]
                s_ = ti % 2
                scr, Rscr = lnscr[s_]
                banks = pt if s_ == 0 else pm

                def tout(t0=t0, nt=nt, scr=scr, Rscr=Rscr, banks=banks):
                    for half in range(2):
                        wb, Rwb, kc = wo[half]
                        ptt, Rptt = banks[half]
                        for k in range(8):
                            T("matmul", [Rmerged, Rwb], [Rptt], ptt[0:nt, :], lhsT=mergedT[:, k, t0:t0 + nt], rhs=wb[:, k, :], start=(k == 0), stop=(k == 7))
                        V("tensor_tensor", [Rptt, RMOD], [Rscr], out=scr[0:nt, half * 512:(half + 1) * 512], in0=ptt[0:nt, :],
                          in1=MOD[0:nt, 2 * D + half * 512:2 * D + (half + 1) * 512], op=ALU.mult)
                ch = [tout,
                      lambda xt=xt, Rxt=Rxt, nt=nt, scr=scr, Rscr=Rscr: V("scalar_tensor_tensor", [Rxt, Rscr], [Rscr], out=scr[0:nt, :], in0=xt[0:nt, :], scalar=ALPHA,
                                                                         in1=scr[0:nt, :], op0=ALU.mult, op1=ALU.add)]
                ch += ln_chain(scr[0:nt, :], Rscr, nt, lnbc[0:nt, 0, :], lnbc[0:nt, 1, :], Rlnbc, xt[0:nt, :], Rxt, s_)
                ch += ln_chain(xt[0:nt, :], Rxt, nt, modv(4, nt), modv(3, nt), RMOD, h2t[0:nt, :], Rh2t, s_)
                ch += [lambda h2t=h2t, Rh2t=Rh2t, nt=nt, t0=t0: to_feature_major(h2t, Rh2t, nt, t0)]
                chains.append(ch)
            interleave(chains)
            if self.debug and (sample or sc_idx == 0):
                self.dump("x1_" + tag, xs_t[0][0][0:tiles[0][1], :], xs_t[0][1], [tiles[0][1], D])

            def qp_proj(t0, nt):
                for blk in range(4):
                    wb, Rwb, kc = load_w(w_pq_d, D, blk * 512, 512)
                    for j in range(4):
                        proj_fm(wb, Rwb, kc, j * 128, 128, hT, RhT, nt,
                                lambda p_, Rp, j=j, blk=blk: A("copy", [Rp], [RqpT], out=qpT[:, blk * 4 + j, t0:t0 + nt], in_=p_), tok0=t0)
            def topk_thunks(ti, t0, nt):
                th = []
                par = ti % 2
                idt, Ridt = ids2[par]
                gat, Rgat = gates2[par]
                Rs_g = [Res("s_g%d" % g) for g in range(16)]
                Rsv_g = [Res("sv_g%d" % g) for g in range(16)]
                Rsi_g = [Res("si_g%d" % g) for g in range(16)]
                Rc_h = [Res("c_h%d" % h) for h in range(8)]
                Rt_h = [Res("t_h%d" % h) for h in range(8)]
                Rtp_h = [Res("tp_h%d" % h) for h in range(8)]
                Roh_h = [Res("oh_h%d" % h) for h in range(8)]
                for R_ in Rs_g:
                    R_.al.append(Rs_sb); Rs_sb.al.append(R_)
                for (lst, big) in ((Rsv_g, Rsv), (Rsi_g, Rsiu), (Rc_h, Rcand), (Rt_h, Rtops), (Rtp_h, Rtpu), (Roh_h, Roh)):
                    for R_ in lst:
                        R_.al.append(big); big.al.append(R_)
                for gq in range(4):
                    def f(gq=gq):
                        pmt, Rpm = next_pm()
                        for j in range(4):
                            gi = gq * 4 + j
                            T("matmul", [RqpT, RskT], [Rpm], pmt[0:nt, j * 128:(j + 1) * 128], lhsT=qpT[:, gi, t0:t0 + nt], rhs=skT[:, gi % 2, :], start=True, stop=True)
                        A("copy", [Rpm], Rs_g[gq * 4:gq * 4 + 4], out=s_sb[0:nt, gq * 4:(gq + 1) * 4, :].rearrange("p g k -> p (g k)"), in_=pmt[0:nt, :])
                    th.append(f)
                for gi in range(16):
                    th.append(lambda gi=gi: V("max", [Rs_g[gi]], [Rsv_g[gi]], out=sv[0:nt, gi, 0:8], in_=s_sb[0:nt, gi, :]))
                for gi in range(16):
                    th.append(lambda gi=gi: V("max_index", [Rs_g[gi], Rsv_g[gi]], [Rsi_g[gi]], out=siu[0:nt, gi, 0:8], in_max=sv[0:nt, gi, 0:8], in_values=s_sb[0:nt, gi, :]))
                for gi in range(16):
                    th.append(lambda gi=gi: V("match_replace", [Rs_g[gi], Rsv_g[gi]], [Rs_g[gi]], out=s_sb[0:nt, gi, :], in_to_replace=sv[0:nt, gi, 0:8],
                                              in_values=s_sb[0:nt, gi, :], imm_value=-1e30))
                for gi in range(16):
                    th.append(lambda gi=gi: V("max", [Rs_g[gi]], [Rsv_g[gi]], out=sv[0:nt, gi, 8:16], in_=s_sb[0:nt, gi, :]))
                for gi in range(16):
                    th.append(lambda gi=gi: V("max_index", [Rs_g[gi], Rsv_g[gi]], [Rsi_g[gi]], out=siu[0:nt, gi, 8:16], in_max=sv[0:nt, gi, 8:16], in_values=s_sb[0:nt, gi, :]))
                th.append(lambda: V("tensor_copy", Rsi_g, [Rsif], out=sif[0:nt], in_=siu[0:nt]))
                svv = sv[0:nt].rearrange("p (h a) k -> p h a k", a=2)
                sfv = sif[0:nt].rearrange("p (h a) k -> p h a k", a=2)
                for h in range(8):
                    th.append(lambda h=h: V("tensor_tensor", [Rsv_g[2 * h], Rsv_g[2 * h + 1]], [Rc_h[h]], out=cand[0:nt, h, :].rearrange("p (a b) -> p a b", a=16),
                                            in0=svv[:, h, 0, :].unsqueeze(2).broadcast_to([nt, 16, 16]), in1=svv[:, h, 1, :].unsqueeze(1).broadcast_to([nt, 16, 16]), op=ALU.add))
                for h in range(8):
                    th.append(lambda h=h: V("max", [Rc_h[h]], [Rt_h[h]], out=tops[0:nt, h, 0:8], in_=cand[0:nt, h, :]))
                for h in range(8):
                    th.append(lambda h=h: V("max_index", [Rc_h[h], Rt_h[h]], [Rtp_h[h]], out=tpu[0:nt, h, 0:8], in_max=tops[0:nt, h, 0:8], in_values=cand[0:nt, h, :]))
                for h in range(8):
                    th.append(lambda h=h: V("match_replace", [Rc_h[h], Rt_h[h]], [Rc_h[h]], out=cand[0:nt, h, :], in_to_replace=tops[0:nt, h, 0:8],
                                            in_values=cand[0:nt, h, :], imm_value=-1e30))
                for h in range(8):
                    th.append(lambda h=h: V("max", [Rc_h[h]], [Rt_h[h]], out=tops[0:nt, h, 8:16], in_=cand[0:nt, h, :]))
                for h in range(8):
                    th.append(lambda h=h: V("max_index", [Rc_h[h], Rt_h[h]], [Rtp_h[h]], out=tpu[0:nt, h, 8:16], in_max=tops[0:nt, h, 8:16], in_values=cand[0:nt, h, :]))
                th.append(lambda: V("tensor_copy", Rtp_h, [Rtpf], out=tpf[0:nt], in_=tpu[0:nt]))
                th.append(lambda: V("tensor_scalar", [Rtpf], [Rta_i], out=ta_i[0:nt], in0=tpf[0:nt], scalar1=-7.5, scalar2=1.0 / 16.0, op0=ALU.add, op1=ALU.mult))
                th.append(lambda: V("tensor_copy", [Rta_i], [Rta], out=ta[0:nt], in_=ta_i[0:nt]))
                th.append(lambda: V("scalar_tensor_tensor", [Rta, Rtpf], [Rtb], out=tb[0:nt], in0=ta[0:nt], scalar=-16.0, in1=tpf[0:nt], op0=ALU.mult, op1=ALU.add))
                for (sel, half_, dst, Rdst) in [(ta, 0, i1, Ri1), (tb, 1, i2, Ri2)]:
                    Rsel = Rta if half_ == 0 else Rtb
                    for h in range(8):
                        th.append(lambda h=h, sel=sel, Rsel=Rsel: V("tensor_tensor", [Rsel, Riota], [Roh_h[h]], out=oh[0:nt, h], in0=sel[0:nt, h, :].unsqueeze(2).broadcast_to([nt, 16, 16]),
                                                                    in1=iota16[0:nt, :].unsqueeze(1).broadcast_to([nt, 16, 16]), op=ALU.is_equal))
                    for h in range(8):
                        th.append(lambda h=h, half_=half_: G("tensor_tensor", [Roh_h[h], Rsif], [Roh_h[h]], out=oh[0:nt, h], in0=oh[0:nt, h],
                                                             in1=sfv[:, h, half_, :].unsqueeze(1).broadcast_to([nt, 16, 16]), op=ALU.mult))
                    th.append(lambda dst=dst, Rdst=Rdst: V("tensor_reduce", Roh_h, [Rdst], out=dst[0:nt], in_=oh[0:nt], axis=AX.X, op=ALU.add))
                th.append(lambda: V("scalar_tensor_tensor", [Ri1, Ri2], [Ridt], out=idt[0:nt, :], in0=i1[0:nt].rearrange("p h k -> p (h k)"), scalar=128.0,
                                    in1=i2[0:nt].rearrange("p h k -> p (h k)"), op0=ALU.mult, op1=ALU.add))
                th.append(lambda: V("tensor_tensor", Rt_h, [Rgat], out=gat[0:nt], in0=tops[0:nt], in1=tops[0:nt, :, 0:1].broadcast_to([nt, 8, 16]), op=ALU.subtract))
                th.append(lambda: A("activation", [Rgat], [Rgat], out=gat[0:nt], in_=gat[0:nt], func=AF.Exp))
                th.append(lambda: V("tensor_reduce", [Rgat], [Rgsum], out=gsum[0:nt], in_=gat[0:nt], axis=AX.X, op=ALU.add))
                th.append(lambda: V("reciprocal", [Rgsum], [Rgsum], out=gsum[0:nt], in_=gsum[0:nt]))
                th.append(lambda: V("tensor_tensor", [Rgat, Rgsum], [Rgat], out=gat[0:nt], in0=gat[0:nt], in1=gsum[0:nt].unsqueeze(2).broadcast_to([nt, 8, 16]), op=ALU.mult))
                return th

            def gather_phase(ti, t0, nt, side):
                xt, Rxt = xs_t[ti]
                h2t, Rh2t = h2tok[ti]
                par = ti % 2
                idt, Ridt = ids2[par]
                gat, Rgat = gates2[par]
                Rdot = [Res("dot%d" % j) for j in range(128)]
                Ract = [Res("act%d" % j) for j in range(128)]
                gflat = gat[0:nt].rearrange("p h k -> p (h k)")
                ybank = pt if ti % 2 == 0 else pm
                side = list(side)
                per = (len(side) + 99) // 100

                def slot_tail(j):
                    gb, Rgb = gbuf[j % NGB]
                    dgt, Rdg = dg[j % 2]
                    V("tensor_scalar", [Ridb, Ract[j], Rgat], [Rdg], out=dgt[0:nt, 0:nt], in0=identb[0:nt, 0:nt], scalar1=wts[0:nt, j:j + 1],
                      scalar2=gflat[:, j:j + 1], op0=ALU.mult, op1=ALU.mult)
                    for half in range(2):
                        ptt, Rptt = ybank[half]
                        T("matmul", [Rdg, Rgb], [Rptt], ptt[0:nt, :], lhsT=dgt[0:nt, 0:nt], rhs=gb[0:nt, D + half * 512:D + (half + 1) * 512],
                          start=(j == 0), stop=(j == 127))

                if self.t_events is not None:
                    P.fence("pool", self.t_events)
                    self.t_events = None
                for j in range(128):
                    gb, Rgb = gbuf[j % NGB]
                    P.dma("pool", lambda e, gb=gb, j=j: e.indirect_dma_start(out=gb[0:nt, :], out_offset=None, in_=tabq,
                                                                          in_offset=bass.IndirectOffsetOnAxis(ap=idt[0:nt, j:j + 1], axis=0)),
                          [Ridt], [Rgb])
                    pb, Rpb = ((jk, Rjk), (hb, Rhb))[j % 2]
                    V("tensor_tensor", [Rgb, Rh2t], [Rpb], out=pb[0:nt, :], in0=gb[0:nt, 0:D], in1=h2t[0:nt, :], op=ALU.mult)
                    A("activation", [Rpb], [Rpb, Rdot[j]], out=pb[0:nt, :], in_=pb[0:nt, :], func=AF.Copy, accum_out=dots[0:nt, j:j + 1])
                    A("activation", [Rdot[j]], [Ract[j]], out=wts[0:nt, j:j + 1], in_=dots[0:nt, j:j + 1], func=AF.Gelu)
                    if j >= 1:
                        slot_tail(j - 1)
                    for _ in range(per):
                        if side:
                            side.pop(0)()
                slot_tail(127)
                while side:
                    side.pop(0)()
                for half in range(2):
                    V("tensor_tensor", [ybank[half][1], RMOD], [RtmpA], out=tmpA[0:nt, half * 512:(half + 1) * 512], in0=ybank[half][0][0:nt, :],
                      in1=MOD[0:nt, 5 * D + half * 512:5 * D + (half + 1) * 512], op=ALU.mult)
                V("scalar_tensor_tensor", [Rxt, RtmpA], [RtmpA], out=tmpA[0:nt, :], in0=xt[0:nt, :], scalar=ALPHA, in1=tmpA[0:nt, :], op0=ALU.mult, op1=ALU.add)
                ln_affine(tmpA[0:nt, :], RtmpA, nt, lnbc[0:nt, 2, :], lnbc[0:nt, 3, :], Rlnbc, xt[0:nt, :], Rxt)
                DMA("sp", [Rxt], [], final=True, out=y_dst[t0:t0 + nt, :], in_=xt[0:nt, :])

            tk = [topk_thunks(ti, t0, nt) for ti, (t0, nt) in enumerate(tiles)]
            qp_proj(*tiles[0])
            for f_ in tk[0]:
                f_()
            if len(tiles) > 1:
                qp_proj(tiles[1][0], ntok - tiles[1][0])
            for ti, (t0, nt) in enumerate(tiles):
                gather_phase(ti, t0, nt, tk[ti + 1] if ti + 1 < len(tiles) else [])

        process_sc("sample", 0)
        for nb in range(12):
            pmt, Rpm = next_pm()
            T("matmul", [RselP, RMOD], [Rpm], pmt[:, :], lhsT=selP[:, :], rhs=MOD[0:NS + 1, nb * 512:(nb + 1) * 512], start=True, stop=True)
            A("copy", [Rpm], [RMOD], out=MOD[:, nb * 512:(nb + 1) * 512], in_=pmt[:, :])
        for sc in range(NSC):
            process_sc("prompt", sc)
        DMA("sp", [RCaug], [], final=True, out=Cp_d.rearrange("h k v -> k h v"), in_=Caug[:, :, 0:128])
        V("tensor_copy", [RCaug], [RtmpB], out=tmpB[:, 0:4], in_=Caug[:, :, 128])
        T("transpose", [RtmpB, Ridf], [Rptf], out=ptf[0:4, 0:128], in_=tmpB[:, 0:4], identity=identf[:])
        A("copy", [Rptf], [RtmpA], out=tmpA[0:4, 0:128], in_=ptf[0:4, 0:128])
        DMA("sp", [RtmpA], [], final=True, out=np_d, in_=tmpA[0:4, 0:128])
        DMA("sp", [Rmstate], [], final=True, out=mp_d.rearrange("o h -> h o"), in_=mstate[:, 0:1])

        with nc.Block() as block:
            P.emit(block)
        self.es.close()
        return nc


_LAST = {}


def kernel(**inputs):
    debug = bool(int(os.environ.get("KDEBUG", "0")))
    ncores = int(os.environ.get("KCORES", str(NCORES)))
    f = lambda a: np.ascontiguousarray(np.asarray(a, dtype=np.float32))
    x_prompt = f(inputs["x_prompt"]); x_sample = f(inputs["x_sample"]); c_prompt = f(inputs["c_prompt"]); c_sample = f(inputs["c_sample"])
    sC = f(inputs["state_mlstm_C"])[0]; sn = f(inputs["state_mlstm_n"])[0]; sm = f(inputs["state_mlstm_m"])[0]; spool = f(inputs["state_pool"])[0]
    shared = {
        "w_mod": f(inputs["w_mod"])[0], "b_mod": f(inputs["b_mod"]), "w_in": f(inputs["w_in"])[0], "b_in": f(inputs["b_in"]),
        "b_fgate": f(inputs["b_fgate"]), "gn_gain": f(inputs["gn_gain"]), "w_pool": f(inputs["w_pool"])[0], "pool_scale": f(inputs["pool_scale"]),
        "w_branch_a": f(inputs["w_branch_a"])[0], "w_branch_b": f(inputs["w_branch_b"])[0], "w_out": f(inputs["w_out"])[0],
        "ln1_g": f(inputs["ln1_g"]), "ln1_b": f(inputs["ln1_b"]), "w_peer_q": f(inputs["w_peer_q"])[0], "peer_subkeys": f(inputs["peer_subkeys"])[0],
        "peer_u": f(inputs["peer_u"])[0], "peer_v": f(inputs["peer_v"])[0], "ln2_g": f(inputs["ln2_g"]), "ln2_b": f(inputs["ln2_b"]),
    }
    in_maps = []
    for c in range(ncores):
        sl = slice(NS * c, NS * (c + 1))
        m = dict(shared)
        m["xp"] = x_prompt[c]
        m["xs"] = np.ascontiguousarray(x_sample[sl, 0, :])
        m["call"] = np.ascontiguousarray(np.concatenate([c_sample[sl], c_prompt[c:c + 1]], axis=0))
        m["sC"] = np.ascontiguousarray(sC[sl]); m["sn"] = np.ascontiguousarray(sn[sl].reshape(NS, 512)); m["sm"] = np.ascontiguousarray(sm[sl])
        m["spool"] = np.ascontiguousarray(spool[sl])
        in_maps.append(m)
    b = Builder(debug=debug)
    nc = b.build()
    res = run_bass_kernel_spmd(nc, in_maps, core_ids=list(range(ncores)))
    R = res.results
    _LAST["results"] = R
    _LAST["dbg"] = b.dbg_names
    B = 8
    y_prompt = np.zeros((B, SEQ, D), np.float32); y_sample = np.zeros((128, 1, D), np.float32)
    C_prompt = np.zeros((1, B, 4, 128, 128), np.float32); n_prompt = np.zeros((1, B, 4, 128), np.float32); m_prompt = np.zeros((1, B, 4), np.float32)
    pool_prompt = np.zeros((1, B, 15, 512), np.float32)
    C_sample = np.zeros((1, 128, 4, 128, 128), np.float32); n_sample = np.zeros((1, 128, 4, 128), np.float32); m_sample = np.zeros((1, 128, 4), np.float32)
    pool_sample = np.zeros((1, 128, 15, 512), np.float32)
    for c in range(ncores):
        sl = slice(NS * c, NS * (c + 1))
        r = R[c]
        y_prompt[c] = r["yp"]; y_sample[sl, 0, :] = r["ys"]
        C_prompt[0, c] = r["Cp"]; n_prompt[0, c] = r["np_"]; m_prompt[0, c] = r["mp"][0]; pool_prompt[0, c] = r["pp"]
        C_sample[0, sl] = r["Cs"]; n_sample[0, sl] = r["ns"].reshape(NS, 4, 128); m_sample[0, sl] = r["ms"]; pool_sample[0, sl] = r["pls"]
    return (y_prompt, y_sample, C_prompt, n_prompt, m_prompt, pool_prompt, C_sample, n_sample, m_sample, pool_sample)
```

```python
import os
import numpy as np
from contextlib import ExitStack
import concourse.bass as bass
import concourse.mybir as mybir
from concourse.bass_utils import run_bass_kernel_spmd

F32 = mybir.dt.float32
BF16 = mybir.dt.bfloat16
I32 = mybir.dt.int32
U32 = mybir.dt.uint32
ALU = mybir.AluOpType
AF = mybir.ActivationFunctionType
AX = mybir.AxisListType

D = 1024
NIN = 4616
ALPHA = 2.0 ** 0.25
EPS = 1e-5
KSCALE = 128.0 ** -0.5
NCORES = 8
SEQ = 2048
NS = 16
SCT = 256
NCH = SCT // 128
NSC = SEQ // SCT
POOLW = (2, 4, 8, 16)

ENGS = ["pe", "dve", "act", "pool", "sp"]


class Res:
    __slots__ = ("name", "w", "r", "al")

    def __init__(self, name):
        self.name = name
        self.w = None
        self.r = []
        self.al = []


class Prog:
    def __init__(self, nc, n_dma_sems):
        self.nc = nc
        self.q = {e: [] for e in ENGS}
        self.tick = {e: 0 for e in ENGS}
        self.esem = {e: nc.alloc_semaphore(name="es_" + e) for e in ENGS}
        self.waited = {e: {} for e in ENGS}
        self.dsem, self.dcnt, self.dval = {}, {}, {}
        for e, n in n_dma_sems.items():
            self.dsem[e] = [nc.alloc_semaphore(name="ds_%s%d" % (e, i)) for i in range(n)]
            self.dcnt[e] = 0
            self.dval[e] = [0] * n
        self.final_events = []
        self.pending = {e: [] for e in ENGS}

    def fence(self, eng, evs):
        self.pending[eng].extend(evs)

    def _collect(self, eng, reads, writes):
        evs = []
        for R in reads:
            if R.w is not None:
                evs.append((R.w, True))
        for R in writes:
            for Q in [R] + R.al:
                if Q.w is not None:
                    evs.append((Q.w, False))
                for ev in Q.r:
                    evs.append((ev, False))
        for ev in self.pending[eng]:
            evs.append((ev, True))
        self.pending[eng] = []
        best = {}
        for (ev, raw) in evs:
            sem, val, src = ev
            if src == eng and eng == "pe":
                continue
            k = id(sem)
            if k not in best or best[k][1] < val:
                best[k] = (sem, val)
        waits = []
        wd = self.waited[eng]
        for k, (sem, val) in best.items():
            if wd.get(k, 0) >= val:
                continue
            wd[k] = val
            waits.append((sem, val))
        return waits

    def _commit(self, ev, reads, writes):
        for R in reads:
            R.r.append(ev)
        for R in writes:
            R.w = ev
            R.r = []

    def op(self, eng, fn, reads=(), writes=()):
        waits = self._collect(eng, reads, writes)
        self.tick[eng] += 1
        ev = (self.esem[eng], self.tick[eng], eng)
        self.q[eng].append((waits, fn, (self.esem[eng], 1)))
        self._commit(ev, reads, writes)
        return ev

    def dma(self, eng, fn, reads=(), writes=(), final=False):
        waits = self._collect(eng, reads, writes)
        n = len(self.dsem[eng])
        i = self.dcnt[eng] % n
        self.dcnt[eng] += 1
        sem = self.dsem[eng][i]
        prev = self.dval[eng][i]
        if prev > 0 and self.waited[eng].get(id(sem), 0) < prev:
            self.waited[eng][id(sem)] = prev
            waits.append((sem, prev))
        val = prev + 16
        self.dval[eng][i] = val
        ev = (sem, val, "dma_" + eng)
        self.q[eng].append((waits, fn, (sem, 16)))
        self._commit(ev, reads, writes)
        if final:
            self.final_events.append(ev)
        return ev

    def emit(self, block):
        fin = {}
        for (sem, val, src) in self.final_events:
            if id(sem) not in fin or fin[id(sem)][1] < val:
                fin[id(sem)] = (sem, val)
        fin_waits = list(fin.values())

        def run(engname, e):
            for (waits, fn, inc) in self.q[engname]:
                for (sem, val) in waits:
                    e.wait_ge(sem, val)
                fn(e).then_inc(inc[0], inc[1])
            if engname == "sp":
                for (sem, val) in fin_waits:
                    e.wait_ge(sem, val)

        @block.tensor
        def _(e):
            run("pe", e)

        @block.vector
        def _(e):
            run("dve", e)

        @block.scalar
        def _(e):
            run("act", e)

        @block.gpsimd
        def _(e):
            run("pool", e)

        @block.sync
        def _(e):
            run("sp", e)


class Arena:
    def __init__(self, b, name, n, dt):
        self.t, _ = b.sb(name, [128, n], dt)
        self.items = []
        self.n = n

    def carve(self, name, off, shape):
        n = 1
        for d_ in shape[1:]:
            n *= d_
        assert off + n <= self.n, (name, off, n, self.n)
        R = Res(name)
        for (lo, hi, Q) in self.items:
            if lo < off + n and off < hi:
                R.al.append(Q)
                Q.al.append(R)
        self.items.append((off, off + n, R))
        ap = self.t[0:shape[0], off:off + n]
        if len(shape) == 3:
            ap = ap.rearrange("p (a b) -> p a b", a=shape[1])
        elif len(shape) == 4:
            ap = ap.rearrange("p (a b c) -> p a b c", a=shape[1], b=shape[2])
        return ap, R


class Builder:
    def __init__(self, debug=False):
        self.debug = debug
        self.nc = bass.Bass("TRN2", target_bir_lowering=False)
        self.P = Prog(self.nc, {"sp": 16, "pool": 12, "act": 4})
        self.es = ExitStack()
        self.dbg_names = []
        self.wcount = 0
        self.pmcount = 0

    def din(self, name, shape, dt=F32):
        return self.nc.dram_tensor(name, list(shape), dt, kind="ExternalInput").ap()

    def dout(self, name, shape, dt=F32):
        return self.nc.dram_tensor(name, list(shape), dt, kind="ExternalOutput").ap()

    def sb(self, name, shape, dt=F32):
        return self.es.enter_context(self.nc.sbuf_tensor(name, list(shape), dt)), Res(name)

    def ps(self, name, shape, dt=F32):
        return self.es.enter_context(self.nc.psum_tensor(name, list(shape), dt)), Res(name)

    def V(self, m, reads, writes, *a, **kw):
        return self.P.op("dve", lambda e: getattr(e, m)(*a, **kw), reads, writes)

    def A(self, m, reads, writes, *a, **kw):
        return self.P.op("act", lambda e: getattr(e, m)(*a, **kw), reads, writes)

    def G(self, m, reads, writes, *a, **kw):
        return self.P.op("pool", lambda e: getattr(e, m)(*a, **kw), reads, writes)

    def T(self, m, reads, writes, *a, **kw):
        return self.P.op("pe", lambda e: getattr(e, m)(*a, **kw), reads, writes)

    def DMA(self, q, reads, writes, final=False, **kw):
        return self.P.dma(q, lambda e: e.dma_start(**kw), reads, writes, final=final)

    def dump(self, name, ap, R, shape):
        if not self.debug:
            return
        d = self.dout("dbg_" + name, shape, ap.dtype)
        self.dbg_names.append("dbg_" + name)
        self.DMA("sp", [R], [], final=True, out=d, in_=ap)

    def build(self):
        nc, P = self.nc, self.P
        xp_d = self.din("xp", [SEQ, D]); xs_d = self.din("xs", [NS, D]); call_d = self.din("call", [NS + 1, D])
        sC_d = self.din("sC", [NS, 4, 128, 128]); sn_d = self.din("sn", [NS, 512]); sm_d = self.din("sm", [NS, 4])
        spool_d = self.din("spool", [NS, 15, 512])
        w_mod_d = self.din("w_mod", [D, 6 * D]); b_mod_d = self.din("b_mod", [1, 6 * D])
        w_in_d = self.din("w_in", [D, NIN]); b_in_d = self.din("b_in", [1, NIN]); b_fg_d = self.din("b_fgate", [1, 4])
        gn_d = self.din("gn_gain", [1, 512]); w_pool_d = self.din("w_pool", [4, 128, 128]); pscale_d = self.din("pool_scale", [1, 512])
        w_a_d = self.din("w_branch_a", [512, D]); w_b_d = self.din("w_branch_b", [512, D]); w_out_d = self.din("w_out", [D, D])
        ln1g_d = self.din("ln1_g", [1, D]); ln1b_d = self.din("ln1_b", [1, D])
        w_pq_d = self.din("w_peer_q", [D, 2048]); sk_d = self.din("peer_subkeys", [2, 128, 128])
        pu_d = self.din("peer_u", [16384, D]); pv_d = self.din("peer_v", [16384, D])
        ln2g_d = self.din("ln2_g", [1, D]); ln2b_d = self.din("ln2_b", [1, D])
        yp_d = self.dout("yp", [SEQ, D]); ys_d = self.dout("ys", [NS, D])
        Cp_d = self.dout("Cp", [4, 128, 128]); np_d = self.dout("np_", [4, 128]); mp_d = self.dout("mp", [1, 4]); pp_d = self.dout("pp", [15, 512])
        Cs_d = self.dout("Cs", [NS, 4, 128, 128]); ns_d = self.dout("ns", [NS, 512]); ms_d = self.dout("ms", [NS, 4]); pls_d = self.dout("pls", [NS, 15, 512])

        sb, ps, V, A, G, T, DMA = self.sb, self.ps, self.V, self.A, self.G, self.T, self.DMA
        def scr(name, shape):
            return nc.dram_tensor("scr_" + name, list(shape), BF16, kind="Internal").ap()
        wq_in = scr("w_in", [D, NIN]); wq_a = scr("w_a", [512, D]); wq_b = scr("w_b", [512, D]); wq_out = scr("w_out", [D, D]); wq_pq = scr("w_pq", [D, 2048])
        tabq = scr("tab", [16384, 2 * D])
        wmap = {id(w_in_d): wq_in, id(w_a_d): wq_a, id(w_b_d): wq_b, id(w_out_d): wq_out, id(w_pq_d): wq_pq}

        identf, Ridf = sb("identf", [128, 128]); identb, Ridb = sb("identb", [128, 128], BF16)
        ones, Rones = sb("ones", [128, 128]); maskT, Rmask = sb("maskT", [128, 128], BF16)
        selP, RselP = sb("selP", [NS + 1, 128]); eyeb, Reyeb = sb("eyeb", [128, 16, 16], BF16)
        iota16, Riota = sb("iota16", [128, 16]); inv16, Rinv = sb("inv16", [128, 4, 16])
        MOD, RMOD = sb("MOD", [128, 6 * D])
        lnbc, Rlnbc = sb("lnbc", [128, 4, D]); gnbc, Rgnbc = sb("gnbc", [128, 512])
        bias_tm, Rbtm = sb("bias_tm", [128, 2048], BF16); bias_if, Rbif = sb("bias_if", [128, 8]); bfg_bc, Rbfgbc = sb("bfg_bc", [128, 4])
        bcol, Rbcol = sb("bcol", [128, 28]); bIF, RbIF = sb("bIF", [4, 2]); bfg, Rbfg = sb("bfg", [4, 1])
        pscol, Rpscol = sb("pscol", [128, 4])
        wpool, Rwpool = sb("wpool", [128, 4, 128], BF16); skT, RskT = sb("skT", [128, 2, 128], BF16)
        stage = [sb("stage%d" % i, [128, 8, 512]) for i in range(2)]
        NWB = 3
        wbuf = [sb("wbuf%d" % i, [128, 8, 512], BF16) for i in range(NWB)]
        hT, RhT = sb("hT", [128, 8, SCT], BF16)
        xs_t = [sb("xs%d" % i, [128, D]) for i in range(NCH)]
        h2tok = [sb("h2tok%d" % i, [128, D], BF16) for i in range(NCH)]
        tmpA, RtmpA = sb("tmpA", [128, D]); tmpB, RtmpB = sb("tmpB", [128, D]); hb, Rhb = sb("hb", [128, D], BF16)
        st6, Rst6 = sb("st6", [128, 4, 6]); mv, Rmv = sb("mv", [128, 4, 2]); rstd, Rrstd = sb("rstd", [128, 4])
        uT, RuT = sb("uT", [128, 4, 15 + SCT])
        Caug, RCaug = sb("Caug", [128, 4, 129]); mstate, Rmstate = sb("mstate", [4, 1])
        hraw, Rhraw = sb("hraw", [128, 4, 128]); den, Rden = sb("den", [128, 4])
        siu, Rsiu = sb("siu", [128, 16, 16], U32); tpu, Rtpu = sb("tpu", [128, 8, 16], U32)
        ta_i, Rta_i = sb("ta_i", [128, 8, 16], I32)
        ids2 = [sb("ids%d" % i, [128, 128], I32) for i in range(2)]
        NGB = 8
        gbuf = []
        for i in range(NGB):
            stg, Rstg = stage[i // 4]
            R_ = Res("gb%d" % i)
            R_.al.append(Rstg); Rstg.al.append(R_)
            q4 = i % 4
            gbuf.append((stg[:, 2 * q4:2 * q4 + 2, :].rearrange("p a n -> p (a n)").bitcast(BF16), R_))
        dg = [sb("dg%d" % i, [128, 128], BF16) for i in range(2)]
        AB = Arena(self, "arenaB", 12288, BF16)
        qT, RqT = AB.carve("qT", 0, [128, 4, SCT]); kT, RkT = AB.carve("kT", 1024, [128, 4, SCT])
        ktok, Rktok = AB.carve("ktok", 2048, [128, NCH, 512]); vaug, Rvaug = AB.carve("vaug", 3072, [128, NCH, 4, 129])
        osig, Rosig = AB.carve("osig", 4104, [128, NCH, 512]); hAT, RhAT = AB.carve("hAT", 5128, [128, 4, SCT])
        pooledT, Rpooled = AB.carve("pooledT", 6152, [128, 4, SCT]); pBT, RpBT = AB.carve("pBT", 7176, [128, 4, SCT])
        mergedT, Rmerged = AB.carve("mergedT", 8200, [128, 8, SCT]); gsig, Rgsig = AB.carve("gsig", 10248, [128, 4, SCT])
        mtmp, Rmtmp = AB.carve("mtmp", 11272, [128, SCT]); Cs_bf, RCs_bf = AB.carve("Cs_bf", 11528, [128, 129])
        wTt, RwTt = AB.carve("wTt", 11660, [128, 128]); kd, Rkd = AB.carve("kd", 11788, [128, 128])
        qpT, RqpT = AB.carve("qpT", 0, [128, 16, SCT])
        AFa = Arena(self, "arenaF", 9984, F32)
        zI, RzI = AFa.carve("zI", 0, [4, SCT]); zF, RzF = AFa.carve("zF", 256, [4, SCT]); brow, Rbrow = AFa.carve("brow", 512, [4, SCT])
        arow, Rarow = AFa.carve("arow", 768, [4, SCT]); ea_r, Rea_r = AFa.carve("ea_r", 1024, [4, SCT]); ds_r, Rds_r = AFa.carve("ds_r", 1280, [4, SCT])
        eb_r, Reb_r = AFa.carve("eb_r", 1536, [4, SCT])
        poolA, RpoolA = AFa.carve("poolA", 1792, [128, 15 + SCT]); poolB, RpoolB = AFa.carve("poolB", 2112, [128, 15 + SCT])
        gbc, Rgbc = AFa.carve("gbc", 2432, [128, 2, NCH, 4]); gcol, Rgcol = AFa.carve("gcol", 2464, [128, NCH, 3, 4])
        gsm, Rgsm = AFa.carve("gsm", 2496, [4, 8, NCH]); bd, Rbd = AFa.carve("bd", 2528, [4, 2, NCH, 4])
        utok, Rutok = AFa.carve("utok", 2560, [128, 512])
        sq, Rsq = AFa.carve("sq", 0, [NS, 512]); sk, Rsk = AFa.carve("sk", 512, [NS, 512]); sva, Rsva = AFa.carve("sva", 1024, [NS, 4, 129])
        Qm, RQm = AFa.carve("Qm", 1540, [128, 16, 16]); Cst, RCst = AFa.carve("Cst", 1796, [128, 16, 129]); Vm, RVm = AFa.carve("Vm", 3860, [NS, 16, 129])
        sn_t, Rsn_t = AFa.carve("sn_t", 5924, [NS, 512]); nso, Rnso = AFa.carve("nso", 6436, [NS, 512])
        sprows, Rsprows = AFa.carve("sprows", 6948, [128, 2, 512]); upre, Rupre = AFa.carve("upre", 7972, [128, 4, 16, 16])
        sutok, Rsutok = AFa.carve("sutok", 8996, [NS, 512]); psumg, Rpsumg = AFa.carve("psumg", 9508, [128, 4, 16])
        kds, Rkds = AFa.carve("kds", 9572, [NS, 128]); dcB, RdcB = AFa.carve("dcB", 9700, [128, 16]); DCm, RDCm = AFa.carve("DCm", 9716, [NS, 16])
        ssm, Rssm = AFa.carve("ssm", 9732, [NS, 16, 4]); sIF, RsIF = AFa.carve("sIF", 9796, [NS, 8]); qTs, RqTs = AFa.carve("qTs", 9804, [128, 4, NS])
        nT, RnT = AFa.carve("nT", 9868, [128, 4, NS])
        s_sb, Rs_sb = AFa.carve("s_sb", 0, [128, 16, 128]); cand, Rcand = AFa.carve("cand", 2048, [128, 8, 256]); oh, Roh = AFa.carve("oh", 4096, [128, 8, 16, 16])
        yacc, Ryacc = AFa.carve("yacc", 6144, [128, D]); s2, Rs2 = AFa.carve("s2", 7168, [128, 128]); sv, Rsv = AFa.carve("sv", 7296, [128, 16, 16])
        sif, Rsif = AFa.carve("sif", 7552, [128, 16, 16]); cand2, Rcand2 = AFa.carve("cand2", 7808, [128, 256]); tops, Rtops = AFa.carve("tops", 8064, [128, 8, 16])
        tpf, Rtpf = AFa.carve("tpf", 8192, [128, 8, 16]); ta, Rta = AFa.carve("ta", 8320, [128, 8, 16]); tb, Rtb = AFa.carve("tb", 8448, [128, 8, 16])
        i1, Ri1 = AFa.carve("i1", 8576, [128, 8, 16]); i2, Ri2 = AFa.carve("i2", 8704, [128, 8, 16]); gates, Rgates = AFa.carve("gates", 8832, [128, 8, 16])
        gsum, Rgsum = AFa.carve("gsum", 8960, [128, 8]); dots, Rdots = AFa.carve("dots", 8968, [128, 128]); wts, Rwts = AFa.carve("wts", 9096, [128, 128])
        gates_b, Rgates_b = AFa.carve("gates_b", 7168 - 128, [128, 8, 16])
        gates2 = [(gates, Rgates), (gates_b, Rgates_b)]
        jk, Rjk = AFa.carve("jk", 6144, [128, 512])
        jk = jk.bitcast(BF16)
        pm = [ps("pm%d" % i, [128, 512]) for i in range(2)]
        ptb, Rptb = ps("ptb", [128, 8, 128], BF16); ptf, Rptf = ps("ptf", [128, 512])
        pS, RpS = ps("pS", [128, 512]); pP, RpP = ps("pP", [128, 512])
        pt = [ps("pt%d" % i, [128, 512]) for i in range(2)]

        def next_pm():
            self.pmcount += 1
            return pm[self.pmcount % 2]

        G("memset", [], [Ridf], identf[:], 0.0)
        G("affine_select", [Ridf], [Ridf], out=identf[:], in_=identf[:], pattern=[[-1, 128]], compare_op=ALU.not_equal, fill=1.0, base=0, channel_multiplier=1)
        V("tensor_copy", [Ridf], [Ridb], out=identb[:], in_=identf[:])
        G("memset", [], [Rones], ones[:], 1.0)
        G("affine_select", [Rones], [Rmask], out=maskT[:], in_=ones[:], pattern=[[1, 128]], compare_op=ALU.is_ge, fill=0.0, base=0, channel_multiplier=-1)
        G("affine_select", [Rones], [RselP], out=selP[:], in_=ones[0:NS + 1, :], pattern=[[0, 128]], compare_op=ALU.is_equal, fill=0.0, base=-NS, channel_multiplier=1)
        G("memset", [], [Reyeb], eyeb[:], 1.0)
        G("affine_select", [Reyeb], [Reyeb], out=eyeb[:], in_=eyeb[:], pattern=[[1, 16], [-1, 16]], compare_op=ALU.is_equal, fill=0.0, base=0, channel_multiplier=0)
        G("iota", [], [Riota], iota16[:], pattern=[[1, 16]], base=0, channel_multiplier=0, allow_small_or_imprecise_dtypes=True)
        for g, w in enumerate(POOLW):
            V("tensor_scalar", [Riota], [Rinv], out=inv16[:, g, :], in0=iota16[:], scalar1=1.0, scalar2=float(w), op0=ALU.add, op1=ALU.min)
        V("reciprocal", [Rinv], [Rinv], out=inv16[:], in_=inv16[:])
        G("memset", [], [RCaug], Caug[:], 0.0)
        G("memset", [], [Rmstate], mstate[:], 0.0)
        G("memset", [], [RuT], uT[:], 0.0)

        def conv_dma(dst_ap, src_ap):
            sem = nc.alloc_semaphore(name="cv%d" % len(self.cv_sems))
            self.cv_sems.append(sem)
            ev = (sem, 16, "dma_pool")
            P.q["pool"].append((P._collect("pool", [], []), lambda e: e.dma_start(out=dst_ap, in_=src_ap), (sem, 16)))
            return ev
        self.cv_sems = []
        w_events, t_events = [], []
        for (w_d, wq, K, N) in ((w_in_d, wq_in, D, NIN), (w_a_d, wq_a, 512, D), (w_b_d, wq_b, 512, D), (w_out_d, wq_out, D, D), (w_pq_d, wq_pq, D, 2048)):
            c0 = 0
            while c0 < N:
                cw = min(2048, N - c0)
                w_events.append(conv_dma(wq[0:K, c0:c0 + cw], w_d[0:K, c0:c0 + cw]))
                c0 += cw
        for r0 in range(0, 16384, 1024):
            t_events.append(conv_dma(tabq[r0:r0 + 1024, 0:D], pu_d[r0:r0 + 1024, :]))
            t_events.append(conv_dma(tabq[r0:r0 + 1024, D:2 * D], pv_d[r0:r0 + 1024, :]))
        self.w_events, self.t_events = w_events, t_events

        for i, d_ in enumerate([ln1g_d, ln1b_d, ln2g_d, ln2b_d]):
            DMA("sp", [], [Rlnbc], out=lnbc[:, i, :], in_=d_[0:1, :].partition_broadcast(128))
        DMA("sp", [], [Rgnbc], out=gnbc[:], in_=gn_d[0:1, :].partition_broadcast(128))
        stb, Rstb = stage[1]
        DMA("sp", [], [Rstb], out=stb[:, 0:4, :].rearrange("p a n -> p (a n)"), in_=b_in_d[0:1, 0:2048].partition_broadcast(128))
        V("tensor_copy", [Rstb], [Rbtm], out=bias_tm[:], in_=stb[:, 0:4, :].rearrange("p a n -> p (a n)"))
        DMA("sp", [], [Rbif], out=bias_if[:], in_=b_in_d[0:1, 2048:2056].partition_broadcast(128))
        DMA("sp", [], [Rbfgbc], out=bfg_bc[:], in_=b_fg_d[0:1, :].partition_broadcast(128))
        V("tensor_tensor", [Rbif, Rbfgbc], [Rbif], out=bias_if[:, 4:8], in0=bias_if[:, 4:8], in1=bfg_bc[:], op=ALU.add)
        colparts = [(0, 4, 0), (4, 4, 512), (8, 4, 2056), (12, 8, 2568), (20, 8, 3592)]
        for (c0, nb, off) in colparts:
            DMA("sp", [], [Rbcol], out=bcol[:, c0:c0 + nb], in_=b_in_d[0, off:off + nb * 128].rearrange("(c p) -> p c", p=128),
                allow_slow_non_contiguous=True)
        DMA("sp", [], [RbIF], out=bIF[:], in_=b_in_d[0, 2048:2056].rearrange("(c p) -> p c", p=4), allow_slow_non_contiguous=True)
        DMA("sp", [], [Rbfg], out=bfg[:], in_=b_fg_d[0, 0:4].rearrange("(p o) -> p o", o=1))
        V("tensor_tensor", [RbIF, Rbfg], [RbIF], out=bIF[:, 1:2], in0=bIF[:, 1:2], in1=bfg[:], op=ALU.add)
        DMA("sp", [], [Rpscol], out=pscol[:], in_=pscale_d[0, :].rearrange("(g p) -> p g", p=128), allow_slow_non_contiguous=True)
        st0, Rst0 = stage[0]
        DMA("sp", [], [Rst0], out=st0[:, 0, :].rearrange("p (g d) -> p g d", g=4), in_=w_pool_d.rearrange("g c d -> c g d"))
        V("tensor_copy", [Rst0], [Rwpool], out=wpool[:], in_=st0[:, 0, :].rearrange("p (g d) -> p g d", g=4))
        DMA("sp", [], [Rst0], out=st0[:, 1, 0:256].rearrange("p (a d) -> p a d", a=2), in_=sk_d.rearrange("a k d -> k a d"))
        for a in range(2):
            T("transpose", [Rst0, Ridf], [Rptf], out=ptf[:, a * 128:(a + 1) * 128], in_=st0[:, 1, a * 128:(a + 1) * 128], identity=identf[:])
        A("copy", [Rptf], [RskT], out=skT[:], in_=ptf[:, 0:256].rearrange("p (a k) -> p a k", a=2))

        NR = NS + 1
        DMA("sp", [], [RtmpA], out=tmpA[0:NR, :], in_=call_d)
        A("activation", [RtmpA], [RtmpA], out=tmpA[0:NR, :], in_=tmpA[0:NR, :], func=AF.Silu)
        for k in range(8):
            T("transpose", [RtmpA, Ridf], [Rptf], out=ptf[:, k * NR:(k + 1) * NR], in_=tmpA[0:NR, k * 128:(k + 1) * 128], identity=identf[0:NR, 0:NR])
        siluT = tmpB[:, 0:8 * NR].rearrange("p (k r) -> p k r", k=8)
        A("copy", [Rptf], [RtmpB], out=siluT, in_=ptf[:, 0:8 * NR].rearrange("p (k r) -> p k r", k=8))
        for nb in range(12):
            stg, Rstg = stage[nb % 2]
            DMA("sp", [], [Rstg], out=stg[:], in_=w_mod_d[:, nb * 512:(nb + 1) * 512].rearrange("(k p) n -> p k n", p=128))
            jb = tmpA[0:NR, (nb % 2) * 512:(nb % 2) * 512 + 512]
            DMA("sp", [], [RtmpA], out=jb, in_=b_mod_d[0:1, nb * 512:(nb + 1) * 512].partition_broadcast(NR))
            pmt, Rpm = next_pm()
            for k in range(8):
                T("matmul", [RtmpB, Rstg], [Rpm], pmt[0:NR, :], lhsT=siluT[:, k, :], rhs=stg[:, k, :], start=(k == 0), stop=(k == 7))
            add1 = 1.0 if nb in (2, 3, 8, 9) else 0.0
            V("scalar_tensor_tensor", [Rpm, RtmpA], [RMOD], out=MOD[0:NR, nb * 512:(nb + 1) * 512], in0=pmt[0:NR, :], scalar=add1, in1=jb,
              op0=ALU.add, op1=ALU.add)
        self.dump("mod", MOD[0:NR, :], RMOD, [NR, 6 * D])

        def layer_norm_stats(x_ap, Rx, nt, slot):
            for c in range(2):
                V("bn_stats", [Rx], [Rst6], out=st6[0:nt, c, :], in_=x_ap[:, c * 512:(c + 1) * 512])
            V("bn_aggr", [Rst6], [Rmv], out=mv[0:nt, slot, :], in_=st6[0:nt, 0:2, :])
            A("activation", [Rmv], [Rrstd], out=rstd[0:nt, slot:slot + 1], in_=mv[0:nt, slot, 1:2], func=AF.Sqrt, bias=EPS, scale=1.0)
            V("reciprocal", [Rrstd], [Rrstd], out=rstd[0:nt, slot:slot + 1], in_=rstd[0:nt, slot:slot + 1])

        def ln_affine(x_ap, Rx, nt, mul_ap, add_ap, Rpar, out_ap, Rout):
            layer_norm_stats(x_ap, Rx, nt, 0)
            V("tensor_scalar", [Rx, Rmv, Rrstd], [RtmpB], out=tmpB[0:nt, :], in0=x_ap, scalar1=mv[0:nt, 0, 0:1], scalar2=rstd[0:nt, 0:1],
              op0=ALU.subtract, op1=ALU.mult)
            G("tensor_tensor", [RtmpB, Rpar], [RtmpB], out=tmpB[0:nt, :], in0=tmpB[0:nt, :], in1=mul_ap, op=ALU.mult)
            V("tensor_tensor", [RtmpB, Rpar], [Rout], out=out_ap, in0=tmpB[0:nt, :], in1=add_ap, op=ALU.add)

        def sub_res(big, names):
            out = []
            for n_ in names:
                R_ = Res(n_)
                R_.al.append(big); big.al.append(R_)
                out.append(R_)
            return out
        Rst6_s = sub_res(Rst6, ["st6_s0", "st6_s1"]); Rmv_s = sub_res(Rmv, ["mv_s0", "mv_s1"]); Rrstd_s = sub_res(Rrstd, ["rstd_s0", "rstd_s1"])
        lnscr = [(tmpA, RtmpA), (tmpB, RtmpB)]

        def ln_chain(x_ap, Rx, nt, mul_ap, add_ap, Rpar, out_ap, Rout, s_):
            scr, Rscr = lnscr[s_]
            R6, Rm, Rr = Rst6_s[s_], Rmv_s[s_], Rrstd_s[s_]
            sc = scr[0:nt, :]
            return [
                lambda: V("bn_stats", [Rx], [R6], out=st6[0:nt, 2 * s_, :], in_=x_ap[:, 0:512]),
                lambda: V("bn_stats", [Rx], [R6], out=st6[0:nt, 2 * s_ + 1, :], in_=x_ap[:, 512:1024]),
                lambda: V("bn_aggr", [R6], [Rm], out=mv[0:nt, s_, :], in_=st6[0:nt, 2 * s_:2 * s_ + 2, :]),
                lambda: A("activation", [Rm], [Rr], out=rstd[0:nt, s_:s_ + 1], in_=mv[0:nt, s_, 1:2], func=AF.Sqrt, bias=EPS, scale=1.0),
                lambda: V("reciprocal", [Rr], [Rr], out=rstd[0:nt, s_:s_ + 1], in_=rstd[0:nt, s_:s_ + 1]),
                lambda: V("tensor_scalar", [Rx, Rm, Rr], [Rscr], out=sc, in0=x_ap, scalar1=mv[0:nt, s_, 0:1], scalar2=rstd[0:nt, s_:s_ + 1],
                          op0=ALU.subtract, op1=ALU.mult),
                lambda: G("tensor_tensor", [Rscr, Rpar], [Rscr], out=sc, in0=sc, in1=mul_ap, op=ALU.mult),
                lambda: V("tensor_tensor", [Rscr, Rpar], [Rout], out=out_ap, in0=sc, in1=add_ap, op=ALU.add),
            ]

        def interleave(chains):
            n_ = max(len(c_) for c_ in chains)
            for i_ in range(n_):
                for c_ in chains:
                    if i_ < len(c_):
                        c_[i_]()

        def to_feature_major(src_bf, Rsrc, nt, tok0):
            for k in range(8):
                T("transpose", [Rsrc, Ridb], [Rptb], out=ptb[:, k, 0:nt], in_=src_bf[0:nt, k * 128:(k + 1) * 128], identity=identb[0:nt, 0:nt])
            A("copy", [Rptb], [RhT], out=hT[:, :, tok0:tok0 + nt], in_=ptb[:, :, 0:nt])

        def load_w(w_d, K, c0, ncols):
            kc = K // 128
            i = self.wcount % NWB
            self.wcount += 1
            wb, Rwb = wbuf[i]
            if self.w_events is not None:
                P.fence("sp", self.w_events)
                self.w_events = None
            DMA("sp", [], [Rwb], out=wb[:, 0:kc, 0:ncols], in_=wmap[id(w_d)][0:K, c0:c0 + ncols].rearrange("(k p) n -> p k n", p=128))
            return wb, Rwb, kc

        def proj_fm(wb, Rwb, kc, col0, M, act, Ract, ntok, evac):
            pmt, Rpm = next_pm()
            for k in range(kc):
                T("matmul", [Rwb, Ract], [Rpm], pmt[0:M, 0:ntok], lhsT=wb[:, k, col0:col0 + M], rhs=act[:, k, 0:ntok], start=(k == 0), stop=(k == kc - 1))
            evac(pmt[0:M, 0:ntok], Rpm)

        def proj_tm(wb, Rwb, kc, ncols, act, Ract, tok0, nt, evac):
            pmt, Rpm = next_pm()
            for k in range(kc):
                T("matmul", [Rwb, Ract], [Rpm], pmt[0:nt, 0:ncols], lhsT=act[:, k, tok0:tok0 + nt], rhs=wb[:, k, 0:ncols], start=(k == 0), stop=(k == kc - 1))
            evac(pmt[0:nt, 0:ncols], Rpm)

        def ha_finish(nt, ti, tok0):
            for h in range(4):
                V("bn_stats", [Rhraw], [Rst6], out=st6[0:nt, h, :], in_=hraw[0:nt, h, :])
                V("bn_aggr", [Rst6], [Rmv], out=mv[0:nt, h, :], in_=st6[0:nt, h:h + 1, :])
            A("activation", [Rmv], [Rrstd], out=rstd[0:nt, :], in_=mv[0:nt, :, 1], func=AF.Sqrt, bias=EPS, scale=1.0)
            V("reciprocal", [Rrstd], [Rrstd], out=rstd[0:nt, :], in_=rstd[0:nt, :])
            for h in range(4):
                V("tensor_scalar", [Rhraw, Rmv, Rrstd], [RtmpB], out=tmpB[0:nt, h * 128:(h + 1) * 128], in0=hraw[0:nt, h, :],
                  scalar1=mv[0:nt, h, 0:1], scalar2=rstd[0:nt, h:h + 1], op0=ALU.subtract, op1=ALU.mult)
            G("tensor_tensor", [RtmpB, Rgnbc], [RtmpB], out=tmpB[0:nt, 0:512], in0=tmpB[0:nt, 0:512], in1=gnbc[0:nt, :], op=ALU.mult)
            V("tensor_tensor", [RtmpB, Rosig], [Rhb], out=hb[0:nt, 0:512], in0=tmpB[0:nt, 0:512], in1=osig[0:nt, ti, :], op=ALU.mult)
            for hc in range(4):
                T("transpose", [Rhb, Ridb], [Rptb], out=ptb[:, hc, 0:nt], in_=hb[0:nt, hc * 128:(hc + 1) * 128], identity=identb[0:nt, 0:nt])
            A("copy", [Rptb], [RhAT], out=hAT[:, :, tok0:tok0 + nt], in_=ptb[:, 0:4, 0:nt])

        def process_sc(kind, sc_idx):
            sample = (kind == "sample")
            ntok = NS if sample else SCT
            tiles = [(0, NS)] if sample else [(i * 128, 128) for i in range(NCH)]
            x_src = xs_d if sample else xp_d[sc_idx * SCT:(sc_idx + 1) * SCT, :]
            y_dst = ys_d if sample else yp_d[sc_idx * SCT:(sc_idx + 1) * SCT, :]
            tag = "s" if sample else "p%d" % sc_idx

            def modv(i, nt):
                return MOD[0:nt, i * D:(i + 1) * D]

            chains = []
            for ti, (t0, nt) in enumerate(tiles):
                xt, Rxt = xs_t[ti]
                h2t, Rh2t = h2tok[ti]
                ch = [lambda xt=xt, Rxt=Rxt, t0=t0, nt=nt: DMA("sp", [], [Rxt], out=xt[0:nt, :], in_=x_src[t0:t0 + nt, :])]
                ch += ln_chain(xt[0:nt, :], Rxt, nt, modv(1, nt), modv(0, nt), RMOD, h2t[0:nt, :], Rh2t, ti % 2)
                ch += [lambda h2t=h2t, Rh2t=Rh2t, nt=nt, t0=t0: to_feature_major(h2t, Rh2t, nt, t0)]
                chains.append(ch)
            interleave(chains)
            if self.debug and (sample or sc_idx == 0):
                self.dump("h1T_" + tag, hT[:, :, 0:ntok], RhT, [128, 8, ntok])

            wb, Rwb, kc = load_w(w_in_d, D, 0, 512)
            dstq = qTs if sample else qT
            Rdq = RqTs if sample else RqT
            for h in range(4):
                proj_fm(wb, Rwb, kc, h * 128, 128, hT, RhT, ntok,
                        lambda p_, Rp, h=h: A("activation", [Rp, Rbcol], [Rdq], out=dstq[:, h, 0:ntok], in_=p_, func=AF.Identity, bias=bcol[:, h:h + 1], scale=1.0))
            if sample:
                proj_tm(wb, Rwb, kc, 512, hT, RhT, 0, NS,
                        lambda p_, Rp: V("tensor_tensor", [Rp, Rbtm], [Rsq], out=sq[:], in0=p_, in1=bias_tm[0:NS, 0:512], op=ALU.add))
            wb, Rwb, kc = load_w(w_in_d, D, 512, 512)
            if not sample:
                for h in range(4):
                    proj_fm(wb, Rwb, kc, h * 128, 128, hT, RhT, ntok,
                            lambda p_, Rp, h=h: A("activation", [Rp, Rbcol], [RkT], out=kT[:, h, 0:ntok], in_=p_, func=AF.Identity, bias=bcol[:, 4 + h:5 + h], scale=1.0))
                for ti, (t0, nt) in enumerate(tiles):
                    proj_tm(wb, Rwb, kc, 512, hT, RhT, t0, nt,
                            lambda p_, Rp, ti=ti, nt=nt: V("tensor_tensor", [Rp, Rbtm], [Rktok], out=ktok[0:nt, ti, :], in0=p_, in1=bias_tm[0:nt, 512:1024], op=ALU.add))
            else:
                proj_tm(wb, Rwb, kc, 512, hT, RhT, 0, NS,
                        lambda p_, Rp: V("tensor_tensor", [Rp, Rbtm], [Rsk], out=sk[:], in0=p_, in1=bias_tm[0:NS, 512:1024], op=ALU.add))
            wb, Rwb, kc = load_w(w_in_d, D, 1024, 512)
            if sample:
                G("memset", [], [Rsva], sva[:], 1.0)
                G("memset", [], [Rsprows], sprows[:], 0.0)
            else:
                G("memset", [], [Rvaug], vaug[:], 1.0)
            for ti, (t0, nt) in enumerate(tiles):
                if sample:
                    ev = lambda p_, Rp: V("tensor_tensor", [Rp, Rbtm], [Rsva], out=sva[:, :, 0:128], in0=p_.rearrange("p (h d) -> p h d", h=4),
                                          in1=bias_tm[0:NS, 1024:1536].rearrange("p (h d) -> p h d", h=4), op=ALU.add)
                else:
                    ev = lambda p_, Rp, ti=ti, nt=nt: V("tensor_tensor", [Rp, Rbtm], [Rvaug], out=vaug[0:nt, ti, :, 0:128], in0=p_.rearrange("p (h d) -> p h d", h=4),
                                                        in1=bias_tm[0:nt, 1024:1536].rearrange("p (h d) -> p h d", h=4), op=ALU.add)
                proj_tm(wb, Rwb, kc, 512, hT, RhT, t0, nt, ev)
            wb, Rwb, kc = load_w(w_in_d, D, 1536, 512)
            for ti, (t0, nt) in enumerate(tiles):
                def ev(p_, Rp, ti=ti, nt=nt):
                    V("tensor_tensor", [Rp, Rbtm], [RtmpA], out=tmpA[0:nt, 0:512], in0=p_, in1=bias_tm[0:nt, 1536:2048], op=ALU.add)
                    A("activation", [RtmpA], [Rosig], out=osig[0:nt, ti, :], in_=tmpA[0:nt, 0:512], func=AF.Sigmoid)
                proj_tm(wb, Rwb, kc, 512, hT, RhT, t0, nt, ev)
            wb, Rwb, kc = load_w(w_in_d, D, 2048, 8)
            if sample:
                proj_tm(wb, Rwb, kc, 8, hT, RhT, 0, NS,
                        lambda p_, Rp: V("tensor_tensor", [Rp, Rbif], [RsIF], out=sIF[:], in0=p_, in1=bias_if[0:NS, :], op=ALU.add))
            else:
                proj_fm(wb, Rwb, kc, 0, 4, hT, RhT, ntok,
                        lambda p_, Rp: A("activation", [Rp, RbIF], [RzI], out=zI[:, 0:ntok], in_=p_, func=AF.Identity, bias=bIF[:, 0:1], scale=1.0))
                proj_fm(wb, Rwb, kc, 4, 4, hT, RhT, ntok,
                        lambda p_, Rp: A("activation", [Rp, RbIF], [RzF], out=zF[:, 0:ntok], in_=p_, func=AF.Identity, bias=bIF[:, 1:2], scale=1.0))

            if not sample:
                nchunk = NCH
                A("activation", [RzF], [RzF], out=zF[:], in_=zF[:], func=AF.Exp, scale=-1.0)
                A("activation", [RzF], [RzF], out=zF[:], in_=zF[:], func=AF.Ln, bias=1.0, scale=1.0)
                V("tensor_scalar", [RzF], [RzF], out=zF[:], in0=zF[:], scalar1=-1.0, scalar2=None, op0=ALU.mult)
                for c in range(nchunk):
                    V("tensor_tensor_scan", [RzF, Rones], [Rbrow], out=brow[:, c * 128:(c + 1) * 128], data0=ones[0:4, :], data1=zF[:, c * 128:(c + 1) * 128],
                      initial=0.0, op0=ALU.mult, op1=ALU.add)
                V("tensor_tensor", [RzI, Rbrow], [Rarow], out=arow[:], in0=zI[:], in1=brow[:], op=ALU.subtract)
                V("tensor_reduce", [Rarow], [Rgsm], out=gsm[:, 0, :], in_=arow[:].rearrange("p (c t) -> p c t", c=NCH), axis=AX.X, op=ALU.max)
                bL = brow[:].rearrange("p (c t) -> p c t", c=NCH)[:, :, 127]
                V("tensor_tensor_scan", [Rgsm, Rbrow, Rmstate], [Rgsm], out=gsm[:, 1, :], data0=gsm[:, 0, :], data1=bL, initial=mstate[:, 0:1],
                  op0=ALU.max, op1=ALU.add)
                V("tensor_copy", [Rmstate], [Rgsm], out=gsm[:, 2, 0:1], in_=mstate[:, 0:1])
                V("tensor_copy", [Rgsm], [Rgsm], out=gsm[:, 2, 1:NCH], in_=gsm[:, 1, 0:NCH - 1])
                V("tensor_copy", [Rgsm], [Rmstate], out=mstate[:, 0:1], in_=gsm[:, 1, NCH - 1:NCH])
                V("tensor_tensor", [Rbrow, Rgsm], [Rgsm], out=gsm[:, 5, :], in0=bL, in1=gsm[:, 1, :], op=ALU.subtract)
                V("tensor_tensor", [Rgsm], [Rgsm], out=gsm[:, 6, :], in0=gsm[:, 5, :], in1=gsm[:, 2, :], op=ALU.add)
                A("activation", [Rgsm], [Rgsm], out=gsm[:, 3, :], in_=gsm[:, 6, :], func=AF.Exp)
                A("activation", [Rgsm], [Rgsm], out=gsm[:, 4, :], in_=gsm[:, 2, :], func=AF.Exp)
                A("activation", [Rarow], [Rea_r], out=ea_r[:], in_=arow[:], func=AF.Exp)
                V("tensor_scalar", [Rea_r], [Rea_r], out=ea_r[:], in0=ea_r[:], scalar1=KSCALE, scalar2=None, op0=ALU.mult)
                for c in range(nchunk):
                    A("activation", [Rarow, Rgsm], [Rds_r], out=ds_r[:, c * 128:(c + 1) * 128], in_=arow[:, c * 128:(c + 1) * 128], func=AF.Exp,
                      bias=gsm[:, 5, c:c + 1], scale=1.0)
                V("tensor_scalar", [Rds_r], [Rds_r], out=ds_r[:], in0=ds_r[:], scalar1=KSCALE, scalar2=None, op0=ALU.mult)
                A("activation", [Rbrow], [Reb_r], out=eb_r[:], in_=brow[:], func=AF.Exp, scale=-1.0)
                for c in range(nchunk):
                    for qi, (src, Rs) in enumerate([(ea_r, Rea_r), (ds_r, Rds_r), (eb_r, Reb_r)]):
                        T("transpose", [Rs, Ridf], [Rptf], out=ptf[:, (c * 3 + qi) * 4:(c * 3 + qi) * 4 + 4], in_=src[:, c * 128:(c + 1) * 128], identity=identf[0:4, 0:4])
                A("copy", [Rptf], [Rgcol], out=gcol[:].rearrange("p c q h -> p (c q h)"), in_=ptf[:, 0:12 * NCH])
                for qi, row in enumerate([3, 4]):
                    V("tensor_tensor", [Rgsm, Ridf], [Rbd], out=bd[:, qi, :, :], in0=gsm[:, row, :].unsqueeze(2).broadcast_to([4, NCH, 4]),
                      in1=identf[0:4, 0:4].unsqueeze(1).broadcast_to([4, NCH, 4]), op=ALU.mult)
                T("matmul", [Rones, Rbd], [Rptf], ptf[:, 64:64 + 8 * NCH], lhsT=ones[0:4, :], rhs=bd[:].rearrange("p q c h -> p (q c h)"), start=True, stop=True)
                A("copy", [Rptf], [Rgbc], out=gbc[:].rearrange("p q c h -> p (q c h)"), in_=ptf[:, 64:64 + 8 * NCH])
                for c in range(nchunk):
                    t0 = c * 128
                    for h in range(4):
                        V("tensor_scalar", [RCaug, Rgbc], [RCs_bf], out=Cs_bf[:], in0=Caug[:, h, :], scalar1=gbc[:, 1, c, h:h + 1], scalar2=None, op0=ALU.mult)
                        T("matmul", [RkT, RqT], [RpS], pS[:, 0:128], lhsT=kT[:, h, t0:t0 + 128], rhs=qT[:, h, t0:t0 + 128], start=True, stop=True)
                        V("scalar_tensor_tensor", [RpS, Rgcol, Rmask], [RwTt], out=wTt[:], in0=pS[:, 0:128], scalar=gcol[:, c, 0, h:h + 1], in1=maskT[:],
                          op0=ALU.mult, op1=ALU.mult)
                        T("matmul", [RwTt, Rvaug], [RpP], pP[:, 0:129], lhsT=wTt[:], rhs=vaug[:, c, h, :], start=True, stop=False)
                        T("matmul", [RqT, RCs_bf], [RpP], pP[:, 0:129], lhsT=qT[:, h, t0:t0 + 128], rhs=Cs_bf[:], start=False, stop=True)
                        A("activation", [RpP], [Rden], out=den[:, h:h + 1], in_=pP[:, 128:129], func=AF.Abs)
                        V("tensor_tensor", [Rden, Rgcol], [Rden], out=den[:, h:h + 1], in0=den[:, h:h + 1], in1=gcol[:, c, 2, h:h + 1], op=ALU.max)
                        V("reciprocal", [Rden], [Rden], out=den[:, h:h + 1], in_=den[:, h:h + 1])
                        V("tensor_scalar", [RpP, Rden], [Rhraw], out=hraw[:, h, :], in0=pP[:, 0:128], scalar1=den[:, h:h + 1], scalar2=None, op0=ALU.mult)
                        G("tensor_scalar", [Rktok, Rgcol], [Rkd], out=kd[:], in0=ktok[:, c, h * 128:(h + 1) * 128], scalar1=gcol[:, c, 1, h:h + 1], scalar2=None, op0=ALU.mult)
                        T("matmul", [Rkd, Rvaug], [RpP], pP[:, 256:385], lhsT=kd[:], rhs=vaug[:, c, h, :], start=True, stop=True)
                        V("scalar_tensor_tensor", [RCaug, Rgbc, RpP], [RCaug], out=Caug[:, h, :], in0=Caug[:, h, :], scalar=gbc[:, 0, c, h:h + 1], in1=pP[:, 256:385],
                          op0=ALU.mult, op1=ALU.add)
                    if self.debug and sc_idx == 0 and c == 0:
                        self.dump("hraw_p0", hraw[:], Rhraw, [128, 4, 128])
                    ha_finish(128, c, t0)
            else:
                S = lambda i: ssm[:, i, :]
                DMA("sp", [], [Rssm], out=ssm[:, 0, :], in_=sm_d)
                A("activation", [RsIF], [Rssm], out=S(1), in_=sIF[:, 4:8], func=AF.Exp, scale=-1.0)
                A("activation", [Rssm], [Rssm], out=S(1), in_=S(1), func=AF.Ln, bias=1.0, scale=1.0)
                V("tensor_scalar", [Rssm], [Rssm], out=S(1), in0=S(1), scalar1=-1.0, scalar2=None, op0=ALU.mult)
                V("tensor_tensor", [RsIF, Rssm], [Rssm], out=S(2), in0=sIF[:, 0:4], in1=S(1), op=ALU.subtract)
                V("tensor_tensor", [Rssm], [Rssm], out=S(3), in0=S(0), in1=S(2), op=ALU.max)
                V("tensor_tensor", [Rssm], [Rssm], out=S(4), in0=S(1), in1=S(3), op=ALU.add)
                DMA("sp", [Rssm], [], final=True, out=ms_d, in_=S(4))
                V("tensor_tensor", [Rsq, Rsk], [RtmpA], out=tmpA[0:NS, 0:512], in0=sq[:], in1=sk[:], op=ALU.mult)
                V("tensor_reduce", [RtmpA], [Rssm], out=S(5), in_=tmpA[0:NS, 0:512].rearrange("p (h d) -> p h d", h=4), axis=AX.X, op=ALU.add)
                A("activation", [Rssm], [Rssm], out=S(6), in_=S(2), func=AF.Exp)
                V("scalar_tensor_tensor", [Rssm], [Rssm], out=S(7), in0=S(6), scalar=KSCALE, in1=S(5), op0=ALU.mult, op1=ALU.mult)
                A("activation", [Rssm], [Rssm], out=S(8), in_=S(0), func=AF.Exp)
                A("activation", [Rssm], [Rssm], out=S(9), in_=S(1), func=AF.Exp, scale=-1.0)
                V("tensor_tensor", [RsIF, Rssm], [Rssm], out=S(10), in0=sIF[:, 0:4], in1=S(4), op=ALU.subtract)
                A("activation", [Rssm], [Rssm], out=S(10), in_=S(10), func=AF.Exp)
                V("tensor_scalar", [Rssm], [Rssm], out=S(10), in0=S(10), scalar1=KSCALE, scalar2=None, op0=ALU.mult)
                V("tensor_tensor", [Rssm], [Rssm], out=S(11), in0=S(1), in1=S(0), op=ALU.add)
                V("tensor_tensor", [Rssm], [Rssm], out=S(11), in0=S(11), in1=S(4), op=ALU.subtract)
                A("activation", [Rssm], [Rssm], out=S(11), in_=S(11), func=AF.Exp)
                DMA("sp", [], [Rsn_t], out=sn_t[:], in_=sn_d)
                for h in range(4):
                    T("transpose", [Rsn_t, Ridf], [Rptf], out=ptf[:, h * NS:(h + 1) * NS], in_=sn_t[:, h * 128:(h + 1) * 128], identity=identf[0:NS, 0:NS])
                A("copy", [Rptf], [RnT], out=nT[:].rearrange("p h j -> p (h j)"), in_=ptf[:, 0:4 * NS])
                for h in range(4):
                    DMA("sp", [], [RCst], out=Cst[:, :, 0:128], in_=sC_d[:, h].rearrange("j k v -> k j v"))
                    V("tensor_copy", [RnT], [RCst], out=Cst[:, :, 128], in_=nT[:, h, :])
                    V("tensor_tensor", [RqTs, Reyeb], [RQm], out=Qm[:], in0=qTs[:, h, :].unsqueeze(2).broadcast_to([128, 16, 16]), in1=eyeb[:], op=ALU.mult)
                    for j in range(NS):
                        T("matmul", [RQm, RCst], [RpP], pP[0:NS, 0:129], lhsT=Qm[:, j, :], rhs=Cst[:, j, :], start=(j == 0), stop=(j == NS - 1))
                    V("tensor_scalar", [RpP, Rssm], [RtmpA], out=tmpA[0:NS, 0:129], in0=pP[0:NS, 0:129], scalar1=ssm[:, 8, h:h + 1], scalar2=None, op0=ALU.mult)
                    V("scalar_tensor_tensor", [Rsva, Rssm, RtmpA], [RtmpA], out=tmpA[0:NS, 0:129], in0=sva[:, h, :], scalar=ssm[:, 7, h:h + 1], in1=tmpA[0:NS, 0:129],
                      op0=ALU.mult, op1=ALU.add)
                    V("scalar_tensor_tensor", [RtmpA], [Rden], out=den[0:NS, h:h + 1], in0=tmpA[0:NS, 128:129], scalar=-1.0, in1=tmpA[0:NS, 128:129], op0=ALU.mult, op1=ALU.max)
                    V("tensor_tensor", [Rden, Rssm], [Rden], out=den[0:NS, h:h + 1], in0=den[0:NS, h:h + 1], in1=ssm[:, 9, h:h + 1], op=ALU.max)
                    V("reciprocal", [Rden], [Rden], out=den[0:NS, h:h + 1], in_=den[0:NS, h:h + 1])
                    V("tensor_scalar", [RtmpA, Rden], [Rhraw], out=hraw[0:NS, h, :], in0=tmpA[0:NS, 0:128], scalar1=den[0:NS, h:h + 1], scalar2=None, op0=ALU.mult)
                    V("tensor_tensor", [Rsva, Ridf], [RVm], out=Vm[:], in0=sva[:, h, :].unsqueeze(1).broadcast_to([NS, 16, 129]),
                      in1=identf[0:NS, 0:NS].unsqueeze(2).broadcast_to([NS, 16, 129]), op=ALU.mult)
                    V("tensor_scalar", [Rsk, Rssm], [Rkds], out=kds[:], in0=sk[:, h * 128:(h + 1) * 128], scalar1=ssm[:, 10, h:h + 1], scalar2=None, op0=ALU.mult)
                    V("tensor_scalar", [Ridf, Rssm], [RDCm], out=DCm[:], in0=identf[0:NS, 0:NS], scalar1=ssm[:, 11, h:h + 1], scalar2=None, op0=ALU.mult)
                    T("matmul", [Rones, RDCm], [Rptf], ptf[:, 128:144], lhsT=ones[0:NS, :], rhs=DCm[:], start=True, stop=True)
                    A("copy", [Rptf], [RdcB], out=dcB[:], in_=ptf[:, 128:144])
                    for j in range(NS):
                        pst, Rpst = pt[j % 2]
                        T("matmul", [Rkds, RVm], [Rpst], pst[:, 0:129], lhsT=kds[:], rhs=Vm[:, j, :], start=True, stop=True)
                        V("scalar_tensor_tensor", [RCst, RdcB, Rpst], [RCst], out=Cst[:, j, :], in0=Cst[:, j, :], scalar=dcB[:, j:j + 1], in1=pst[:, 0:129],
                          op0=ALU.mult, op1=ALU.add)
                    DMA("sp", [RCst], [], final=True, out=Cs_d[:, h].rearrange("j k v -> k j v"), in_=Cst[:, :, 0:128])
                    V("tensor_copy", [RCst], [RtmpB], out=tmpB[:, 0:NS], in_=Cst[:, :, 128])
                    T("transpose", [RtmpB, Ridf], [Rptf], out=ptf[0:NS, 256:384], in_=tmpB[:, 0:NS], identity=identf[:])
                    A("copy", [Rptf], [Rnso], out=nso[:, h * 128:(h + 1) * 128], in_=ptf[0:NS, 256:384])
                DMA("sp", [Rnso], [], final=True, out=ns_d, in_=nso[:])
                self.dump("hraw_s", hraw[0:NS], Rhraw, [NS, 4, 128])
                ha_finish(NS, 0, 0)
            if self.debug and (sample or sc_idx == 0):
                self.dump("hAT_" + tag, hAT[:, :, 0:ntok], RhAT, [128, 4, ntok])

            wb, Rwb, kc = load_w(w_in_d, D, 2056, 512)
            for g in range(4):
                proj_fm(wb, Rwb, kc, g * 128, 128, hT, RhT, ntok,
                        lambda p_, Rp, g=g: A("activation", [Rp, Rbcol], [RuT], out=uT[:, g, 15:15 + ntok], in_=p_, func=AF.Identity, bias=bcol[:, 8 + g:9 + g], scale=1.0))
            if not sample:
                L = 15 + ntok
                for g, w in enumerate(POOLW):
                    src, Rsrc = uT[:, g, :], RuT
                    bufs = [(poolA, RpoolA), (poolB, RpoolB)]
                    d_, bi = 1, 0
                    while d_ < w:
                        dst, Rdst = bufs[bi]
                        eng = V if (g + bi) % 2 == 0 else G
                        lo = 2 * d_ - 1
                        eng("tensor_tensor", [Rsrc], [Rdst], out=dst[:, lo:L], in0=src[:, lo:L], in1=src[:, lo - d_:L - d_], op=ALU.add)
                        src, Rsrc = dst[:, :], Rdst
                        d_ *= 2
                        bi ^= 1
                    V("scalar_tensor_tensor", [Rsrc, RuT], [Rpooled], out=pooledT[:, g, 0:ntok], in0=src[:, 15:L], scalar=1.0 / w, in1=uT[:, g, 15:L],
                      op0=ALU.mult, op1=ALU.subtract)
                    if sc_idx == 0:
                        V("tensor_tensor", [Rsrc, Rinv], [RtmpA], out=tmpA[:, 0:16], in0=src[:, 15:31], in1=inv16[:, g, :], op=ALU.mult)
                        V("tensor_tensor", [RtmpA, RuT], [Rpooled], out=pooledT[:, g, 0:16], in0=tmpA[:, 0:16], in1=uT[:, g, 15:31], op=ALU.subtract)
                if sc_idx == NSC - 1:
                    for g in range(4):
                        T("transpose", [RuT, Ridf], [Rptf], out=ptf[:, g * 128:(g + 1) * 128], in_=uT[:, g, 15 + ntok - 128:15 + ntok], identity=identf[:])
                    A("copy", [Rptf], [Rutok], out=utok[:], in_=ptf[:, 0:512])
                    DMA("sp", [Rutok], [], final=True, out=pp_d, in_=utok[113:128, :])
                A("copy", [RuT], [RuT], out=uT[:, :, 0:15], in_=uT[:, :, ntok:ntok + 15])
            else:
                for j in range(NS):
                    r0 = (j % 8) * 16
                    DMA("sp", [], [Rsprows], out=sprows[r0:r0 + 15, j // 8, :], in_=spool_d[j])
                for t_ in range(2):
                    for g in range(4):
                        T("transpose", [Rsprows, Ridf], [Rptf], out=ptf[:, g * 128:(g + 1) * 128], in_=sprows[:, t_, g * 128:(g + 1) * 128], identity=identf[:])
                    A("copy", [Rptf], [Rupre], out=upre[:, :, t_ * 8:(t_ + 1) * 8, :].rearrange("p g j q -> p g (j q)"), in_=ptf[:, 0:512].rearrange("p (g r) -> p g r", g=4))
                for g, w in enumerate(POOLW):
                    V("tensor_reduce", [Rupre], [Rpsumg], out=psumg[:, g, :], in_=upre[:, g, :, 16 - w:15], axis=AX.X, op=ALU.add)
                    V("tensor_tensor", [Rpsumg, RuT], [Rpsumg], out=psumg[:, g, :], in0=psumg[:, g, :], in1=uT[:, g, 15:15 + NS], op=ALU.add)
                    V("scalar_tensor_tensor", [Rpsumg, RuT], [Rpooled], out=pooledT[:, g, 0:NS], in0=psumg[:, g, :], scalar=1.0 / w, in1=uT[:, g, 15:15 + NS],
                      op0=ALU.mult, op1=ALU.subtract)
                for j in range(NS):
                    r0 = (j % 8) * 16
                    DMA("sp", [Rsprows], [], final=True, out=pls_d[j, 0:14, :], in_=sprows[r0 + 1:r0 + 15, j // 8, :])
                for g in range(4):
                    T("transpose", [RuT, Ridf], [Rptf], out=ptf[0:NS, g * 128:(g + 1) * 128], in_=uT[:, g, 15:15 + NS], identity=identf[:])
                A("copy", [Rptf], [Rsutok], out=sutok[0:NS, :], in_=ptf[0:NS, 0:512])
                DMA("sp", [Rsutok], [], final=True, out=pls_d[:, 14, :], in_=sutok[0:NS, :])
                G("memset", [RuT], [RuT], uT[:, :, 0:15], 0.0)
            if self.debug and (sample or sc_idx == 0):
                self.dump("pooledT_" + tag, pooledT[:, :, 0:ntok], Rpooled, [128, 4, ntok])
            for g in range(4):
                pmt, Rpm = next_pm()
                T("matmul", [Rwpool, Rpooled], [Rpm], pmt[:, 0:ntok], lhsT=wpool[:, g, :], rhs=pooledT[:, g, 0:ntok], start=True, stop=True)
                V("tensor_scalar", [Rpm, Rpscol], [RpBT], out=pBT[:, g, 0:ntok], in0=pmt[:, 0:ntok], scalar1=pscol[:, g:g + 1], scalar2=None, op0=ALU.mult)

            for half in range(2):
                wb, Rwb, kc = load_w(w_in_d, D, 2568 + half * 512, 512)
                for j in range(4):
                    proj_fm(wb, Rwb, kc, j * 128, 128, hT, RhT, ntok,
                            lambda p_, Rp, j=j: A("activation", [Rp, Rbcol], [Rgsig], out=gsig[:, j, 0:ntok], in_=p_, func=AF.Sigmoid,
                                                  bias=bcol[:, 12 + half * 4 + j:13 + half * 4 + j], scale=1.0))
                wb, Rwb, kc = load_w(w_a_d, 512, half * 512, 512)
                for j in range(4):
                    proj_fm(wb, Rwb, kc, j * 128, 128, hAT, RhAT, ntok,
                            lambda p_, Rp, j=j: V("tensor_tensor", [Rp, Rgsig], [Rmerged], out=mergedT[:, half * 4 + j, 0:ntok], in0=p_, in1=gsig[:, j, 0:ntok], op=ALU.mult))
                wb, Rwb, kc = load_w(w_in_d, D, 3592 + half * 512, 512)
                for j in range(4):
                    proj_fm(wb, Rwb, kc, j * 128, 128, hT, RhT, ntok,
                            lambda p_, Rp, j=j: A("activation", [Rp, Rbcol], [Rgsig], out=gsig[:, j, 0:ntok], in_=p_, func=AF.Sigmoid,
                                                  bias=bcol[:, 20 + half * 4 + j:21 + half * 4 + j], scale=1.0))
                wb, Rwb, kc = load_w(w_b_d, 512, half * 512, 512)
                for j in range(4):
                    def ev(p_, Rp, j=j):
                        V("tensor_tensor", [Rp, Rgsig], [Rmtmp], out=mtmp[:, 0:ntok], in0=p_, in1=gsig[:, j, 0:ntok], op=ALU.mult)
                        G("tensor_tensor", [Rmtmp, Rmerged], [Rmerged], out=mergedT[:, half * 4 + j, 0:ntok], in0=mergedT[:, half * 4 + j, 0:ntok], in1=mtmp[:, 0:ntok], op=ALU.add)
                    proj_fm(wb, Rwb, kc, j * 128, 128, pBT, RpBT, ntok, ev)
            if self.debug and (sample or sc_idx == 0):
                self.dump("mergedT_" + tag, mergedT[:, :, 0:ntok], Rmerged, [128, 8, ntok])

            wo = [load_w(w_out_d, D, half * 512, 512) for half in range(2)]
            chains = []
            for ti, (t0, nt) in enumerate(tiles):
                xt, Rxt = xs_t[ti]
                h2t, Rh2t = h2tok[ti]
                s_ = ti % 2
                scr, Rscr = lnscr[s_]
                banks = pt if s_ == 0 else pm

                def tout(t0=t0, nt=nt, scr=scr, Rscr=Rscr, banks=banks):
                    for half in range(2):
                        wb, Rwb, kc = wo[half]
                        ptt, Rptt = banks[half]
                        for k in range(8):
                            T("matmul", [Rmerged, Rwb], [Rptt], ptt[0:nt, :], lhsT=mergedT[:, k, t0:t0 + nt], rhs=wb[:, k, :], start=(k == 0), stop=(k == 7))
                        V("tensor_tensor", [Rptt, RMOD], [Rscr], out=scr[0:nt, half * 512:(half + 1) * 512], in0=ptt[0:nt, :],
                          in1=MOD[0:nt, 2 * D + half * 512:2 * D + (half + 1) * 512], op=ALU.mult)
                ch = [tout,
                      lambda xt=xt, Rxt=Rxt, nt=nt, scr=scr, Rscr=Rscr: V("scalar_tensor_tensor", [Rxt, Rscr], [Rscr], out=scr[0:nt, :], in0=xt[0:nt, :], scalar=ALPHA,
                                                                         in1=scr[0:nt, :], op0=ALU.mult, op1=ALU.add)]
                ch += ln_chain(scr[0:nt, :], Rscr, nt, lnbc[0:nt, 0, :], lnbc[0:nt, 1, :], Rlnbc, xt[0:nt, :], Rxt, s_)
                ch += ln_chain(xt[0:nt, :], Rxt, nt, modv(4, nt), modv(3, nt), RMOD, h2t[0:nt, :], Rh2t, s_)
                ch += [lambda h2t=h2t, Rh2t=Rh2t, nt=nt, t0=t0: to_feature_major(h2t, Rh2t, nt, t0)]
                chains.append(ch)
            interleave(chains)
            if self.debug and (sample or sc_idx == 0):
                self.dump("x1_" + tag, xs_t[0][0][0:tiles[0][1], :], xs_t[0][1], [tiles[0][1], D])

            for blk in range(4):
                wb, Rwb, kc = load_w(w_pq_d, D, blk * 512, 512)
                for j in range(4):
                    proj_fm(wb, Rwb, kc, j * 128, 128, hT, RhT, ntok,
                            lambda p_, Rp, j=j: A("copy", [Rp], [RqpT], out=qpT[:, blk * 4 + j, 0:ntok], in_=p_))
            def topk_thunks(ti, t0, nt):
                th = []
                par = ti % 2
                idt, Ridt = ids2[par]
                gat, Rgat = gates2[par]
                Rs_g = [Res("s_g%d" % g) for g in range(16)]
                Rsv_g = [Res("sv_g%d" % g) for g in range(16)]
                Rsi_g = [Res("si_g%d" % g) for g in range(16)]
                Rc_h = [Res("c_h%d" % h) for h in range(8)]
                Rt_h = [Res("t_h%d" % h) for h in range(8)]
                Rtp_h = [Res("tp_h%d" % h) for h in range(8)]
                Roh_h = [Res("oh_h%d" % h) for h in range(8)]
                for R_ in Rs_g:
                    R_.al.append(Rs_sb); Rs_sb.al.append(R_)
                for (lst, big) in ((Rsv_g, Rsv), (Rsi_g, Rsiu), (Rc_h, Rcand), (Rt_h, Rtops), (Rtp_h, Rtpu), (Roh_h, Roh)):
                    for R_ in lst:
                        R_.al.append(big); big.al.append(R_)
                for gq in range(4):
                    def f(gq=gq):
                        pmt, Rpm = next_pm()
                        for j in range(4):
                            gi = gq * 4 + j
                            T("matmul", [RqpT, RskT], [Rpm], pmt[0:nt, j * 128:(j + 1) * 128], lhsT=qpT[:, gi, t0:t0 + nt], rhs=skT[:, gi % 2, :], start=True, stop=True)
                        A("copy", [Rpm], Rs_g[gq * 4:gq * 4 + 4], out=s_sb[0:nt, gq * 4:(gq + 1) * 4, :].rearrange("p g k -> p (g k)"), in_=pmt[0:nt, :])
                    th.append(f)
                for gi in range(16):
                    th.append(lambda gi=gi: V("max", [Rs_g[gi]], [Rsv_g[gi]], out=sv[0:nt, gi, 0:8], in_=s_sb[0:nt, gi, :]))
                for gi in range(16):
                    th.append(lambda gi=gi: V("max_index", [Rs_g[gi], Rsv_g[gi]], [Rsi_g[gi]], out=siu[0:nt, gi, 0:8], in_max=sv[0:nt, gi, 0:8], in_values=s_sb[0:nt, gi, :]))
                for gi in range(16):
                    th.append(lambda gi=gi: V("match_replace", [Rs_g[gi], Rsv_g[gi]], [Rs_g[gi]], out=s_sb[0:nt, gi, :], in_to_replace=sv[0:nt, gi, 0:8],
                                              in_values=s_sb[0:nt, gi, :], imm_value=-1e30))
                for gi in range(16):
                    th.append(lambda gi=gi: V("max", [Rs_g[gi]], [Rsv_g[gi]], out=sv[0:nt, gi, 8:16], in_=s_sb[0:nt, gi, :]))
                for gi in range(16):
                    th.append(lambda gi=gi: V("max_index", [Rs_g[gi], Rsv_g[gi]], [Rsi_g[gi]], out=siu[0:nt, gi, 8:16], in_max=sv[0:nt, gi, 8:16], in_values=s_sb[0:nt, gi, :]))
                th.append(lambda: V("tensor_copy", Rsi_g, [Rsif], out=sif[0:nt], in_=siu[0:nt]))
                svv = sv[0:nt].rearrange("p (h a) k -> p h a k", a=2)
                sfv = sif[0:nt].rearrange("p (h a) k -> p h a k", a=2)
                for h in range(8):
                    th.append(lambda h=h: V("tensor_tensor", [Rsv_g[2 * h], Rsv_g[2 * h + 1]], [Rc_h[h]], out=cand[0:nt, h, :].rearrange("p (a b) -> p a b", a=16),
                                            in0=svv[:, h, 0, :].unsqueeze(2).broadcast_to([nt, 16, 16]), in1=svv[:, h, 1, :].unsqueeze(1).broadcast_to([nt, 16, 16]), op=ALU.add))
                for h in range(8):
                    th.append(lambda h=h: V("max", [Rc_h[h]], [Rt_h[h]], out=tops[0:nt, h, 0:8], in_=cand[0:nt, h, :]))
                for h in range(8):
                    th.append(lambda h=h: V("max_index", [Rc_h[h], Rt_h[h]], [Rtp_h[h]], out=tpu[0:nt, h, 0:8], in_max=tops[0:nt, h, 0:8], in_values=cand[0:nt, h, :]))
                for h in range(8):
                    th.append(lambda h=h: V("match_replace", [Rc_h[h], Rt_h[h]], [Rc_h[h]], out=cand[0:nt, h, :], in_to_replace=tops[0:nt, h, 0:8],
                                            in_values=cand[0:nt, h, :], imm_value=-1e30))
                for h in range(8):
                    th.append(lambda h=h: V("max", [Rc_h[h]], [Rt_h[h]], out=tops[0:nt, h, 8:16], in_=cand[0:nt, h, :]))
                for h in range(8):
                    th.append(lambda h=h: V("max_index", [Rc_h[h], Rt_h[h]], [Rtp_h[h]], out=tpu[0:nt, h, 8:16], in_max=tops[0:nt, h, 8:16], in_values=cand[0:nt, h, :]))
                th.append(lambda: V("tensor_copy", Rtp_h, [Rtpf], out=tpf[0:nt], in_=tpu[0:nt]))
                th.append(lambda: V("tensor_scalar", [Rtpf], [Rta_i], out=ta_i[0:nt], in0=tpf[0:nt], scalar1=-7.5, scalar2=1.0 / 16.0, op0=ALU.add, op1=ALU.mult))
                th.append(lambda: V("tensor_copy", [Rta_i], [Rta], out=ta[0:nt], in_=ta_i[0:nt]))
                th.append(lambda: V("scalar_tensor_tensor", [Rta, Rtpf], [Rtb], out=tb[0:nt], in0=ta[0:nt], scalar=-16.0, in1=tpf[0:nt], op0=ALU.mult, op1=ALU.add))
                for (sel, half_, dst, Rdst) in [(ta, 0, i1, Ri1), (tb, 1, i2, Ri2)]:
                    Rsel = Rta if half_ == 0 else Rtb
                    for h in range(8):
                        th.append(lambda h=h, sel=sel, Rsel=Rsel: V("tensor_tensor", [Rsel, Riota], [Roh_h[h]], out=oh[0:nt, h], in0=sel[0:nt, h, :].unsqueeze(2).broadcast_to([nt, 16, 16]),
                                                                    in1=iota16[0:nt, :].unsqueeze(1).broadcast_to([nt, 16, 16]), op=ALU.is_equal))
                    for h in range(8):
                        th.append(lambda h=h, half_=half_: G("tensor_tensor", [Roh_h[h], Rsif], [Roh_h[h]], out=oh[0:nt, h], in0=oh[0:nt, h],
                                                             in1=sfv[:, h, half_, :].unsqueeze(1).broadcast_to([nt, 16, 16]), op=ALU.mult))
                    th.append(lambda dst=dst, Rdst=Rdst: V("tensor_reduce", Roh_h, [Rdst], out=dst[0:nt], in_=oh[0:nt], axis=AX.X, op=ALU.add))
                th.append(lambda: V("scalar_tensor_tensor", [Ri1, Ri2], [Ridt], out=idt[0:nt, :], in0=i1[0:nt].rearrange("p h k -> p (h k)"), scalar=128.0,
                                    in1=i2[0:nt].rearrange("p h k -> p (h k)"), op0=ALU.mult, op1=ALU.add))
                th.append(lambda: V("tensor_tensor", Rt_h, [Rgat], out=gat[0:nt], in0=tops[0:nt], in1=tops[0:nt, :, 0:1].broadcast_to([nt, 8, 16]), op=ALU.subtract))
                th.append(lambda: A("activation", [Rgat], [Rgat], out=gat[0:nt], in_=gat[0:nt], func=AF.Exp))
                th.append(lambda: V("tensor_reduce", [Rgat], [Rgsum], out=gsum[0:nt], in_=gat[0:nt], axis=AX.X, op=ALU.add))
                th.append(lambda: V("reciprocal", [Rgsum], [Rgsum], out=gsum[0:nt], in_=gsum[0:nt]))
                th.append(lambda: V("tensor_tensor", [Rgat, Rgsum], [Rgat], out=gat[0:nt], in0=gat[0:nt], in1=gsum[0:nt].unsqueeze(2).broadcast_to([nt, 8, 16]), op=ALU.mult))
                return th

            def gather_phase(ti, t0, nt, side):
                xt, Rxt = xs_t[ti]
                h2t, Rh2t = h2tok[ti]
                par = ti % 2
                idt, Ridt = ids2[par]
                gat, Rgat = gates2[par]
                Rdot = [Res("dot%d" % j) for j in range(128)]
                Ract = [Res("act%d" % j) for j in range(128)]
                gflat = gat[0:nt].rearrange("p h k -> p (h k)")
                ybank = pt if ti % 2 == 0 else pm
                side = list(side)
                per = (len(side) + 99) // 100

                def slot_tail(j):
                    gb, Rgb = gbuf[j % NGB]
                    dgt, Rdg = dg[j % 2]
                    V("tensor_scalar", [Ridb, Ract[j], Rgat], [Rdg], out=dgt[0:nt, 0:nt], in0=identb[0:nt, 0:nt], scalar1=wts[0:nt, j:j + 1],
                      scalar2=gflat[:, j:j + 1], op0=ALU.mult, op1=ALU.mult)
                    for half in range(2):
                        ptt, Rptt = ybank[half]
                        T("matmul", [Rdg, Rgb], [Rptt], ptt[0:nt, :], lhsT=dgt[0:nt, 0:nt], rhs=gb[0:nt, D + half * 512:D + (half + 1) * 512],
                          start=(j == 0), stop=(j == 127))

                if self.t_events is not None:
                    P.fence("pool", self.t_events)
                    self.t_events = None
                for j in range(128):
                    gb, Rgb = gbuf[j % NGB]
                    P.dma("pool", lambda e, gb=gb, j=j: e.indirect_dma_start(out=gb[0:nt, :], out_offset=None, in_=tabq,
                                                                          in_offset=bass.IndirectOffsetOnAxis(ap=idt[0:nt, j:j + 1], axis=0)),
                          [Ridt], [Rgb])
                    pb, Rpb = ((jk, Rjk), (hb, Rhb))[j % 2]
                    V("tensor_tensor", [Rgb, Rh2t], [Rpb], out=pb[0:nt, :], in0=gb[0:nt, 0:D], in1=h2t[0:nt, :], op=ALU.mult)
                    A("activation", [Rpb], [Rpb, Rdot[j]], out=pb[0:nt, :], in_=pb[0:nt, :], func=AF.Copy, accum_out=dots[0:nt, j:j + 1])
                    A("activation", [Rdot[j]], [Ract[j]], out=wts[0:nt, j:j + 1], in_=dots[0:nt, j:j + 1], func=AF.Gelu)
                    if j >= 1:
                        slot_tail(j - 1)
                    for _ in range(per):
                        if side:
                            side.pop(0)()
                slot_tail(127)
                while side:
                    side.pop(0)()
                for half in range(2):
                    V("tensor_tensor", [ybank[half][1], RMOD], [RtmpA], out=tmpA[0:nt, half * 512:(half + 1) * 512], in0=ybank[half][0][0:nt, :],
                      in1=MOD[0:nt, 5 * D + half * 512:5 * D + (half + 1) * 512], op=ALU.mult)
                V("scalar_tensor_tensor", [Rxt, RtmpA], [RtmpA], out=tmpA[0:nt, :], in0=xt[0:nt, :], scalar=ALPHA, in1=tmpA[0:nt, :], op0=ALU.mult, op1=ALU.add)
                ln_affine(tmpA[0:nt, :], RtmpA, nt, lnbc[0:nt, 2, :], lnbc[0:nt, 3, :], Rlnbc, xt[0:nt, :], Rxt)
                DMA("sp", [Rxt], [], final=True, out=y_dst[t0:t0 + nt, :], in_=xt[0:nt, :])

            tk = [topk_thunks(ti, t0, nt) for ti, (t0, nt) in enumerate(tiles)]
            for f_ in tk[0]:
                f_()
            for ti, (t0, nt) in enumerate(tiles):
                gather_phase(ti, t0, nt, tk[ti + 1] if ti + 1 < len(tiles) else [])

        process_sc("sample", 0)
        for nb in range(12):
            pmt, Rpm = next_pm()
            T("matmul", [RselP, RMOD], [Rpm], pmt[:, :], lhsT=selP[:, :], rhs=MOD[0:NS + 1, nb * 512:(nb + 1) * 512], start=True, stop=True)
            A("copy", [Rpm], [RMOD], out=MOD[:, nb * 512:(nb + 1) * 512], in_=pmt[:, :])
        for sc in range(NSC):
            process_sc("prompt", sc)
        DMA("sp", [RCaug], [], final=True, out=Cp_d.rearrange("h k v -> k h v"), in_=Caug[:, :, 0:128])
        V("tensor_copy", [RCaug], [RtmpB], out=tmpB[:, 0:4], in_=Caug[:, :, 128])
        T("transpose", [RtmpB, Ridf], [Rptf], out=ptf[0:4, 0:128], in_=tmpB[:, 0:4], identity=identf[:])
        A("copy", [Rptf], [RtmpA], out=tmpA[0:4, 0:128], in_=ptf[0:4, 0:128])
        DMA("sp", [RtmpA], [], final=True, out=np_d, in_=tmpA[0:4, 0:128])
        DMA("sp", [Rmstate], [], final=True, out=mp_d.rearrange("o h -> h o"), in_=mstate[:, 0:1])

        with nc.Block() as block:
            P.emit(block)
        self.es.close()
        return nc


_LAST = {}


def kernel(**inputs):
    debug = bool(int(os.environ.get("KDEBUG", "0")))
    ncores = int(os.environ.get("KCORES", str(NCORES)))
    f = lambda a: np.ascontiguousarray(np.asarray(a, dtype=np.float32))
    x_prompt = f(inputs["x_prompt"]); x_sample = f(inputs["x_sample"]); c_prompt = f(inputs["c_prompt"]); c_sample = f(inputs["c_sample"])
    sC = f(inputs["state_mlstm_C"])[0]; sn = f(inputs["state_mlstm_n"])[0]; sm = f(inputs["state_mlstm_m"])[0]; spool = f(inputs["state_pool"])[0]
    shared = {
        "w_mod": f(inputs["w_mod"])[0], "b_mod": f(inputs["b_mod"]), "w_in": f(inputs["w_in"])[0], "b_in": f(inputs["b_in"]),
        "b_fgate": f(inputs["b_fgate"]), "gn_gain": f(inputs["gn_gain"]), "w_pool": f(inputs["w_pool"])[0], "pool_scale": f(inputs["pool_scale"]),
        "w_branch_a": f(inputs["w_branch_a"])[0], "w_branch_b": f(inputs["w_branch_b"])[0], "w_out": f(inputs["w_out"])[0],
        "ln1_g": f(inputs["ln1_g"]), "ln1_b": f(inputs["ln1_b"]), "w_peer_q": f(inputs["w_peer_q"])[0], "peer_subkeys": f(inputs["peer_subkeys"])[0],
        "peer_u": f(inputs["peer_u"])[0], "peer_v": f(inputs["peer_v"])[0], "ln2_g": f(inputs["ln2_g"]), "ln2_b": f(inputs["ln2_b"]),
    }
    in_maps = []
    for c in range(ncores):
        sl = slice(NS * c, NS * (c + 1))
        m = dict(shared)
        m["xp"] = x_prompt[c]
        m["xs"] = np.ascontiguousarray(x_sample[sl, 0, :])
        m["call"] = np.ascontiguousarray(np.concatenate([c_sample[sl], c_prompt[c:c + 1]], axis=0))
        m["sC"] = np.ascontiguousarray(sC[sl]); m["sn"] = np.ascontiguousarray(sn[sl].reshape(NS, 512)); m["sm"] = np.ascontiguousarray(sm[sl])
        m["spool"] = np.ascontiguousarray(spool[sl])
        in_maps.append(m)
    b = Builder(debug=debug)
    nc = b.build()
    res = run_bass_kernel_spmd(nc, in_maps, core_ids=list(range(ncores)))
    R = res.results
    _LAST["results"] = R
    _LAST["dbg"] = b.dbg_names
    B = 8
    y_prompt = np.zeros((B, SEQ, D), np.float32); y_sample = np.zeros((128, 1, D), np.float32)
    C_prompt = np.zeros((1, B, 4, 128, 128), np.float32); n_prompt = np.zeros((1, B, 4, 128), np.float32); m_prompt = np.zeros((1, B, 4), np.float32)
    pool_prompt = np.zeros((1, B, 15, 512), np.float32)
    C_sample = np.zeros((1, 128, 4, 128, 128), np.float32); n_sample = np.zeros((1, 128, 4, 128), np.float32); m_sample = np.zeros((1, 128, 4), np.float32)
    pool_sample = np.zeros((1, 128, 15, 512), np.float32)
    for c in range(ncores):
        sl = slice(NS * c, NS * (c + 1))
        r = R[c]
        y_prompt[c] = r["yp"]; y_sample[sl, 0, :] = r["ys"]
        C_prompt[0, c] = r["Cp"]; n_prompt[0, c] = r["np_"]; m_prompt[0, c] = r["mp"][0]; pool_prompt[0, c] = r["pp"]
        C_sample[0, sl] = r["Cs"]; n_sample[0, sl] = r["ns"].reshape(NS, 4, 128); m_sample[0, sl] = r["ms"]; pool_sample[0, sl] = r["pls"]
    return (y_prompt, y_sample, C_prompt, n_prompt, m_prompt, pool_prompt, C_sample, n_sample, m_sample, pool_sample)
```

```python
import os
import numpy as np
from contextlib import ExitStack
import concourse.bass as bass
import concourse.mybir as mybir
from concourse.bass_utils import run_bass_kernel_spmd

F32 = mybir.dt.float32
BF16 = mybir.dt.bfloat16
I32 = mybir.dt.int32
U32 = mybir.dt.uint32
ALU = mybir.AluOpType
AF = mybir.ActivationFunctionType
AX = mybir.AxisListType

D = 1024
NIN = 4616
ALPHA = 2.0 ** 0.25
EPS = 1e-5
KSCALE = 128.0 ** -0.5
NCORES = 8
SEQ = 2048
NS = 16
SCT = 256
NCH = SCT // 128
NSC = SEQ // SCT
POOLW = (2, 4, 8, 16)

ENGS = ["pe", "dve", "act", "pool", "sp"]


class Res:
    __slots__ = ("name", "w", "r", "al")

    def __init__(self, name):
        self.name = name
        self.w = None
        self.r = []
        self.al = []


class Prog:
    def __init__(self, nc, n_dma_sems):
        self.nc = nc
        self.q = {e: [] for e in ENGS}
        self.tick = {e: 0 for e in ENGS}
        self.esem = {e: nc.alloc_semaphore(name="es_" + e) for e in ENGS}
        self.waited = {e: {} for e in ENGS}
        self.dsem, self.dcnt, self.dval = {}, {}, {}
        for e, n in n_dma_sems.items():
            self.dsem[e] = [nc.alloc_semaphore(name="ds_%s%d" % (e, i)) for i in range(n)]
            self.dcnt[e] = 0
            self.dval[e] = [0] * n
        self.final_events = []
        self.pending = {e: [] for e in ENGS}

    def fence(self, eng, evs):
        self.pending[eng].extend(evs)

    def _collect(self, eng, reads, writes):
        evs = []
        for R in reads:
            if R.w is not None:
                evs.append((R.w, True))
        for R in writes:
            for Q in [R] + R.al:
                if Q.w is not None:
                    evs.append((Q.w, False))
                for ev in Q.r:
                    evs.append((ev, False))
        for ev in self.pending[eng]:
            evs.append((ev, True))
        self.pending[eng] = []
        best = {}
        for (ev, raw) in evs:
            sem, val, src = ev
            if src == eng and eng == "pe":
                continue
            k = id(sem)
            if k not in best or best[k][1] < val:
                best[k] = (sem, val)
        waits = []
        wd = self.waited[eng]
        for k, (sem, val) in best.items():
            if wd.get(k, 0) >= val:
                continue
            wd[k] = val
            waits.append((sem, val))
        return waits

    def _commit(self, ev, reads, writes):
        for R in reads:
            R.r.append(ev)
        for R in writes:
            R.w = ev
            R.r = []

    def op(self, eng, fn, reads=(), writes=()):
        waits = self._collect(eng, reads, writes)
        self.tick[eng] += 1
        ev = (self.esem[eng], self.tick[eng], eng)
        self.q[eng].append((waits, fn, (self.esem[eng], 1)))
        self._commit(ev, reads, writes)
        return ev

    def dma(self, eng, fn, reads=(), writes=(), final=False):
        waits = self._collect(eng, reads, writes)
        n = len(self.dsem[eng])
        i = self.dcnt[eng] % n
        self.dcnt[eng] += 1
        sem = self.dsem[eng][i]
        prev = self.dval[eng][i]
        if prev > 0 and self.waited[eng].get(id(sem), 0) < prev:
            self.waited[eng][id(sem)] = prev
            waits.append((sem, prev))
        val = prev + 16
        self.dval[eng][i] = val
        ev = (sem, val, "dma_" + eng)
        self.q[eng].append((waits, fn, (sem, 16)))
        self._commit(ev, reads, writes)
        if final:
            self.final_events.append(ev)
        return ev

    def emit(self, block):
        fin = {}
        for (sem, val, src) in self.final_events:
            if id(sem) not in fin or fin[id(sem)][1] < val:
                fin[id(sem)] = (sem, val)
        fin_waits = list(fin.values())

        def run(engname, e):
            for (waits, fn, inc) in self.q[engname]:
                for (sem, val) in waits:
                    e.wait_ge(sem, val)
                fn(e).then_inc(inc[0], inc[1])
            if engname == "sp":
                for (sem, val) in fin_waits:
                    e.wait_ge(sem, val)

        @block.tensor
        def _(e):
            run("pe", e)

        @block.vector
        def _(e):
            run("dve", e)

        @block.scalar
        def _(e):
            run("act", e)

        @block.gpsimd
        def _(e):
            run("pool", e)

        @block.sync
        def _(e):
            run("sp", e)


class Arena:
    def __init__(self, b, name, n, dt):
        self.t, _ = b.sb(name, [128, n], dt)
        self.items = []
        self.n = n

    def carve(self, name, off, shape):
        n = 1
        for d_ in shape[1:]:
            n *= d_
        assert off + n <= self.n, (name, off, n, self.n)
        R = Res(name)
        for (lo, hi, Q) in self.items:
            if lo < off + n and off < hi:
                R.al.append(Q)
                Q.al.append(R)
        self.items.append((off, off + n, R))
        ap = self.t[0:shape[0], off:off + n]
        if len(shape) == 3:
            ap = ap.rearrange("p (a b) -> p a b", a=shape[1])
        elif len(shape) == 4:
            ap = ap.rearrange("p (a b c) -> p a b c", a=shape[1], b=shape[2])
        return ap, R


class Builder:
    def __init__(self, debug=False):
        self.debug = debug
        self.nc = bass.Bass("TRN2", target_bir_lowering=False)
        self.P = Prog(self.nc, {"sp": 16, "pool": 12, "act": 4})
        self.es = ExitStack()
        self.dbg_names = []
        self.wcount = 0
        self.pmcount = 0

    def din(self, name, shape, dt=F32):
        return self.nc.dram_tensor(name, list(shape), dt, kind="ExternalInput").ap()

    def dout(self, name, shape, dt=F32):
        return self.nc.dram_tensor(name, list(shape), dt, kind="ExternalOutput").ap()

    def sb(self, name, shape, dt=F32):
        return self.es.enter_context(self.nc.sbuf_tensor(name, list(shape), dt)), Res(name)

    def ps(self, name, shape, dt=F32):
        return self.es.enter_context(self.nc.psum_tensor(name, list(shape), dt)), Res(name)

    def V(self, m, reads, writes, *a, **kw):
        return self.P.op("dve", lambda e: getattr(e, m)(*a, **kw), reads, writes)

    def A(self, m, reads, writes, *a, **kw):
        return self.P.op("act", lambda e: getattr(e, m)(*a, **kw), reads, writes)

    def G(self, m, reads, writes, *a, **kw):
        return self.P.op("pool", lambda e: getattr(e, m)(*a, **kw), reads, writes)

    def T(self, m, reads, writes, *a, **kw):
        return self.P.op("pe", lambda e: getattr(e, m)(*a, **kw), reads, writes)

    def DMA(self, q, reads, writes, final=False, **kw):
        return self.P.dma(q, lambda e: e.dma_start(**kw), reads, writes, final=final)

    def dump(self, name, ap, R, shape):
        if not self.debug:
            return
        d = self.dout("dbg_" + name, shape, ap.dtype)
        self.dbg_names.append("dbg_" + name)
        self.DMA("sp", [R], [], final=True, out=d, in_=ap)

    def build(self):
        nc, P = self.nc, self.P
        xp_d = self.din("xp", [SEQ, D]); xs_d = self.din("xs", [NS, D]); call_d = self.din("call", [NS + 1, D])
        sC_d = self.din("sC", [NS, 4, 128, 128]); sn_d = self.din("sn", [NS, 512]); sm_d = self.din("sm", [NS, 4])
        spool_d = self.din("spool", [NS, 15, 512])
        w_mod_d = self.din("w_mod", [D, 6 * D]); b_mod_d = self.din("b_mod", [1, 6 * D])
        w_in_d = self.din("w_in", [D, NIN]); b_in_d = self.din("b_in", [1, NIN]); b_fg_d = self.din("b_fgate", [1, 4])
        gn_d = self.din("gn_gain", [1, 512]); w_pool_d = self.din("w_pool", [4, 128, 128]); pscale_d = self.din("pool_scale", [1, 512])
        w_a_d = self.din("w_branch_a", [512, D]); w_b_d = self.din("w_branch_b", [512, D]); w_out_d = self.din("w_out", [D, D])
        ln1g_d = self.din("ln1_g", [1, D]); ln1b_d = self.din("ln1_b", [1, D])
        w_pq_d = self.din("w_peer_q", [D, 2048]); sk_d = self.din("peer_subkeys", [2, 128, 128])
        pu_d = self.din("peer_u", [16384, D]); pv_d = self.din("peer_v", [16384, D])
        ln2g_d = self.din("ln2_g", [1, D]); ln2b_d = self.din("ln2_b", [1, D])
        yp_d = self.dout("yp", [SEQ, D]); ys_d = self.dout("ys", [NS, D])
        Cp_d = self.dout("Cp", [4, 128, 128]); np_d = self.dout("np_", [4, 128]); mp_d = self.dout("mp", [1, 4]); pp_d = self.dout("pp", [15, 512])
        Cs_d = self.dout("Cs", [NS, 4, 128, 128]); ns_d = self.dout("ns", [NS, 512]); ms_d = self.dout("ms", [NS, 4]); pls_d = self.dout("pls", [NS, 15, 512])

        sb, ps, V, A, G, T, DMA = self.sb, self.ps, self.V, self.A, self.G, self.T, self.DMA
        def scr(name, shape):
            return nc.dram_tensor("scr_" + name, list(shape), BF16, kind="Internal").ap()
        wq_in = scr("w_in", [D, NIN]); wq_a = scr("w_a", [512, D]); wq_b = scr("w_b", [512, D]); wq_out = scr("w_out", [D, D]); wq_pq = scr("w_pq", [D, 2048])
        tabq = scr("tab", [16384, 2 * D])
        wmap = {id(w_in_d): wq_in, id(w_a_d): wq_a, id(w_b_d): wq_b, id(w_out_d): wq_out, id(w_pq_d): wq_pq}

        identf, Ridf = sb("identf", [128, 128]); identb, Ridb = sb("identb", [128, 128], BF16)
        ones, Rones = sb("ones", [128, 128]); maskT, Rmask = sb("maskT", [128, 128], BF16)
        selP, RselP = sb("selP", [NS + 1, 128]); eyeb, Reyeb = sb("eyeb", [128, 16, 16], BF16)
        iota16, Riota = sb("iota16", [128, 16]); inv16, Rinv = sb("inv16", [128, 4, 16])
        MOD, RMOD = sb("MOD", [128, 6 * D])
        lnbc, Rlnbc = sb("lnbc", [128, 4, D]); gnbc, Rgnbc = sb("gnbc", [128, 512])
        bias_tm, Rbtm = sb("bias_tm", [128, 2048], BF16); bias_if, Rbif = sb("bias_if", [128, 8]); bfg_bc, Rbfgbc = sb("bfg_bc", [128, 4])
        bcol, Rbcol = sb("bcol", [128, 28]); bIF, RbIF = sb("bIF", [4, 2]); bfg, Rbfg = sb("bfg", [4, 1])
        pscol, Rpscol = sb("pscol", [128, 4])
        wpool, Rwpool = sb("wpool", [128, 4, 128], BF16); skT, RskT = sb("skT", [128, 2, 128], BF16)
        stage = [sb("stage%d" % i, [128, 8, 512]) for i in range(2)]
        NWB = 3
        wbuf = [sb("wbuf%d" % i, [128, 8, 512], BF16) for i in range(NWB)]
        hT, RhT = sb("hT", [128, 8, SCT], BF16)
        xs_t = [sb("xs%d" % i, [128, D]) for i in range(NCH)]
        h2tok = [sb("h2tok%d" % i, [128, D], BF16) for i in range(NCH)]
        tmpA, RtmpA = sb("tmpA", [128, D]); tmpB, RtmpB = sb("tmpB", [128, D]); hb, Rhb = sb("hb", [128, D], BF16)
        st6, Rst6 = sb("st6", [128, 4, 6]); mv, Rmv = sb("mv", [128, 4, 2]); rstd, Rrstd = sb("rstd", [128, 4])
        uT, RuT = sb("uT", [128, 4, 15 + SCT])
        Caug, RCaug = sb("Caug", [128, 4, 129]); mstate, Rmstate = sb("mstate", [4, 1])
        hraw, Rhraw = sb("hraw", [128, 4, 128]); den, Rden = sb("den", [128, 4])
        siu, Rsiu = sb("siu", [128, 16, 16], U32); tpu, Rtpu = sb("tpu", [128, 8, 16], U32)
        ta_i, Rta_i = sb("ta_i", [128, 8, 16], I32)
        ids2 = [sb("ids%d" % i, [128, 128], I32) for i in range(2)]
        NGB = 8
        gbuf = []
        for i in range(NGB):
            stg, Rstg = stage[i // 4]
            R_ = Res("gb%d" % i)
            R_.al.append(Rstg); Rstg.al.append(R_)
            q4 = i % 4
            gbuf.append((stg[:, 2 * q4:2 * q4 + 2, :].rearrange("p a n -> p (a n)").bitcast(BF16), R_))
        dg = [sb("dg%d" % i, [128, 128], BF16) for i in range(2)]
        AB = Arena(self, "arenaB", 12288, BF16)
        qT, RqT = AB.carve("qT", 0, [128, 4, SCT]); kT, RkT = AB.carve("kT", 1024, [128, 4, SCT])
        ktok, Rktok = AB.carve("ktok", 2048, [128, NCH, 512]); vaug, Rvaug = AB.carve("vaug", 3072, [128, NCH, 4, 129])
        osig, Rosig = AB.carve("osig", 4104, [128, NCH, 512]); hAT, RhAT = AB.carve("hAT", 5128, [128, 4, SCT])
        pooledT, Rpooled = AB.carve("pooledT", 6152, [128, 4, SCT]); pBT, RpBT = AB.carve("pBT", 7176, [128, 4, SCT])
        mergedT, Rmerged = AB.carve("mergedT", 8200, [128, 8, SCT]); gsig, Rgsig = AB.carve("gsig", 10248, [128, 4, SCT])
        mtmp, Rmtmp = AB.carve("mtmp", 11272, [128, SCT]); Cs_bf, RCs_bf = AB.carve("Cs_bf", 11528, [128, 129])
        wTt, RwTt = AB.carve("wTt", 11660, [128, 128]); kd, Rkd = AB.carve("kd", 11788, [128, 128])
        qpT, RqpT = AB.carve("qpT", 0, [128, 16, SCT])
        AFa = Arena(self, "arenaF", 9984, F32)
        zI, RzI = AFa.carve("zI", 0, [4, SCT]); zF, RzF = AFa.carve("zF", 256, [4, SCT]); brow, Rbrow = AFa.carve("brow", 512, [4, SCT])
        arow, Rarow = AFa.carve("arow", 768, [4, SCT]); ea_r, Rea_r = AFa.carve("ea_r", 1024, [4, SCT]); ds_r, Rds_r = AFa.carve("ds_r", 1280, [4, SCT])
        eb_r, Reb_r = AFa.carve("eb_r", 1536, [4, SCT])
        poolA, RpoolA = AFa.carve("poolA", 1792, [128, 15 + SCT]); poolB, RpoolB = AFa.carve("poolB", 2112, [128, 15 + SCT])
        gbc, Rgbc = AFa.carve("gbc", 2432, [128, 2, NCH, 4]); gcol, Rgcol = AFa.carve("gcol", 2464, [128, NCH, 3, 4])
        gsm, Rgsm = AFa.carve("gsm", 2496, [4, 8, NCH]); bd, Rbd = AFa.carve("bd", 2528, [4, 2, NCH, 4])
        utok, Rutok = AFa.carve("utok", 2560, [128, 512])
        sq, Rsq = AFa.carve("sq", 0, [NS, 512]); sk, Rsk = AFa.carve("sk", 512, [NS, 512]); sva, Rsva = AFa.carve("sva", 1024, [NS, 4, 129])
        Qm, RQm = AFa.carve("Qm", 1540, [128, 16, 16]); Cst, RCst = AFa.carve("Cst", 1796, [128, 16, 129]); Vm, RVm = AFa.carve("Vm", 3860, [NS, 16, 129])
        sn_t, Rsn_t = AFa.carve("sn_t", 5924, [NS, 512]); nso, Rnso = AFa.carve("nso", 6436, [NS, 512])
        sprows, Rsprows = AFa.carve("sprows", 6948, [128, 2, 512]); upre, Rupre = AFa.carve("upre", 7972, [128, 4, 16, 16])
        sutok, Rsutok = AFa.carve("sutok", 8996, [NS, 512]); psumg, Rpsumg = AFa.carve("psumg", 9508, [128, 4, 16])
        kds, Rkds = AFa.carve("kds", 9572, [NS, 128]); dcB, RdcB = AFa.carve("dcB", 9700, [128, 16]); DCm, RDCm = AFa.carve("DCm", 9716, [NS, 16])
        ssm, Rssm = AFa.carve("ssm", 9732, [NS, 16, 4]); sIF, RsIF = AFa.carve("sIF", 9796, [NS, 8]); qTs, RqTs = AFa.carve("qTs", 9804, [128, 4, NS])
        nT, RnT = AFa.carve("nT", 9868, [128, 4, NS])
        s_sb, Rs_sb = AFa.carve("s_sb", 0, [128, 16, 128]); cand, Rcand = AFa.carve("cand", 2048, [128, 8, 256]); oh, Roh = AFa.carve("oh", 4096, [128, 8, 16, 16])
        yacc, Ryacc = AFa.carve("yacc", 6144, [128, D]); s2, Rs2 = AFa.carve("s2", 7168, [128, 128]); sv, Rsv = AFa.carve("sv", 7296, [128, 16, 16])
        sif, Rsif = AFa.carve("sif", 7552, [128, 16, 16]); cand2, Rcand2 = AFa.carve("cand2", 7808, [128, 256]); tops, Rtops = AFa.carve("tops", 8064, [128, 8, 16])
        tpf, Rtpf = AFa.carve("tpf", 8192, [128, 8, 16]); ta, Rta = AFa.carve("ta", 8320, [128, 8, 16]); tb, Rtb = AFa.carve("tb", 8448, [128, 8, 16])
        i1, Ri1 = AFa.carve("i1", 8576, [128, 8, 16]); i2, Ri2 = AFa.carve("i2", 8704, [128, 8, 16]); gates, Rgates = AFa.carve("gates", 8832, [128, 8, 16])
        gsum, Rgsum = AFa.carve("gsum", 8960, [128, 8]); dots, Rdots = AFa.carve("dots", 8968, [128, 128]); wts, Rwts = AFa.carve("wts", 9096, [128, 128])
        gates_b, Rgates_b = AFa.carve("gates_b", 7168 - 128, [128, 8, 16])
        gates2 = [(gates, Rgates), (gates_b, Rgates_b)]
        jk, Rjk = AFa.carve("jk", 6144, [128, 512])
        jk = jk.bitcast(BF16)
        pm = [ps("pm%d" % i, [128, 512]) for i in range(2)]
        ptb, Rptb = ps("ptb", [128, 8, 128], BF16); ptf, Rptf = ps("ptf", [128, 512])
        pS, RpS = ps("pS", [128, 512]); pP, RpP = ps("pP", [128, 512])
        pt = [ps("pt%d" % i, [128, 512]) for i in range(2)]

        def next_pm():
            self.pmcount += 1
            return pm[self.pmcount % 2]

        G("memset", [], [Ridf], identf[:], 0.0)
        G("affine_select", [Ridf], [Ridf], out=identf[:], in_=identf[:], pattern=[[-1, 128]], compare_op=ALU.not_equal, fill=1.0, base=0, channel_multiplier=1)
        V("tensor_copy", [Ridf], [Ridb], out=identb[:], in_=identf[:])
        G("memset", [], [Rones], ones[:], 1.0)
        G("affine_select", [Rones], [Rmask], out=maskT[:], in_=ones[:], pattern=[[1, 128]], compare_op=ALU.is_ge, fill=0.0, base=0, channel_multiplier=-1)
        G("affine_select", [Rones], [RselP], out=selP[:], in_=ones[0:NS + 1, :], pattern=[[0, 128]], compare_op=ALU.is_equal, fill=0.0, base=-NS, channel_multiplier=1)
        G("memset", [], [Reyeb], eyeb[:], 1.0)
        G("affine_select", [Reyeb], [Reyeb], out=eyeb[:], in_=eyeb[:], pattern=[[1, 16], [-1, 16]], compare_op=ALU.is_equal, fill=0.0, base=0, channel_multiplier=0)
        G("iota", [], [Riota], iota16[:], pattern=[[1, 16]], base=0, channel_multiplier=0, allow_small_or_imprecise_dtypes=True)
        for g, w in enumerate(POOLW):
            V("tensor_scalar", [Riota], [Rinv], out=inv16[:, g, :], in0=iota16[:], scalar1=1.0, scalar2=float(w), op0=ALU.add, op1=ALU.min)
        V("reciprocal", [Rinv], [Rinv], out=inv16[:], in_=inv16[:])
        G("memset", [], [RCaug], Caug[:], 0.0)
        G("memset", [], [Rmstate], mstate[:], 0.0)
        G("memset", [], [RuT], uT[:], 0.0)

        def conv_dma(dst_ap, src_ap):
            sem = nc.alloc_semaphore(name="cv%d" % len(self.cv_sems))
            self.cv_sems.append(sem)
            ev = (sem, 16, "dma_pool")
            P.q["pool"].append((P._collect("pool", [], []), lambda e: e.dma_start(out=dst_ap, in_=src_ap), (sem, 16)))
            return ev
        self.cv_sems = []
        w_events, t_events = [], []
        for (w_d, wq, K, N) in ((w_in_d, wq_in, D, NIN), (w_a_d, wq_a, 512, D), (w_b_d, wq_b, 512, D), (w_out_d, wq_out, D, D), (w_pq_d, wq_pq, D, 2048)):
            c0 = 0
            while c0 < N:
                cw = min(2048, N - c0)
                w_events.append(conv_dma(wq[0:K, c0:c0 + cw], w_d[0:K, c0:c0 + cw]))
                c0 += cw
        for r0 in range(0, 16384, 1024):
            t_events.append(conv_dma(tabq[r0:r0 + 1024, 0:D], pu_d[r0:r0 + 1024, :]))
            t_events.append(conv_dma(tabq[r0:r0 + 1024, D:2 * D], pv_d[r0:r0 + 1024, :]))
        self.w_events, self.t_events = w_events, t_events

        for i, d_ in enumerate([ln1g_d, ln1b_d, ln2g_d, ln2b_d]):
            DMA("sp", [], [Rlnbc], out=lnbc[:, i, :], in_=d_[0:1, :].partition_broadcast(128))
        DMA("sp", [], [Rgnbc], out=gnbc[:], in_=gn_d[0:1, :].partition_broadcast(128))
        stb, Rstb = stage[1]
        DMA("sp", [], [Rstb], out=stb[:, 0:4, :].rearrange("p a n -> p (a n)"), in_=b_in_d[0:1, 0:2048].partition_broadcast(128))
        V("tensor_copy", [Rstb], [Rbtm], out=bias_tm[:], in_=stb[:, 0:4, :].rearrange("p a n -> p (a n)"))
        DMA("sp", [], [Rbif], out=bias_if[:], in_=b_in_d[0:1, 2048:2056].partition_broadcast(128))
        DMA("sp", [], [Rbfgbc], out=bfg_bc[:], in_=b_fg_d[0:1, :].partition_broadcast(128))
        V("tensor_tensor", [Rbif, Rbfgbc], [Rbif], out=bias_if[:, 4:8], in0=bias_if[:, 4:8], in1=bfg_bc[:], op=ALU.add)
        colparts = [(0, 4, 0), (4, 4, 512), (8, 4, 2056), (12, 8, 2568), (20, 8, 3592)]
        for (c0, nb, off) in colparts:
            DMA("sp", [], [Rbcol], out=bcol[:, c0:c0 + nb], in_=b_in_d[0, off:off + nb * 128].rearrange("(c p) -> p c", p=128),
                allow_slow_non_contiguous=True)
        DMA("sp", [], [RbIF], out=bIF[:], in_=b_in_d[0, 2048:2056].rearrange("(c p) -> p c", p=4), allow_slow_non_contiguous=True)
        DMA("sp", [], [Rbfg], out=bfg[:], in_=b_fg_d[0, 0:4].rearrange("(p o) -> p o", o=1))
        V("tensor_tensor", [RbIF, Rbfg], [RbIF], out=bIF[:, 1:2], in0=bIF[:, 1:2], in1=bfg[:], op=ALU.add)
        DMA("sp", [], [Rpscol], out=pscol[:], in_=pscale_d[0, :].rearrange("(g p) -> p g", p=128), allow_slow_non_contiguous=True)
        st0, Rst0 = stage[0]
        DMA("sp", [], [Rst0], out=st0[:, 0, :].rearrange("p (g d) -> p g d", g=4), in_=w_pool_d.rearrange("g c d -> c g d"))
        V("tensor_copy", [Rst0], [Rwpool], out=wpool[:], in_=st0[:, 0, :].rearrange("p (g d) -> p g d", g=4))
        DMA("sp", [], [Rst0], out=st0[:, 1, 0:256].rearrange("p (a d) -> p a d", a=2), in_=sk_d.rearrange("a k d -> k a d"))
        for a in range(2):
            T("transpose", [Rst0, Ridf], [Rptf], out=ptf[:, a * 128:(a + 1) * 128], in_=st0[:, 1, a * 128:(a + 1) * 128], identity=identf[:])
        A("copy", [Rptf], [RskT], out=skT[:], in_=ptf[:, 0:256].rearrange("p (a k) -> p a k", a=2))

        NR = NS + 1
        DMA("sp", [], [RtmpA], out=tmpA[0:NR, :], in_=call_d)
        A("activation", [RtmpA], [RtmpA], out=tmpA[0:NR, :], in_=tmpA[0:NR, :], func=AF.Silu)
        for k in range(8):
            T("transpose", [RtmpA, Ridf], [Rptf], out=ptf[:, k * NR:(k + 1) * NR], in_=tmpA[0:NR, k * 128:(k + 1) * 128], identity=identf[0:NR, 0:NR])
        siluT = tmpB[:, 0:8 * NR].rearrange("p (k r) -> p k r", k=8)
        A("copy", [Rptf], [RtmpB], out=siluT, in_=ptf[:, 0:8 * NR].rearrange("p (k r) -> p k r", k=8))
        for nb in range(12):
            stg, Rstg = stage[nb % 2]
            DMA("sp", [], [Rstg], out=stg[:], in_=w_mod_d[:, nb * 512:(nb + 1) * 512].rearrange("(k p) n -> p k n", p=128))
            jb = tmpA[0:NR, (nb % 2) * 512:(nb % 2) * 512 + 512]
            DMA("sp", [], [RtmpA], out=jb, in_=b_mod_d[0:1, nb * 512:(nb + 1) * 512].partition_broadcast(NR))
            pmt, Rpm = next_pm()
            for k in range(8):
                T("matmul", [RtmpB, Rstg], [Rpm], pmt[0:NR, :], lhsT=siluT[:, k, :], rhs=stg[:, k, :], start=(k == 0), stop=(k == 7))
            add1 = 1.0 if nb in (2, 3, 8, 9) else 0.0
            V("scalar_tensor_tensor", [Rpm, RtmpA], [RMOD], out=MOD[0:NR, nb * 512:(nb + 1) * 512], in0=pmt[0:NR, :], scalar=add1, in1=jb,
              op0=ALU.add, op1=ALU.add)
        self.dump("mod", MOD[0:NR, :], RMOD, [NR, 6 * D])

        def layer_norm_stats(x_ap, Rx, nt, slot):
            for c in range(2):
                V("bn_stats", [Rx], [Rst6], out=st6[0:nt, c, :], in_=x_ap[:, c * 512:(c + 1) * 512])
            V("bn_aggr", [Rst6], [Rmv], out=mv[0:nt, slot, :], in_=st6[0:nt, 0:2, :])
            A("activation", [Rmv], [Rrstd], out=rstd[0:nt, slot:slot + 1], in_=mv[0:nt, slot, 1:2], func=AF.Sqrt, bias=EPS, scale=1.0)
            V("reciprocal", [Rrstd], [Rrstd], out=rstd[0:nt, slot:slot + 1], in_=rstd[0:nt, slot:slot + 1])

        def ln_affine(x_ap, Rx, nt, mul_ap, add_ap, Rpar, out_ap, Rout):
            layer_norm_stats(x_ap, Rx, nt, 0)
            V("tensor_scalar", [Rx, Rmv, Rrstd], [RtmpB], out=tmpB[0:nt, :], in0=x_ap, scalar1=mv[0:nt, 0, 0:1], scalar2=rstd[0:nt, 0:1],
              op0=ALU.subtract, op1=ALU.mult)
            G("tensor_tensor", [RtmpB, Rpar], [RtmpB], out=tmpB[0:nt, :], in0=tmpB[0:nt, :], in1=mul_ap, op=ALU.mult)
            V("tensor_tensor", [RtmpB, Rpar], [Rout], out=out_ap, in0=tmpB[0:nt, :], in1=add_ap, op=ALU.add)

        def sub_res(big, names):
            out = []
            for n_ in names:
                R_ = Res(n_)
                R_.al.append(big); big.al.append(R_)
                out.append(R_)
            return out
        Rst6_s = sub_res(Rst6, ["st6_s0", "st6_s1"]); Rmv_s = sub_res(Rmv, ["mv_s0", "mv_s1"]); Rrstd_s = sub_res(Rrstd, ["rstd_s0", "rstd_s1"])
        lnscr = [(tmpA, RtmpA), (tmpB, RtmpB)]

        def ln_chain(x_ap, Rx, nt, mul_ap, add_ap, Rpar, out_ap, Rout, s_):
            scr, Rscr = lnscr[s_]
            R6, Rm, Rr = Rst6_s[s_], Rmv_s[s_], Rrstd_s[s_]
            sc = scr[0:nt, :]
            return [
                lambda: V("bn_stats", [Rx], [R6], out=st6[0:nt, 2 * s_, :], in_=x_ap[:, 0:512]),
                lambda: V("bn_stats", [Rx], [R6], out=st6[0:nt, 2 * s_ + 1, :], in_=x_ap[:, 512:1024]),
                lambda: V("bn_aggr", [R6], [Rm], out=mv[0:nt, s_, :], in_=st6[0:nt, 2 * s_:2 * s_ + 2, :]),
                lambda: A("activation", [Rm], [Rr], out=rstd[0:nt, s_:s_ + 1], in_=mv[0:nt, s_, 1:2], func=AF.Sqrt, bias=EPS, scale=1.0),
                lambda: V("reciprocal", [Rr], [Rr], out=rstd[0:nt, s_:s_ + 1], in_=rstd[0:nt, s_:s_ + 1]),
                lambda: V("tensor_scalar", [Rx, Rm, Rr], [Rscr], out=sc, in0=x_ap, scalar1=mv[0:nt, s_, 0:1], scalar2=rstd[0:nt, s_:s_ + 1],
                          op0=ALU.subtract, op1=ALU.mult),
                lambda: G("tensor_tensor", [Rscr, Rpar], [Rscr], out=sc, in0=sc, in1=mul_ap, op=ALU.mult),
                lambda: V("tensor_tensor", [Rscr, Rpar], [Rout], out=out_ap, in0=sc, in1=add_ap, op=ALU.add),
            ]

        def interleave(chains):
            n_ = max(len(c_) for c_ in chains)
            for i_ in range(n_):
                for c_ in chains:
                    if i_ < len(c_):
                        c_[i_]()

        def to_feature_major(src_bf, Rsrc, nt, tok0):
            for k in range(8):
                T("transpose", [Rsrc, Ridb], [Rptb], out=ptb[:, k, 0:nt], in_=src_bf[0:nt, k * 128:(k + 1) * 128], identity=identb[0:nt, 0:nt])
            A("copy", [Rptb], [RhT], out=hT[:, :, tok0:tok0 + nt], in_=ptb[:, :, 0:nt])

        def load_w(w_d, K, c0, ncols):
            kc = K // 128
            i = self.wcount % NWB
            self.wcount += 1
            wb, Rwb = wbuf[i]
            if self.w_events is not None:
                P.fence("sp", self.w_events)
                self.w_events = None
            DMA("sp", [], [Rwb], out=wb[:, 0:kc, 0:ncols], in_=wmap[id(w_d)][0:K, c0:c0 + ncols].rearrange("(k p) n -> p k n", p=128))
            return wb, Rwb, kc

        def proj_fm(wb, Rwb, kc, col0, M, act, Ract, ntok, evac):
            pmt, Rpm = next_pm()
            for k in range(kc):
                T("matmul", [Rwb, Ract], [Rpm], pmt[0:M, 0:ntok], lhsT=wb[:, k, col0:col0 + M], rhs=act[:, k, 0:ntok], start=(k == 0), stop=(k == kc - 1))
            evac(pmt[0:M, 0:ntok], Rpm)

        def proj_tm(wb, Rwb, kc, ncols, act, Ract, tok0, nt, evac):
            pmt, Rpm = next_pm()
            for k in range(kc):
                T("matmul", [Rwb, Ract], [Rpm], pmt[0:nt, 0:ncols], lhsT=act[:, k, tok0:tok0 + nt], rhs=wb[:, k, 0:ncols], start=(k == 0), stop=(k == kc - 1))
            evac(pmt[0:nt, 0:ncols], Rpm)

        def ha_finish(nt, ti, tok0):
            for h in range(4):
                V("bn_stats", [Rhraw], [Rst6], out=st6[0:nt, h, :], in_=hraw[0:nt, h, :])
                V("bn_aggr", [Rst6], [Rmv], out=mv[0:nt, h, :], in_=st6[0:nt, h:h + 1, :])
            A("activation", [Rmv], [Rrstd], out=rstd[0:nt, :], in_=mv[0:nt, :, 1], func=AF.Sqrt, bias=EPS, scale=1.0)
            V("reciprocal", [Rrstd], [Rrstd], out=rstd[0:nt, :], in_=rstd[0:nt, :])
            for h in range(4):
                V("tensor_scalar", [Rhraw, Rmv, Rrstd], [RtmpB], out=tmpB[0:nt, h * 128:(h + 1) * 128], in0=hraw[0:nt, h, :],
                  scalar1=mv[0:nt, h, 0:1], scalar2=rstd[0:nt, h:h + 1], op0=ALU.subtract, op1=ALU.mult)
            G("tensor_tensor", [RtmpB, Rgnbc], [RtmpB], out=tmpB[0:nt, 0:512], in0=tmpB[0:nt, 0:512], in1=gnbc[0:nt, :], op=ALU.mult)
            V("tensor_tensor", [RtmpB, Rosig], [Rhb], out=hb[0:nt, 0:512], in0=tmpB[0:nt, 0:512], in1=osig[0:nt, ti, :], op=ALU.mult)
            for hc in range(4):
                T("transpose", [Rhb, Ridb], [Rptb], out=ptb[:, hc, 0:nt], in_=hb[0:nt, hc * 128:(hc + 1) * 128], identity=identb[0:nt, 0:nt])
            A("copy", [Rptb], [RhAT], out=hAT[:, :, tok0:tok0 + nt], in_=ptb[:, 0:4, 0:nt])

        def process_sc(kind, sc_idx):
            sample = (kind == "sample")
            ntok = NS if sample else SCT
            tiles = [(0, NS)] if sample else [(i * 128, 128) for i in range(NCH)]
            x_src = xs_d if sample else xp_d[sc_idx * SCT:(sc_idx + 1) * SCT, :]
            y_dst = ys_d if sample else yp_d[sc_idx * SCT:(sc_idx + 1) * SCT, :]
            tag = "s" if sample else "p%d" % sc_idx

            def modv(i, nt):
                return MOD[0:nt, i * D:(i + 1) * D]

            chains = []
            for ti, (t0, nt) in enumerate(tiles):
                xt, Rxt = xs_t[ti]
                h2t, Rh2t = h2tok[ti]
                ch = [lambda xt=xt, Rxt=Rxt, t0=t0, nt=nt: DMA("sp", [], [Rxt], out=xt[0:nt, :], in_=x_src[t0:t0 + nt, :])]
                ch += ln_chain(xt[0:nt, :], Rxt, nt, modv(1, nt), modv(0, nt), RMOD, h2t[0:nt, :], Rh2t, ti % 2)
                ch += [lambda h2t=h2t, Rh2t=Rh2t, nt=nt, t0=t0: to_feature_major(h2t, Rh2t, nt, t0)]
                chains.append(ch)
            interleave(chains)
            if self.debug and (sample or sc_idx == 0):
                self.dump("h1T_" + tag, hT[:, :, 0:ntok], RhT, [128, 8, ntok])

            wb, Rwb, kc = load_w(w_in_d, D, 0, 512)
            dstq = qTs if sample else qT
            Rdq = RqTs if sample else RqT
            for h in range(4):
                proj_fm(wb, Rwb, kc, h * 128, 128, hT, RhT, ntok,
                        lambda p_, Rp, h=h: A("activation", [Rp, Rbcol], [Rdq], out=dstq[:, h, 0:ntok], in_=p_, func=AF.Identity, bias=bcol[:, h:h + 1], scale=1.0))
            if sample:
                proj_tm(wb, Rwb, kc, 512, hT, RhT, 0, NS,
                        lambda p_, Rp: V("tensor_tensor", [Rp, Rbtm], [Rsq], out=sq[:], in0=p_, in1=bias_tm[0:NS, 0:512], op=ALU.add))
            wb, Rwb, kc = load_w(w_in_d, D, 512, 512)
            if not sample:
                for h in range(4):
                    proj_fm(wb, Rwb, kc, h * 128, 128, hT, RhT, ntok,
                            lambda p_, Rp, h=h: A("activation", [Rp, Rbcol], [RkT], out=kT[:, h, 0:ntok], in_=p_, func=AF.Identity, bias=bcol[:, 4 + h:5 + h], scale=1.0))
                for ti, (t0, nt) in enumerate(tiles):
                    proj_tm(wb, Rwb, kc, 512, hT, RhT, t0, nt,
                            lambda p_, Rp, ti=ti, nt=nt: V("tensor_tensor", [Rp, Rbtm], [Rktok], out=ktok[0:nt, ti, :], in0=p_, in1=bias_tm[0:nt, 512:1024], op=ALU.add))
            else:
                proj_tm(wb, Rwb, kc, 512, hT, RhT, 0, NS,
                        lambda p_, Rp: V("tensor_tensor", [Rp, Rbtm], [Rsk], out=sk[:], in0=p_, in1=bias_tm[0:NS, 512:1024], op=ALU.add))
            wb, Rwb, kc = load_w(w_in_d, D, 1024, 512)
            if sample:
                G("memset", [], [Rsva], sva[:], 1.0)
                G("memset", [], [Rsprows], sprows[:], 0.0)
            else:
                G("memset", [], [Rvaug], vaug[:], 1.0)
            for ti, (t0, nt) in enumerate(tiles):
                if sample:
                    ev = lambda p_, Rp: V("tensor_tensor", [Rp, Rbtm], [Rsva], out=sva[:, :, 0:128], in0=p_.rearrange("p (h d) -> p h d", h=4),
                                          in1=bias_tm[0:NS, 1024:1536].rearrange("p (h d) -> p h d", h=4), op=ALU.add)
                else:
                    ev = lambda p_, Rp, ti=ti, nt=nt: V("tensor_tensor", [Rp, Rbtm], [Rvaug], out=vaug[0:nt, ti, :, 0:128], in0=p_.rearrange("p (h d) -> p h d", h=4),
                                                        in1=bias_tm[0:nt, 1024:1536].rearrange("p (h d) -> p h d", h=4), op=ALU.add)
                proj_tm(wb, Rwb, kc, 512, hT, RhT, t0, nt, ev)
            wb, Rwb, kc = load_w(w_in_d, D, 1536, 512)
            for ti, (t0, nt) in enumerate(tiles):
                def ev(p_, Rp, ti=ti, nt=nt):
                    V("tensor_tensor", [Rp, Rbtm], [RtmpA], out=tmpA[0:nt, 0:512], in0=p_, in1=bias_tm[0:nt, 1536:2048], op=ALU.add)
                    A("activation", [RtmpA], [Rosig], out=osig[0:nt, ti, :], in_=tmpA[0:nt, 0:512], func=AF.Sigmoid)
                proj_tm(wb, Rwb, kc, 512, hT, RhT, t0, nt, ev)
            wb, Rwb, kc = load_w(w_in_d, D, 2048, 8)
            if sample:
                proj_tm(wb, Rwb, kc, 8, hT, RhT, 0, NS,
                        lambda p_, Rp: V("tensor_tensor", [Rp, Rbif], [RsIF], out=sIF[:], in0=p_, in1=bias_if[0:NS, :], op=ALU.add))
            else:
                proj_fm(wb, Rwb, kc, 0, 4, hT, RhT, ntok,
                        lambda p_, Rp: A("activation", [Rp, RbIF], [RzI], out=zI[:, 0:ntok], in_=p_, func=AF.Identity, bias=bIF[:, 0:1], scale=1.0))
                proj_fm(wb, Rwb, kc, 4, 4, hT, RhT, ntok,
                        lambda p_, Rp: A("activation", [Rp, RbIF], [RzF], out=zF[:, 0:ntok], in_=p_, func=AF.Identity, bias=bIF[:, 1:2], scale=1.0))

            if not sample:
                nchunk = NCH
                A("activation", [RzF], [RzF], out=zF[:], in_=zF[:], func=AF.Exp, scale=-1.0)
                A("activation", [RzF], [RzF], out=zF[:], in_=zF[:], func=AF.Ln, bias=1.0, scale=1.0)
                V("tensor_scalar", [RzF], [RzF], out=zF[:], in0=zF[:], scalar1=-1.0, scalar2=None, op0=ALU.mult)
                for c in range(nchunk):
                    V("tensor_tensor_scan", [RzF, Rones], [Rbrow], out=brow[:, c * 128:(c + 1) * 128], data0=ones[0:4, :], data1=zF[:, c * 128:(c + 1) * 128],
                      initial=0.0, op0=ALU.mult, op1=ALU.add)
                V("tensor_tensor", [RzI, Rbrow], [Rarow], out=arow[:], in0=zI[:], in1=brow[:], op=ALU.subtract)
                V("tensor_reduce", [Rarow], [Rgsm], out=gsm[:, 0, :], in_=arow[:].rearrange("p (c t) -> p c t", c=NCH), axis=AX.X, op=ALU.max)
                bL = brow[:].rearrange("p (c t) -> p c t", c=NCH)[:, :, 127]
                V("tensor_tensor_scan", [Rgsm, Rbrow, Rmstate], [Rgsm], out=gsm[:, 1, :], data0=gsm[:, 0, :], data1=bL, initial=mstate[:, 0:1],
                  op0=ALU.max, op1=ALU.add)
                V("tensor_copy", [Rmstate], [Rgsm], out=gsm[:, 2, 0:1], in_=mstate[:, 0:1])
                V("tensor_copy", [Rgsm], [Rgsm], out=gsm[:, 2, 1:NCH], in_=gsm[:, 1, 0:NCH - 1])
                V("tensor_copy", [Rgsm], [Rmstate], out=mstate[:, 0:1], in_=gsm[:, 1, NCH - 1:NCH])
                V("tensor_tensor", [Rbrow, Rgsm], [Rgsm], out=gsm[:, 5, :], in0=bL, in1=gsm[:, 1, :], op=ALU.subtract)
                V("tensor_tensor", [Rgsm], [Rgsm], out=gsm[:, 6, :], in0=gsm[:, 5, :], in1=gsm[:, 2, :], op=ALU.add)
                A("activation", [Rgsm], [Rgsm], out=gsm[:, 3, :], in_=gsm[:, 6, :], func=AF.Exp)
                A("activation", [Rgsm], [Rgsm], out=gsm[:, 4, :], in_=gsm[:, 2, :], func=AF.Exp)
                A("activation", [Rarow], [Rea_r], out=ea_r[:], in_=arow[:], func=AF.Exp)
                V("tensor_scalar", [Rea_r], [Rea_r], out=ea_r[:], in0=ea_r[:], scalar1=KSCALE, scalar2=None, op0=ALU.mult)
                for c in range(nchunk):
                    A("activation", [Rarow, Rgsm], [Rds_r], out=ds_r[:, c * 128:(c + 1) * 128], in_=arow[:, c * 128:(c + 1) * 128], func=AF.Exp,
                      bias=gsm[:, 5, c:c + 1], scale=1.0)
                V("tensor_scalar", [Rds_r], [Rds_r], out=ds_r[:], in0=ds_r[:], scalar1=KSCALE, scalar2=None, op0=ALU.mult)
                A("activation", [Rbrow], [Reb_r], out=eb_r[:], in_=brow[:], func=AF.Exp, scale=-1.0)
                for c in range(nchunk):
                    for qi, (src, Rs) in enumerate([(ea_r, Rea_r), (ds_r, Rds_r), (eb_r, Reb_r)]):
                        T("transpose", [Rs, Ridf], [Rptf], out=ptf[:, (c * 3 + qi) * 4:(c * 3 + qi) * 4 + 4], in_=src[:, c * 128:(c + 1) * 128], identity=identf[0:4, 0:4])
                A("copy", [Rptf], [Rgcol], out=gcol[:].rearrange("p c q h -> p (c q h)"), in_=ptf[:, 0:12 * NCH])
                for qi, row in enumerate([3, 4]):
                    V("tensor_tensor", [Rgsm, Ridf], [Rbd], out=bd[:, qi, :, :], in0=gsm[:, row, :].unsqueeze(2).broadcast_to([4, NCH, 4]),
                      in1=identf[0:4, 0:4].unsqueeze(1).broadcast_to([4, NCH, 4]), op=ALU.mult)
                T("matmul", [Rones, Rbd], [Rptf], ptf[:, 64:64 + 8 * NCH], lhsT=ones[0:4, :], rhs=bd[:].rearrange("p q c h -> p (q c h)"), start=True, stop=True)
                A("copy", [Rptf], [Rgbc], out=gbc[:].rearrange("p q c h -> p (q c h)"), in_=ptf[:, 64:64 + 8 * NCH])
                for c in range(nchunk):
                    t0 = c * 128
                    for h in range(4):
                        V("tensor_scalar", [RCaug, Rgbc], [RCs_bf], out=Cs_bf[:], in0=Caug[:, h, :], scalar1=gbc[:, 1, c, h:h + 1], scalar2=None, op0=ALU.mult)
                        T("matmul", [RkT, RqT], [RpS], pS[:, 0:128], lhsT=kT[:, h, t0:t0 + 128], rhs=qT[:, h, t0:t0 + 128], start=True, stop=True)
                        V("scalar_tensor_tensor", [RpS, Rgcol, Rmask], [RwTt], out=wTt[:], in0=pS[:, 0:128], scalar=gcol[:, c, 0, h:h + 1], in1=maskT[:],
                          op0=ALU.mult, op1=ALU.mult)
                        T("matmul", [RwTt, Rvaug], [RpP], pP[:, 0:129], lhsT=wTt[:], rhs=vaug[:, c, h, :], start=True, stop=False)
                        T("matmul", [RqT, RCs_bf], [RpP], pP[:, 0:129], lhsT=qT[:, h, t0:t0 + 128], rhs=Cs_bf[:], start=False, stop=True)
                        A("activation", [RpP], [Rden], out=den[:, h:h + 1], in_=pP[:, 128:129], func=AF.Abs)
                        V("tensor_tensor", [Rden, Rgcol], [Rden], out=den[:, h:h + 1], in0=den[:, h:h + 1], in1=gcol[:, c, 2, h:h + 1], op=ALU.max)
                        V("reciprocal", [Rden], [Rden], out=den[:, h:h + 1], in_=den[:, h:h + 1])
                        V("tensor_scalar", [RpP, Rden], [Rhraw], out=hraw[:, h, :], in0=pP[:, 0:128], scalar1=den[:, h:h + 1], scalar2=None, op0=ALU.mult)
                        G("tensor_scalar", [Rktok, Rgcol], [Rkd], out=kd[:], in0=ktok[:, c, h * 128:(h + 1) * 128], scalar1=gcol[:, c, 1, h:h + 1], scalar2=None, op0=ALU.mult)
                        T("matmul", [Rkd, Rvaug], [RpP], pP[:, 256:385], lhsT=kd[:], rhs=vaug[:, c, h, :], start=True, stop=True)
                        V("scalar_tensor_tensor", [RCaug, Rgbc, RpP], [RCaug], out=Caug[:, h, :], in0=Caug[:, h, :], scalar=gbc[:, 0, c, h:h + 1], in1=pP[:, 256:385],
                          op0=ALU.mult, op1=ALU.add)
                    if self.debug and sc_idx == 0 and c == 0:
                        self.dump("hraw_p0", hraw[:], Rhraw, [128, 4, 128])
                    ha_finish(128, c, t0)
            else:
                S = lambda i: ssm[:, i, :]
                DMA("sp", [], [Rssm], out=ssm[:, 0, :], in_=sm_d)
                A("activation", [RsIF], [Rssm], out=S(1), in_=sIF[:, 4:8], func=AF.Exp, scale=-1.0)
                A("activation", [Rssm], [Rssm], out=S(1), in_=S(1), func=AF.Ln, bias=1.0, scale=1.0)
                V("tensor_scalar", [Rssm], [Rssm], out=S(1), in0=S(1), scalar1=-1.0, scalar2=None, op0=ALU.mult)
                V("tensor_tensor", [RsIF, Rssm], [Rssm], out=S(2), in0=sIF[:, 0:4], in1=S(1), op=ALU.subtract)
                V("tensor_tensor", [Rssm], [Rssm], out=S(3), in0=S(0), in1=S(2), op=ALU.max)
                V("tensor_tensor", [Rssm], [Rssm], out=S(4), in0=S(1), in1=S(3), op=ALU.add)
                DMA("sp", [Rssm], [], final=True, out=ms_d, in_=S(4))
                V("tensor_tensor", [Rsq, Rsk], [RtmpA], out=tmpA[0:NS, 0:512], in0=sq[:], in1=sk[:], op=ALU.mult)
                V("tensor_reduce", [RtmpA], [Rssm], out=S(5), in_=tmpA[0:NS, 0:512].rearrange("p (h d) -> p h d", h=4), axis=AX.X, op=ALU.add)
                A("activation", [Rssm], [Rssm], out=S(6), in_=S(2), func=AF.Exp)
                V("scalar_tensor_tensor", [Rssm], [Rssm], out=S(7), in0=S(6), scalar=KSCALE, in1=S(5), op0=ALU.mult, op1=ALU.mult)
                A("activation", [Rssm], [Rssm], out=S(8), in_=S(0), func=AF.Exp)
                A("activation", [Rssm], [Rssm], out=S(9), in_=S(1), func=AF.Exp, scale=-1.0)
                V("tensor_tensor", [RsIF, Rssm], [Rssm], out=S(10), in0=sIF[:, 0:4], in1=S(4), op=ALU.subtract)
                A("activation", [Rssm], [Rssm], out=S(10), in_=S(10), func=AF.Exp)
                V("tensor_scalar", [Rssm], [Rssm], out=S(10), in0=S(10), scalar1=KSCALE, scalar2=None, op0=ALU.mult)
                V("tensor_tensor", [Rssm], [Rssm], out=S(11), in0=S(1), in1=S(0), op=ALU.add)
                V("tensor_tensor", [Rssm], [Rssm], out=S(11), in0=S(11), in1=S(4), op=ALU.subtract)
                A("activation", [Rssm], [Rssm], out=S(11), in_=S(11), func=AF.Exp)
                DMA("sp", [], [Rsn_t], out=sn_t[:], in_=sn_d)
                for h in range(4):
                    T("transpose", [Rsn_t, Ridf], [Rptf], out=ptf[:, h * NS:(h + 1) * NS], in_=sn_t[:, h * 128:(h + 1) * 128], identity=identf[0:NS, 0:NS])
                A("copy", [Rptf], [RnT], out=nT[:].rearrange("p h j -> p (h j)"), in_=ptf[:, 0:4 * NS])
                for h in range(4):
                    DMA("sp", [], [RCst], out=Cst[:, :, 0:128], in_=sC_d[:, h].rearrange("j k v -> k j v"))
                    V("tensor_copy", [RnT], [RCst], out=Cst[:, :, 128], in_=nT[:, h, :])
                    V("tensor_tensor", [RqTs, Reyeb], [RQm], out=Qm[:], in0=qTs[:, h, :].unsqueeze(2).broadcast_to([128, 16, 16]), in1=eyeb[:], op=ALU.mult)
                    for j in range(NS):
                        T("matmul", [RQm, RCst], [RpP], pP[0:NS, 0:129], lhsT=Qm[:, j, :], rhs=Cst[:, j, :], start=(j == 0), stop=(j == NS - 1))
                    V("tensor_scalar", [RpP, Rssm], [RtmpA], out=tmpA[0:NS, 0:129], in0=pP[0:NS, 0:129], scalar1=ssm[:, 8, h:h + 1], scalar2=None, op0=ALU.mult)
                    V("scalar_tensor_tensor", [Rsva, Rssm, RtmpA], [RtmpA], out=tmpA[0:NS, 0:129], in0=sva[:, h, :], scalar=ssm[:, 7, h:h + 1], in1=tmpA[0:NS, 0:129],
                      op0=ALU.mult, op1=ALU.add)
                    V("scalar_tensor_tensor", [RtmpA], [Rden], out=den[0:NS, h:h + 1], in0=tmpA[0:NS, 128:129], scalar=-1.0, in1=tmpA[0:NS, 128:129], op0=ALU.mult, op1=ALU.max)
                    V("tensor_tensor", [Rden, Rssm], [Rden], out=den[0:NS, h:h + 1], in0=den[0:NS, h:h + 1], in1=ssm[:, 9, h:h + 1], op=ALU.max)
                    V("reciprocal", [Rden], [Rden], out=den[0:NS, h:h + 1], in_=den[0:NS, h:h + 1])
                    V("tensor_scalar", [RtmpA, Rden], [Rhraw], out=hraw[0:NS, h, :], in0=tmpA[0:NS, 0:128], scalar1=den[0:NS, h:h + 1], scalar2=None, op0=ALU.mult)
                    V("tensor_tensor", [Rsva, Ridf], [RVm], out=Vm[:], in0=sva[:, h, :].unsqueeze(1).broadcast_to([NS, 16, 129]),
                      in1=identf[0:NS, 0:NS].unsqueeze(2).broadcast_to([NS, 16, 129]), op=ALU.mult)
                    V("tensor_scalar", [Rsk, Rssm], [Rkds], out=kds[:], in0=sk[:, h * 128:(h + 1) * 128], scalar1=ssm[:, 10, h:h + 1], scalar2=None, op0=ALU.mult)
                    V("tensor_scalar", [Ridf, Rssm], [RDCm], out=DCm[:], in0=identf[0:NS, 0:NS], scalar1=ssm[:, 11, h:h + 1], scalar2=None, op0=ALU.mult)
                    T("matmul", [Rones, RDCm], [Rptf], ptf[:, 128:144], lhsT=ones[0:NS, :], rhs=DCm[:], start=True, stop=True)
                    A("copy", [Rptf], [RdcB], out=dcB[:], in_=ptf[:, 128:144])
                    for j in range(NS):
                        pst, Rpst = pt[j % 2]
                        T("matmul", [Rkds, RVm], [Rpst], pst[:, 0:129], lhsT=kds[:], rhs=Vm[:, j, :], start=True, stop=True)
                        V("scalar_tensor_tensor", [RCst, RdcB, Rpst], [RCst], out=Cst[:, j, :], in0=Cst[:, j, :], scalar=dcB[:, j:j + 1], in1=pst[:, 0:129],
                          op0=ALU.mult, op1=ALU.add)
                    DMA("sp", [RCst], [], final=True, out=Cs_d[:, h].rearrange("j k v -> k j v"), in_=Cst[:, :, 0:128])
                    V("tensor_copy", [RCst], [RtmpB], out=tmpB[:, 0:NS], in_=Cst[:, :, 128])
                    T("transpose", [RtmpB, Ridf], [Rptf], out=ptf[0:NS, 256:384], in_=tmpB[:, 0:NS], identity=identf[:])
                    A("copy", [Rptf], [Rnso], out=nso[:, h * 128:(h + 1) * 128], in_=ptf[0:NS, 256:384])
                DMA("sp", [Rnso], [], final=True, out=ns_d, in_=nso[:])
                self.dump("hraw_s", hraw[0:NS], Rhraw, [NS, 4, 128])
                ha_finish(NS, 0, 0)
            if self.debug and (sample or sc_idx == 0):
                self.dump("hAT_" + tag, hAT[:, :, 0:ntok], RhAT, [128, 4, ntok])

            wb, Rwb, kc = load_w(w_in_d, D, 2056, 512)
            for g in range(4):
                proj_fm(wb, Rwb, kc, g * 128, 128, hT, RhT, ntok,
                        lambda p_, Rp, g=g: A("activation", [Rp, Rbcol], [RuT], out=uT[:, g, 15:15 + ntok], in_=p_, func=AF.Identity, bias=bcol[:, 8 + g:9 + g], scale=1.0))
            if not sample:
                L = 15 + ntok
                for g, w in enumerate(POOLW):
                    src, Rsrc = uT[:, g, :], RuT
                    bufs = [(poolA, RpoolA), (poolB, RpoolB)]
                    d_, bi = 1, 0
                    while d_ < w:
                        dst, Rdst = bufs[bi]
                        eng = V if (g + bi) % 2 == 0 else G
                        lo = 2 * d_ - 1
                        eng("tensor_tensor", [Rsrc], [Rdst], out=dst[:, lo:L], in0=src[:, lo:L], in1=src[:, lo - d_:L - d_], op=ALU.add)
                        src, Rsrc = dst[:, :], Rdst
                        d_ *= 2
                        bi ^= 1
                    V("scalar_tensor_tensor", [Rsrc, RuT], [Rpooled], out=pooledT[:, g, 0:ntok], in0=src[:, 15:L], scalar=1.0 / w, in1=uT[:, g, 15:L],
                      op0=ALU.mult, op1=ALU.subtract)
                    if sc_idx == 0:
                        V("tensor_tensor", [Rsrc, Rinv], [RtmpA], out=tmpA[:, 0:16], in0=src[:, 15:31], in1=inv16[:, g, :], op=ALU.mult)
                        V("tensor_tensor", [RtmpA, RuT], [Rpooled], out=pooledT[:, g, 0:16], in0=tmpA[:, 0:16], in1=uT[:, g, 15:31], op=ALU.subtract)
                if sc_idx == NSC - 1:
                    for g in range(4):
                        T("transpose", [RuT, Ridf], [Rptf], out=ptf[:, g * 128:(g + 1) * 128], in_=uT[:, g, 15 + ntok - 128:15 + ntok], identity=identf[:])
                    A("copy", [Rptf], [Rutok], out=utok[:], in_=ptf[:, 0:512])
                    DMA("sp", [Rutok], [], final=True, out=pp_d, in_=utok[113:128, :])
                A("copy", [RuT], [RuT], out=uT[:, :, 0:15], in_=uT[:, :, ntok:ntok + 15])
            else:
                for j in range(NS):
                    r0 = (j % 8) * 16
                    DMA("sp", [], [Rsprows], out=sprows[r0:r0 + 15, j // 8, :], in_=spool_d[j])
                for t_ in range(2):
                    for g in range(4):
                        T("transpose", [Rsprows, Ridf], [Rptf], out=ptf[:, g * 128:(g + 1) * 128], in_=sprows[:, t_, g * 128:(g + 1) * 128], identity=identf[:])
                    A("copy", [Rptf], [Rupre], out=upre[:, :, t_ * 8:(t_ + 1) * 8, :].rearrange("p g j q -> p g (j q)"), in_=ptf[:, 0:512].rearrange("p (g r) -> p g r", g=4))
                for g, w in enumerate(POOLW):
                    V("tensor_reduce", [Rupre], [Rpsumg], out=psumg[:, g, :], in_=upre[:, g, :, 16 - w:15], axis=AX.X, op=ALU.add)
                    V("tensor_tensor", [Rpsumg, RuT], [Rpsumg], out=psumg[:, g, :], in0=psumg[:, g, :], in1=uT[:, g, 15:15 + NS], op=ALU.add)
                    V("scalar_tensor_tensor", [Rpsumg, RuT], [Rpooled], out=pooledT[:, g, 0:NS], in0=psumg[:, g, :], scalar=1.0 / w, in1=uT[:, g, 15:15 + NS],
                      op0=ALU.mult, op1=ALU.subtract)
                for j in range(NS):
                    r0 = (j % 8) * 16
                    DMA("sp", [Rsprows], [], final=True, out=pls_d[j, 0:14, :], in_=sprows[r0 + 1:r0 + 15, j // 8, :])
                for g in range(4):
                    T("transpose", [RuT, Ridf], [Rptf], out=ptf[0:NS, g * 128:(g + 1) * 128], in_=uT[:, g, 15:15 + NS], identity=identf[:])
                A("copy", [Rptf], [Rsutok], out=sutok[0:NS, :], in_=ptf[0:NS, 0:512])
                DMA("sp", [Rsutok], [], final=True, out=pls_d[:, 14, :], in_=sutok[0:NS, :])
                G("memset", [RuT], [RuT], uT[:, :, 0:15], 0.0)
            if self.debug and (sample or sc_idx == 0):
                self.dump("pooledT_" + tag, pooledT[:, :, 0:ntok], Rpooled, [128, 4, ntok])
            for g in range(4):
                pmt, Rpm = next_pm()
                T("matmul", [Rwpool, Rpooled], [Rpm], pmt[:, 0:ntok], lhsT=wpool[:, g, :], rhs=pooledT[:, g, 0:ntok], start=True, stop=True)
                V("tensor_scalar", [Rpm, Rpscol], [RpBT], out=pBT[:, g, 0:ntok], in0=pmt[:, 0:ntok], scalar1=pscol[:, g:g + 1], scalar2=None, op0=ALU.mult)

            for half in range(2):
                wb, Rwb, kc = load_w(w_in_d, D, 2568 + half * 512, 512)
                for j in range(4):
                    proj_fm(wb, Rwb, kc, j * 128, 128, hT, RhT, ntok,
                            lambda p_, Rp, j=j: A("activation", [Rp, Rbcol], [Rgsig], out=gsig[:, j, 0:ntok], in_=p_, func=AF.Sigmoid,
                                                  bias=bcol[:, 12 + half * 4 + j:13 + half * 4 + j], scale=1.0))
                wb, Rwb, kc = load_w(w_a_d, 512, half * 512, 512)
                for j in range(4):
                    proj_fm(wb, Rwb, kc, j * 128, 128, hAT, RhAT, ntok,
                            lambda p_, Rp, j=j: V("tensor_tensor", [Rp, Rgsig], [Rmerged], out=mergedT[:, half * 4 + j, 0:ntok], in0=p_, in1=gsig[:, j, 0:ntok], op=ALU.mult))
                wb, Rwb, kc = load_w(w_in_d, D, 3592 + half * 512, 512)
                for j in range(4):
                    proj_fm(wb, Rwb, kc, j * 128, 128, hT, RhT, ntok,
                            lambda p_, Rp, j=j: A("activation", [Rp, Rbcol], [Rgsig], out=gsig[:, j, 0:ntok], in_=p_, func=AF.Sigmoid,
                                                  bias=bcol[:, 20 + half * 4 + j:21 + half * 4 + j], scale=1.0))
                wb, Rwb, kc = load_w(w_b_d, 512, half * 512, 512)
                for j in range(4):
                    def ev(p_, Rp, j=j):
                        V("tensor_tensor", [Rp, Rgsig], [Rmtmp], out=mtmp[:, 0:ntok], in0=p_, in1=gsig[:, j, 0:ntok], op=ALU.mult)
                        G("tensor_tensor", [Rmtmp, Rmerged], [Rmerged], out=mergedT[:, half * 4 + j, 0:ntok], in0=mergedT[:, half * 4 + j, 0:ntok], in1=mtmp[:, 0:ntok], op=ALU.add)
                    proj_fm(wb, Rwb, kc, j * 128, 128, pBT, RpBT, ntok, ev)
            if self.debug and (sample or sc_idx == 0):
                self.dump("mergedT_" + tag, mergedT[:, :, 0:ntok], Rmerged, [128, 8, ntok])

            wo = [load_w(w_out_d, D, half * 512, 512) for half in range(2)]
            chains = []
            for ti, (t0, nt) in enumerate(tiles):
                xt, Rxt = xs_t[ti]
                h2t, Rh2t = h2tok[ti]
                s_ = ti % 2
                scr, Rscr = lnscr[s_]
                banks = pt if s_ == 0 else pm

                def tout(t0=t0, nt=nt, scr=scr, Rscr=Rscr, banks=banks):
                    for half in range(2):
                        wb, Rwb, kc = wo[half]
                        ptt, Rptt = banks[half]
                        for k in range(8):
                            T("matmul", [Rmerged, Rwb], [Rptt], ptt[0:nt, :], lhsT=mergedT[:, k, t0:t0 + nt], rhs=wb[:, k, :], start=(k == 0), stop=(k == 7))
                        V("tensor_tensor", [Rptt, RMOD], [Rscr], out=scr[0:nt, half * 512:(half + 1) * 512], in0=ptt[0:nt, :],
                          in1=MOD[0:nt, 2 * D + half * 512:2 * D + (half + 1) * 512], op=ALU.mult)
                ch = [tout,
                      lambda xt=xt, Rxt=Rxt, nt=nt, scr=scr, Rscr=Rscr: V("scalar_tensor_tensor", [Rxt, Rscr], [Rscr], out=scr[0:nt, :], in0=xt[0:nt, :], scalar=ALPHA,
                                                                         in1=scr[0:nt, :], op0=ALU.mult, op1=ALU.add)]
                ch += ln_chain(scr[0:nt, :], Rscr, nt, lnbc[0:nt, 0, :], lnbc[0:nt, 1, :], Rlnbc, xt[0:nt, :], Rxt, s_)
                ch += ln_chain(xt[0:nt, :], Rxt, nt, modv(4, nt), modv(3, nt), RMOD, h2t[0:nt, :], Rh2t, s_)
                ch += [lambda h2t=h2t, Rh2t=Rh2t, nt=nt, t0=t0: to_feature_major(h2t, Rh2t, nt, t0)]
                chains.append(ch)
            interleave(chains)
            if self.debug and (sample or sc_idx == 0):
                self.dump("x1_" + tag, xs_t[0][0][0:tiles[0][1], :], xs_t[0][1], [tiles[0][1], D])

            for blk in range(4):
                wb, Rwb, kc = load_w(w_pq_d, D, blk * 512, 512)
                for j in range(4):
                    proj_fm(wb, Rwb, kc, j * 128, 128, hT, RhT, ntok,
                            lambda p_, Rp, j=j: A("copy", [Rp], [RqpT], out=qpT[:, blk * 4 + j, 0:ntok], in_=p_))
            def topk_thunks(ti, t0, nt):
                th = []
                par = ti % 2
                idt, Ridt = ids2[par]
                gat, Rgat = gates2[par]
                Rs_g = [Res("s_g%d" % g) for g in range(16)]
                Rsv_g = [Res("sv_g%d" % g) for g in range(16)]
                Rsi_g = [Res("si_g%d" % g) for g in range(16)]
                Rc_h = [Res("c_h%d" % h) for h in range(8)]
                Rt_h = [Res("t_h%d" % h) for h in range(8)]
                Rtp_h = [Res("tp_h%d" % h) for h in range(8)]
                Roh_h = [Res("oh_h%d" % h) for h in range(8)]
                for R_ in Rs_g:
                    R_.al.append(Rs_sb); Rs_sb.al.append(R_)
                for (lst, big) in ((Rsv_g, Rsv), (Rsi_g, Rsiu), (Rc_h, Rcand), (Rt_h, Rtops), (Rtp_h, Rtpu), (Roh_h, Roh)):
                    for R_ in lst:
                        R_.al.append(big); big.al.append(R_)
                for gq in range(4):
                    def f(gq=gq):
                        pmt, Rpm = next_pm()
                        for j in range(4):
                            gi = gq * 4 + j
                            T("matmul", [RqpT, RskT], [Rpm], pmt[0:nt, j * 128:(j + 1) * 128], lhsT=qpT[:, gi, t0:t0 + nt], rhs=skT[:, gi % 2, :], start=True, stop=True)
                        A("copy", [Rpm], Rs_g[gq * 4:gq * 4 + 4], out=s_sb[0:nt, gq * 4:(gq + 1) * 4, :].rearrange("p g k -> p (g k)"), in_=pmt[0:nt, :])
                    th.append(f)
                for gi in range(16):
                    th.append(lambda gi=gi: V("max", [Rs_g[gi]], [Rsv_g[gi]], out=sv[0:nt, gi, 0:8], in_=s_sb[0:nt, gi, :]))
                for gi in range(16):
                    th.append(lambda gi=gi: V("max_index", [Rs_g[gi], Rsv_g[gi]], [Rsi_g[gi]], out=siu[0:nt, gi, 0:8], in_max=sv[0:nt, gi, 0:8], in_values=s_sb[0:nt, gi, :]))
                for gi in range(16):
                    th.append(lambda gi=gi: V("match_replace", [Rs_g[gi], Rsv_g[gi]], [Rs_g[gi]], out=s_sb[0:nt, gi, :], in_to_replace=sv[0:nt, gi, 0:8],
                                              in_values=s_sb[0:nt, gi, :], imm_value=-1e30))
                for gi in range(16):
                    th.append(lambda gi=gi: V("max", [Rs_g[gi]], [Rsv_g[gi]], out=sv[0:nt, gi, 8:16], in_=s_sb[0:nt, gi, :]))
                for gi in range(16):
                    th.append(lambda gi=gi: V("max_index", [Rs_g[gi], Rsv_g[gi]], [Rsi_g[gi]], out=siu[0:nt, gi, 8:16], in_max=sv[0:nt, gi, 8:16], in_values=s_sb[0:nt, gi, :]))
                th.append(lambda: V("tensor_copy", Rsi_g, [Rsif], out=sif[0:nt], in_=siu[0:nt]))
                svv = sv[0:nt].rearrange("p (h a) k -> p h a k", a=2)
                sfv = sif[0:nt].rearrange("p (h a) k -> p h a k", a=2)
                for h in range(8):
                    th.append(lambda h=h: V("tensor_tensor", [Rsv_g[2 * h], Rsv_g[2 * h + 1]], [Rc_h[h]], out=cand[0:nt, h, :].rearrange("p (a b) -> p a b", a=16),
                                            in0=svv[:, h, 0, :].unsqueeze(2).broadcast_to([nt, 16, 16]), in1=svv[:, h, 1, :].unsqueeze(1).broadcast_to([nt, 16, 16]), op=ALU.add))
                for h in range(8):
                    th.append(lambda h=h: V("max", [Rc_h[h]], [Rt_h[h]], out=tops[0:nt, h, 0:8], in_=cand[0:nt, h, :]))
                for h in range(8):
                    th.append(lambda h=h: V("max_index", [Rc_h[h], Rt_h[h]], [Rtp_h[h]], out=tpu[0:nt, h, 0:8], in_max=tops[0:nt, h, 0:8], in_values=cand[0:nt, h, :]))
                for h in range(8):
                    th.append(lambda h=h: V("match_replace", [Rc_h[h], Rt_h[h]], [Rc_h[h]], out=cand[0:nt, h, :], in_to_replace=tops[0:nt, h, 0:8],
                                            in_values=cand[0:nt, h, :], imm_value=-1e30))
                for h in range(8):
                    th.append(lambda h=h: V("max", [Rc_h[h]], [Rt_h[h]], out=tops[0:nt, h, 8:16], in_=cand[0:nt, h, :]))
                for h in range(8):
                    th.append(lambda h=h: V("max_index", [Rc_h[h], Rt_h[h]], [Rtp_h[h]], out=tpu[0:nt, h, 8:16], in_max=tops[0:nt, h, 8:16], in_values=cand[0:nt, h, :]))
                th.append(lambda: V("tensor_copy", Rtp_h, [Rtpf], out=tpf[0:nt], in_=tpu[0:nt]))
                th.append(lambda: V("tensor_scalar", [Rtpf], [Rta_i], out=ta_i[0:nt], in0=tpf[0:nt], scalar1=-7.5, scalar2=1.0 / 16.0, op0=ALU.add, op1=ALU.mult))
                th.append(lambda: V("tensor_copy", [Rta_i], [Rta], out=ta[0:nt], in_=ta_i[0:nt]))
                th.append(lambda: V("scalar_tensor_tensor", [Rta, Rtpf], [Rtb], out=tb[0:nt], in0=ta[0:nt], scalar=-16.0, in1=tpf[0:nt], op0=ALU.mult, op1=ALU.add))
                for (sel, half_, dst, Rdst) in [(ta, 0, i1, Ri1), (tb, 1, i2, Ri2)]:
                    Rsel = Rta if half_ == 0 else Rtb
                    for h in range(8):
                        th.append(lambda h=h, sel=sel, Rsel=Rsel: V("tensor_tensor", [Rsel, Riota], [Roh_h[h]], out=oh[0:nt, h], in0=sel[0:nt, h, :].unsqueeze(2).broadcast_to([nt, 16, 16]),
                                                                    in1=iota16[0:nt, :].unsqueeze(1).broadcast_to([nt, 16, 16]), op=ALU.is_equal))
                    for h in range(8):
                        th.append(lambda h=h, half_=half_: V("tensor_tensor", [Roh_h[h], Rsif], [Roh_h[h]], out=oh[0:nt, h], in0=oh[0:nt, h],
                                                             in1=sfv[:, h, half_, :].unsqueeze(1).broadcast_to([nt, 16, 16]), op=ALU.mult))
                    th.append(lambda dst=dst, Rdst=Rdst: V("tensor_reduce", Roh_h, [Rdst], out=dst[0:nt], in_=oh[0:nt], axis=AX.X, op=ALU.add))
                th.append(lambda: V("scalar_tensor_tensor", [Ri1, Ri2], [Ridt], out=idt[0:nt, :], in0=i1[0:nt].rearrange("p h k -> p (h k)"), scalar=128.0,
                                    in1=i2[0:nt].rearrange("p h k -> p (h k)"), op0=ALU.mult, op1=ALU.add))
                th.append(lambda: V("tensor_tensor", Rt_h, [Rgat], out=gat[0:nt], in0=tops[0:nt], in1=tops[0:nt, :, 0:1].broadcast_to([nt, 8, 16]), op=ALU.subtract))
                th.append(lambda: A("activation", [Rgat], [Rgat], out=gat[0:nt], in_=gat[0:nt], func=AF.Exp))
                th.append(lambda: V("tensor_reduce", [Rgat], [Rgsum], out=gsum[0:nt], in_=gat[0:nt], axis=AX.X, op=ALU.add))
                th.append(lambda: V("reciprocal", [Rgsum], [Rgsum], out=gsum[0:nt], in_=gsum[0:nt]))
                th.append(lambda: V("tensor_tensor", [Rgat, Rgsum], [Rgat], out=gat[0:nt], in0=gat[0:nt], in1=gsum[0:nt].unsqueeze(2).broadcast_to([nt, 8, 16]), op=ALU.mult))
                return th

            def gather_phase(ti, t0, nt, side):
                xt, Rxt = xs_t[ti]
                h2t, Rh2t = h2tok[ti]
                par = ti % 2
                idt, Ridt = ids2[par]
                gat, Rgat = gates2[par]
                Rdot = [Res("dot%d" % j) for j in range(128)]
                Ract = [Res("act%d" % j) for j in range(128)]
                gflat = gat[0:nt].rearrange("p h k -> p (h k)")
                side = list(side)
                per = (len(side) + 99) // 100

                def slot_tail(j):
                    gb, Rgb = gbuf[j % NGB]
                    dgt, Rdg = dg[j % 2]
                    V("tensor_scalar", [Ridb, Ract[j], Rgat], [Rdg], out=dgt[0:nt, 0:nt], in0=identb[0:nt, 0:nt], scalar1=wts[0:nt, j:j + 1],
                      scalar2=gflat[:, j:j + 1], op0=ALU.mult, op1=ALU.mult)
                    for half in range(2):
                        ptt, Rptt = pt[half]
                        T("matmul", [Rdg, Rgb], [Rptt], ptt[0:nt, :], lhsT=dgt[0:nt, 0:nt], rhs=gb[0:nt, D + half * 512:D + (half + 1) * 512],
                          start=(j == 0), stop=(j == 127))

                if self.t_events is not None:
                    P.fence("pool", self.t_events)
                    self.t_events = None
                for j in range(128):
                    gb, Rgb = gbuf[j % NGB]
                    P.dma("pool", lambda e, gb=gb, j=j: e.indirect_dma_start(out=gb[0:nt, :], out_offset=None, in_=tabq,
                                                                          in_offset=bass.IndirectOffsetOnAxis(ap=idt[0:nt, j:j + 1], axis=0)),
                          [Ridt], [Rgb])
                    pb, Rpb = ((jk, Rjk), (hb, Rhb))[j % 2]
                    V("tensor_tensor", [Rgb, Rh2t], [Rpb], out=pb[0:nt, :], in0=gb[0:nt, 0:D], in1=h2t[0:nt, :], op=ALU.mult)
                    A("activation", [Rpb], [Rpb, Rdot[j]], out=pb[0:nt, :], in_=pb[0:nt, :], func=AF.Copy, accum_out=dots[0:nt, j:j + 1])
                    A("activation", [Rdot[j]], [Ract[j]], out=wts[0:nt, j:j + 1], in_=dots[0:nt, j:j + 1], func=AF.Gelu)
                    if j >= 1:
                        slot_tail(j - 1)
                    for _ in range(per):
                        if side:
                            side.pop(0)()
                slot_tail(127)
                while side:
                    side.pop(0)()
                for half in range(2):
                    V("tensor_tensor", [pt[half][1], RMOD], [RtmpA], out=tmpA[0:nt, half * 512:(half + 1) * 512], in0=pt[half][0][0:nt, :],
                      in1=MOD[0:nt, 5 * D + half * 512:5 * D + (half + 1) * 512], op=ALU.mult)
                V("scalar_tensor_tensor", [Rxt, RtmpA], [RtmpA], out=tmpA[0:nt, :], in0=xt[0:nt, :], scalar=ALPHA, in1=tmpA[0:nt, :], op0=ALU.mult, op1=ALU.add)
                ln_affine(tmpA[0:nt, :], RtmpA, nt, lnbc[0:nt, 2, :], lnbc[0:nt, 3, :], Rlnbc, xt[0:nt, :], Rxt)
                DMA("sp", [Rxt], [], final=True, out=y_dst[t0:t0 + nt, :], in_=xt[0:nt, :])

            tk = [topk_thunks(ti, t0, nt) for ti, (t0, nt) in enumerate(tiles)]
            for f_ in tk[0]:
                f_()
            for ti, (t0, nt) in enumerate(tiles):
                gather_phase(ti, t0, nt, tk[ti + 1] if ti + 1 < len(tiles) else [])

        process_sc("sample", 0)
        for nb in range(12):
            pmt, Rpm = next_pm()
            T("matmul", [RselP, RMOD], [Rpm], pmt[:, :], lhsT=selP[:, :], rhs=MOD[0:NS + 1, nb * 512:(nb + 1) * 512], start=True, stop=True)
            A("copy", [Rpm], [RMOD], out=MOD[:, nb * 512:(nb + 1) * 512], in_=pmt[:, :])
        for sc in range(NSC):
            process_sc("prompt", sc)
        DMA("sp", [RCaug], [], final=True, out=Cp_d.rearrange("h k v -> k h v"), in_=Caug[:, :, 0:128])
        V("tensor_copy", [RCaug], [RtmpB], out=tmpB[:, 0:4], in_=Caug[:, :, 128])
        T("transpose", [RtmpB, Ridf], [Rptf], out=ptf[0:4, 0:128], in_=tmpB[:, 0:4], identity=identf[:])
        A("copy", [Rptf], [RtmpA], out=tmpA[0:4, 0:128], in_=ptf[0:4, 0:128])
        DMA("sp", [RtmpA], [], final=True, out=np_d, in_=tmpA[0:4, 0:128])
        DMA("sp", [Rmstate], [], final=True, out=mp_d.rearrange("o h -> h o"), in_=mstate[:, 0:1])

        with nc.Block() as block:
            P.emit(block)
        self.es.close()
        return nc


_LAST = {}


def kernel(**inputs):
    debug = bool(int(os.environ.get("KDEBUG", "0")))
    ncores = int(os.environ.get("KCORES", str(NCORES)))
    f = lambda a: np.ascontiguousarray(np.asarray(a, dtype=np.float32))
    x_prompt = f(inputs["x_prompt"]); x_sample = f(inputs["x_sample"]); c_prompt = f(inputs["c_prompt"]); c_sample = f(inputs["c_sample"])
    sC = f(inputs["state_mlstm_C"])[0]; sn = f(inputs["state_mlstm_n"])[0]; sm = f(inputs["state_mlstm_m"])[0]; spool = f(inputs["state_pool"])[0]
    shared = {
        "w_mod": f(inputs["w_mod"])[0], "b_mod": f(inputs["b_mod"]), "w_in": f(inputs["w_in"])[0], "b_in": f(inputs["b_in"]),
        "b_fgate": f(inputs["b_fgate"]), "gn_gain": f(inputs["gn_gain"]), "w_pool": f(inputs["w_pool"])[0], "pool_scale": f(inputs["pool_scale"]),
        "w_branch_a": f(inputs["w_branch_a"])[0], "w_branch_b": f(inputs["w_branch_b"])[0], "w_out": f(inputs["w_out"])[0],
        "ln1_g": f(inputs["ln1_g"]), "ln1_b": f(inputs["ln1_b"]), "w_peer_q": f(inputs["w_peer_q"])[0], "peer_subkeys": f(inputs["peer_subkeys"])[0],
        "peer_u": f(inputs["peer_u"])[0], "peer_v": f(inputs["peer_v"])[0], "ln2_g": f(inputs["ln2_g"]), "ln2_b": f(inputs["ln2_b"]),
    }
    in_maps = []
    for c in range(ncores):
        sl = slice(NS * c, NS * (c + 1))
        m = dict(shared)
        m["xp"] = x_prompt[c]
        m["xs"] = np.ascontiguousarray(x_sample[sl, 0, :])
        m["call"] = np.ascontiguousarray(np.concatenate([c_sample[sl], c_prompt[c:c + 1]], axis=0))
        m["sC"] = np.ascontiguousarray(sC[sl]); m["sn"] = np.ascontiguousarray(sn[sl].reshape(NS, 512)); m["sm"] = np.ascontiguousarray(sm[sl])
        m["spool"] = np.ascontiguousarray(spool[sl])
        in_maps.append(m)
    b = Builder(debug=debug)
    nc = b.build()
    res = run_bass_kernel_spmd(nc, in_maps, core_ids=list(range(ncores)))
    R = res.results
    _LAST["results"] = R
    _LAST["dbg"] = b.dbg_names
    B = 8
    y_prompt = np.zeros((B, SEQ, D), np.float32); y_sample = np.zeros((128, 1, D), np.float32)
    C_prompt = np.zeros((1, B, 4, 128, 128), np.float32); n_prompt = np.zeros((1, B, 4, 128), np.float32); m_prompt = np.zeros((1, B, 4), np.float32)
    pool_prompt = np.zeros((1, B, 15, 512), np.float32)
    C_sample = np.zeros((1, 128, 4, 128, 128), np.float32); n_sample = np.zeros((1, 128, 4, 128), np.float32); m_sample = np.zeros((1, 128, 4), np.float32)
    pool_sample = np.zeros((1, 128, 15, 512), np.float32)
    for c in range(ncores):
        sl = slice(NS * c, NS * (c + 1))
        r = R[c]
        y_prompt[c] = r["yp"]; y_sample[sl, 0, :] = r["ys"]
        C_prompt[0, c] = r["Cp"]; n_prompt[0, c] = r["np_"]; m_prompt[0, c] = r["mp"][0]; pool_prompt[0, c] = r["pp"]
        C_sample[0, sl] = r["Cs"]; n_sample[0, sl] = r["ns"].reshape(NS, 4, 128); m_sample[0, sl] = r["ms"]; pool_sample[0, sl] = r["pls"]
    return (y_prompt, y_sample, C_prompt, n_prompt, m_prompt, pool_prompt, C_sample, n_sample, m_sample, pool_sample)
```
